# Optimizing a Trainium2 kernel written in Bass

```python
import jax, jax.numpy as jnp
from jax import lax
import numpy as np

D_MODEL = 1024
BATCH = 16
SEQ = 2048
DEPTH = 2

GRID_W = 64
CTX_LEN = 256
EPS = 1e-6
N_MOD = 9

D_FF = 2816

A_HEADS = 8
A_HEAD_DIM = 64
A_WIDTH = A_HEADS * A_HEAD_DIM
WIN_R = 8
WIN_C = 16
COL_BLOCK = 16
COL_BAND = COL_BLOCK + WIN_C

POOL_WINDOWS = (2, 4, 8, 16)
B_WIDTH = D_MODEL - A_WIDTH
B_GROUP = B_WIDTH // len(POOL_WINDOWS)

HG_EXPAND = 128
HG_HEADS = D_MODEL // HG_EXPAND
HG_DK = HG_EXPAND
HG_DV = D_MODEL // HG_HEADS
HG_KDIM = HG_HEADS * HG_DK
HG_VDIM = HG_HEADS * HG_DV
HG_CHUNK = 32

kernel_name = "hybrid_natten_pool_hgrn2_diffusion_block"


def rms_norm(x, g):
    xf = x.astype(jnp.float32)
    y = xf * lax.rsqrt(jnp.mean(xf * xf, axis=-1, keepdims=True) + EPS)
    return (y * g.astype(jnp.float32)).astype(x.dtype)


def adaln_in(h, g, shift, scale):
    return rms_norm(h, g) * (1 + scale) + shift


def swiglu(h, w13, w2):
    a, b = jnp.split(h @ w13, 2, axis=-1)
    return (jax.nn.silu(a) * b) @ w2


def neighbourhood_attention(q, k, v, k_ctx, v_ctx, rpb):
    bsz, n, nh, dh = q.shape
    rows = n // GRID_W
    kr = min(WIN_R, rows)
    ncb = GRID_W // COL_BLOCK
    scale = dh ** -0.5
    qg = q.reshape(bsz, rows, ncb, COL_BLOCK, nh, dh)
    kg = k.reshape(bsz, rows, GRID_W, nh, dh)
    vg = v.reshape(bsz, rows, GRID_W, nh, dh)
    col_q = jnp.arange(GRID_W).reshape(ncb, COL_BLOCK)
    win_c0 = jnp.clip(col_q - WIN_C // 2, 0, GRID_W - WIN_C)
    band0 = jnp.clip(jnp.arange(ncb) * COL_BLOCK - WIN_C // 2, 0, GRID_W - COL_BAND)
    key_col = band0[:, None] + jnp.arange(COL_BAND)
    kc = key_col[:, None, :]
    col_ok = (kc >= win_c0[:, :, None]) & (kc < win_c0[:, :, None] + WIN_C)
    rel_c = jnp.clip(kc - col_q[:, :, None], -(WIN_C - 1), WIN_C - 1) + (WIN_C - 1)
    rpb_c = rpb[:, :, rel_c]
    nloc = kr * COL_BAND

    def row_block(r):
        r0 = jnp.clip(r - kr // 2, 0, rows - kr)
        key_rows = r0 + jnp.arange(kr)
        k_band = jnp.take(kg, key_rows, axis=1)[:, :, key_col]
        v_band = jnp.take(vg, key_rows, axis=1)[:, :, key_col]
        q_blk = lax.dynamic_index_in_dim(qg, r, axis=1, keepdims=False)
        s_loc = jnp.einsum('bnqhd,bjnkhd->bhnqjk', q_blk, k_band).astype(jnp.float32) * scale
        bias = rpb_c[:, key_rows - r + (WIN_R - 1)].transpose(0, 2, 3, 1, 4)
        s_loc = s_loc + bias[None].astype(jnp.float32)
        s_loc = jnp.where(col_ok[:, :, None, :], s_loc, -jnp.inf)
        s_loc = s_loc.reshape(bsz, nh, ncb, COL_BLOCK, nloc)
        s_ctx = jnp.einsum('bnqhd,bchd->bhnqc', q_blk, k_ctx).astype(jnp.float32) * scale
        p = jax.nn.softmax(jnp.concatenate([s_loc, s_ctx], axis=-1), axis=-1).astype(v.dtype)
        p_loc = p[..., :nloc].reshape(bsz, nh, ncb, COL_BLOCK, kr, COL_BAND)
        p_ctx = p[..., nloc:]
        return (jnp.einsum('bhnqjk,bjnkhd->bnqhd', p_loc, v_band)
                + jnp.einsum('bhnqc,bchd->bnqhd', p_ctx, v_ctx))

    out = lax.map(row_block, jnp.arange(rows))
    return out.transpose(1, 0, 2, 3, 4, 5).reshape(bsz, n, nh * dh)


def context_attention(q, k, v):
    bsz, L, nh, dh = q.shape
    s = jnp.einsum('bqhd,bkhd->bhqk', q, k).astype(jnp.float32) * (dh ** -0.5)
    p = jax.nn.softmax(s, axis=-1).astype(v.dtype)
    return jnp.einsum('bhqk,bkhd->bqhd', p, v).reshape(bsz, L, nh * dh)


def centred_pool_minus_self(u, w):
    bsz, L, ch = u.shape
    t = jnp.arange(L)
    lo = jnp.clip(t - w // 2, 0, L)
    hi = jnp.clip(t - w // 2 + w, 0, L)
    uf = u.astype(jnp.float32)
    cs = jnp.concatenate([jnp.zeros((bsz, 1, ch), jnp.float32), jnp.cumsum(uf, axis=1)], axis=1)
    mean = (cs[:, hi] - cs[:, lo]) / (hi - lo).astype(jnp.float32)[None, :, None]
    return (mean - uf).astype(u.dtype)


def pool_mixer(u, pool_w, pool_scale):
    bsz, L, _ = u.shape
    groups = jnp.split(u, len(POOL_WINDOWS), axis=-1)
    d = jnp.stack([centred_pool_minus_self(gi, w) for gi, w in zip(groups, POOL_WINDOWS)], axis=2)
    y = jnp.einsum('blgc,gcd->blgd', d, pool_w).reshape(bsz, L, B_WIDTH)
    return y * pool_scale


def ab_mixer(x_lat, x_ctx, w_in, rpb, pool_w, pool_scale, w_out, need_ctx_out):
    def split(p):
        bsz, L, _ = p.shape
        q, k, v, u = jnp.split(p, [A_WIDTH, 2 * A_WIDTH, 3 * A_WIDTH], axis=-1)
        heads = lambda t: t.reshape(bsz, L, A_HEADS, A_HEAD_DIM)
        return heads(q), heads(k), heads(v), u

    q_l, k_l, v_l, u_l = split(x_lat @ w_in)
    q_c, k_c, v_c, u_c = split(x_ctx @ w_in)
    a_l = neighbourhood_attention(q_l, k_l, v_l, k_c, v_c, rpb)
    y_l = jnp.concatenate([a_l, pool_mixer(u_l, pool_w, pool_scale)], axis=-1) @ w_out
    if not need_ctx_out:
        return y_l, None
    a_c = context_attention(q_c, k_c, v_c)
    y_c = jnp.concatenate([a_c, pool_mixer(u_c, pool_w, pool_scale)], axis=-1) @ w_out
    return y_l, y_c


def gla_chunk_scan(q, k, v, logf, s0, reverse):
    if reverse:
        q, k, v, logf = (jnp.flip(t, axis=1) for t in (q, k, v, logf))
    bsz, L, nh, _ = q.shape
    nc = L // HG_CHUNK
    to_chunks = lambda t: t.reshape(bsz, nc, HG_CHUNK, nh, t.shape[-1]).transpose(1, 0, 3, 2, 4)
    causal = jnp.tril(jnp.ones((HG_CHUNK, HG_CHUNK), dtype=bool))[:, :, None]

    def step(s, xs):
        qi, ki, vi, gi = xs
        G = jnp.cumsum(gi, axis=2)
        o_inter = jnp.einsum('bhtk,bhkv->bhtv', qi * jnp.exp(G), s)
        diff = G[:, :, :, None, :] - G[:, :, None, :, :]
        decay = jnp.exp(jnp.where(causal, diff, -jnp.inf))
        a = jnp.einsum('bhtk,bhsk,bhtsk->bhts', qi, ki, decay)
        o = o_inter + jnp.einsum('bhts,bhsv->bhtv', a, vi)
        g_last = G[:, :, -1:, :]
        s_new = (jnp.exp(g_last[:, :, 0, :, None]) * s
                 + jnp.einsum('bhsk,bhsv->bhkv', ki * jnp.exp(g_last - G), vi))
        return s_new, o

    s_fin, o = lax.scan(step, s0, (to_chunks(q), to_chunks(k), to_chunks(v), to_chunks(logf)))
    o = o.transpose(1, 0, 3, 2, 4).reshape(bsz, L, nh, HG_DV)
    if reverse:
        o = jnp.flip(o, axis=1)
    return o, s_fin


def hg_project(x, w_in, lower_bounds):
    bsz, L, _ = x.shape
    p = (x @ w_in).astype(jnp.float32)
    q, i, f_fw, f_bw, g = jnp.split(
        p, [HG_KDIM, HG_KDIM + HG_VDIM, 2 * HG_KDIM + HG_VDIM, 3 * HG_KDIM + HG_VDIM], axis=-1)
    hk = lambda t: t.reshape(bsz, L, HG_HEADS, HG_DK)
    lb = lower_bounds.astype(jnp.float32)
    forget = [hk(lb[d] + (1.0 - lb[d]) * jax.nn.sigmoid(f)) for d, f in enumerate((f_fw, f_bw))]
    return hk(jax.nn.silu(q)), i.reshape(bsz, L, HG_HEADS, HG_DV), forget, g


def hg_readout(o, g, gnorm, w_out, dtype):
    bsz, L = o.shape[:2]
    o = o * lax.rsqrt(jnp.mean(o * o, axis=-1, keepdims=True) + EPS) * gnorm.astype(jnp.float32)
    o = o.reshape(bsz, L, HG_VDIM) * jax.nn.silu(g)
    return o.astype(dtype) @ w_out


def hg_mixer(x_lat, x_ctx, w_in, lower_bounds, gnorm, w_out, need_ctx_out):
    q_l, i_l, f_l, g_l = hg_project(x_lat, w_in, lower_bounds)
    q_c, i_c, f_c, g_c = hg_project(x_ctx, w_in, lower_bounds)
    bsz = x_lat.shape[0]
    outs_l, outs_c = [], []
    for d in range(2):
        rev = d == 1
        s0 = jnp.zeros((bsz, HG_HEADS, HG_DK, HG_DV), jnp.float32)
        o_c, s_ctx = gla_chunk_scan(q_c, 1.0 - f_c[d], i_c, jnp.log(f_c[d]), s0, rev)
        o_l, _ = gla_chunk_scan(q_l, 1.0 - f_l[d], i_l, jnp.log(f_l[d]), s_ctx, rev)
        outs_l.append(o_l)
        outs_c.append(o_c)
    y_l = hg_readout(outs_l[0] + outs_l[1], g_l, gnorm, w_out, x_lat.dtype)
    if not need_ctx_out:
        return y_l, None
    return y_l, hg_readout(outs_c[0] + outs_c[1], g_c, gnorm, w_out, x_ctx.dtype)


def setup_inputs(seed: int = 0) -> dict:
    key = jax.random.key(seed)
    ks = jax.random.split(key, 19)
    D = D_MODEL
    n_even = (DEPTH + 1) // 2
    n_odd = DEPTH // 2
    f32 = jnp.float32
    nrm = lambda k, shape: jax.random.normal(k, shape, f32)
    return {
        "x": nrm(ks[0], (BATCH, SEQ, D)),
        "c": nrm(ks[1], (BATCH, D)),
        "ctx": nrm(ks[2], (BATCH, CTX_LEN, D)),
        "c_ctx": nrm(ks[3], (D,)),
        "w_mod": nrm(ks[4], (DEPTH, D, N_MOD * D)) * (0.5 * D ** -0.5),
        "b_mod": nrm(ks[5], (DEPTH, N_MOD * D)) * 0.02,
        "norm_g": 1.0 + 0.1 * nrm(ks[6], (DEPTH, 3, D)),
        "ffn_w13": nrm(ks[7], (DEPTH, 2, D, 2 * D_FF)) * D ** -0.5,
        "ffn_w2": nrm(ks[8], (DEPTH, 2, D_FF, D)) * D_FF ** -0.5,
        "ab_w_in": nrm(ks[9], (n_even, D, 3 * A_WIDTH + B_WIDTH)) * D ** -0.5,
        "ab_rpb": 0.5 * nrm(ks[10], (n_even, A_HEADS, 2 * WIN_R - 1, 2 * WIN_C - 1)),
        "ab_pool_w": nrm(ks[11], (n_even, len(POOL_WINDOWS), B_GROUP, B_GROUP)) * B_GROUP ** -0.5,
        "ab_pool_scale": 1.0 + 0.1 * nrm(ks[12], (n_even, B_WIDTH)),
        "ab_w_out": nrm(ks[13], (n_even, A_WIDTH + B_WIDTH, D)) * (A_WIDTH + B_WIDTH) ** -0.5,
        "hg_w_in": nrm(ks[14], (n_odd, D, 3 * HG_KDIM + 2 * HG_VDIM)) * D ** -0.5,
        "hg_lb_logits": 0.5 * nrm(ks[15], (DEPTH, 2, HG_KDIM)),
        "hg_gnorm": 1.0 + 0.1 * nrm(ks[16], (n_odd, HG_DV)),
        "hg_w_out": nrm(ks[17], (n_odd, HG_VDIM, D)) * HG_VDIM ** -0.5,
        "final_g": 1.0 + 0.1 * nrm(ks[18], (D,)),
    }


def reference(x, c, ctx, c_ctx, w_mod, b_mod, norm_g, ffn_w13, ffn_w2, ab_w_in, ab_rpb, ab_pool_w,
              ab_pool_scale, ab_w_out, hg_w_in, hg_lb_logits, hg_gnorm, hg_w_out, final_g):
    sm = jax.nn.softmax(hg_lb_logits.astype(jnp.float32), axis=0)
    lb_all = jnp.cumsum(sm, axis=0) - sm[0:1]
    silu_c = jax.nn.silu(c)
    silu_cc = jax.nn.silu(c_ctx)
    h_lat, h_ctx = x, ctx
    for l in range(DEPTH):
        last = l == DEPTH - 1
        ml = jnp.split((silu_c @ w_mod[l] + b_mod[l])[:, None, :], N_MOD, axis=-1)
        mc = jnp.split(silu_cc @ w_mod[l] + b_mod[l], N_MOD, axis=-1)
        h_lat = h_lat + 0.5 * ml[2] * swiglu(adaln_in(h_lat, norm_g[l, 0], ml[0], ml[1]), ffn_w13[l, 0], ffn_w2[l, 0])
        h_ctx = h_ctx + 0.5 * mc[2] * swiglu(adaln_in(h_ctx, norm_g[l, 0], mc[0], mc[1]), ffn_w13[l, 0], ffn_w2[l, 0])
        xl = adaln_in(h_lat, norm_g[l, 1], ml[3], ml[4])
        xc = adaln_in(h_ctx, norm_g[l, 1], mc[3], mc[4])
        j = l // 2
        if l % 2 == 0:
            y_lat, y_ctx = ab_mixer(xl, xc, ab_w_in[j], ab_rpb[j], ab_pool_w[j], ab_pool_scale[j], ab_w_out[j],
                                    not last)
        else:
            y_lat, y_ctx = hg_mixer(xl, xc, hg_w_in[j], lb_all[l], hg_gnorm[j], hg_w_out[j], not last)
        h_lat = h_lat + ml[5] * y_lat
        h_lat = h_lat + 0.5 * ml[8] * swiglu(adaln_in(h_lat, norm_g[l, 2], ml[6], ml[7]), ffn_w13[l, 1], ffn_w2[l, 1])
        if not last:
            h_ctx = h_ctx + mc[5] * y_ctx
            h_ctx = h_ctx + 0.5 * mc[8] * swiglu(adaln_in(h_ctx, norm_g[l, 2], mc[6], mc[7]), ffn_w13[l, 1],
                                                 ffn_w2[l, 1])
    return rms_norm(h_lat, final_g)
```

```python
import bisect
import numpy as np
import concourse.bass as bass
import concourse.mybir as mybir
from concourse.bass_utils import run_bass_kernel_spmd

F32 = mybir.dt.float32
BF16 = mybir.dt.bfloat16
AF = mybir.ActivationFunctionType
ALU = mybir.AluOpType

D = 1024
KC = 8
SEQ = 2048
CTX = 256
NT = SEQ + CTX
DFF = 2816
NFC = 22
EPS = 1e-6
N_CORES = 8
ARENA_BYTES = 206 * 1024


class _Seg:
    __slots__ = ("w", "r")

    def __init__(self, w=None, r=None):
        self.w = w
        self.r = r or []


class _Space:
    def __init__(self, size):
        self.starts = [0]
        self.segs = [_Seg()]
        self.size = size

    def _split(self, pos):
        i = bisect.bisect_right(self.starts, pos) - 1
        if self.starts[i] == pos:
            return i
        s = self.segs[i]
        self.starts.insert(i + 1, pos)
        self.segs.insert(i + 1, _Seg(s.w, list(s.r)))
        return i + 1

    def access(self, lo, hi, is_write, me, deps_out, war_only_other=None):
        assert 0 <= lo < hi <= self.size, (lo, hi, self.size)
        i0 = self._split(lo)
        i1 = self._split(hi) if hi < self.size else len(self.starts)
        for i in range(i0, i1):
            s = self.segs[i]
            if s.w is not None:
                deps_out.append(("raw" if not is_write else "waw", s.w))
            if is_write:
                for r in s.r:
                    deps_out.append(("war", r))
                s.w = me
                s.r = []
            else:
                s.r.append(me)
                if len(s.r) > 6:
                    best = {}
                    for k, v in s.r:
                        if best.get(k, -1) < v:
                            best[k] = v
                    s.r = list(best.items())
        if is_write and i1 - i0 > 1:
            del self.starts[i0 + 1:i1]
            del self.segs[i0 + 1:i1]


def _ap_interval(ap):
    pat = ap.ap
    esz = {F32: 4, BF16: 2}[ap.dtype]
    pstep = pat[0][0]
    off = ap.offset % pstep if pstep > 0 else ap.offset
    span = 0
    for st, n in pat[1:]:
        span += abs(st) * (n - 1)
    return off * esz, (off + span + 1) * esz


class Prog:
    ENG = ("pe", "act", "dve", "pool", "sp")
    EPOCH = 24000

    def __init__(self, nc, stack):
        self.nc = nc
        self.stack = stack
        self.e = {"pe": nc.tensor, "act": nc.scalar, "dve": nc.vector, "pool": nc.gpsimd, "sp": nc.sync}
        self.sem = {}
        self.cnt = {}
        self.epoch = {}
        for n in ("pe", "act", "dve", "pool"):
            self.epoch[n] = 0
            self._new_epoch_sem(n)
        self.seen = {n: {} for n in self.ENG}
        self.spaces = {}
        self.dram = {}
        self.n_inst = 0

    def _new_epoch_sem(self, n):
        k = (n, self.epoch[n])
        self.sem[k] = self.stack.enter_context(self.nc.semaphore("s_%s_%d" % (n, self.epoch[n])))
        self.cnt[k] = 0

    def space_of(self, ap):
        nm = ap.tensor.name
        sp = self.spaces.get(nm)
        if sp is None:
            esz = {F32: 4, BF16: 2}[ap.dtype]
            sp = self.spaces[nm] = _Space(ap.ap[0][0] * esz)
        return sp

    def dsem(self, key):
        k = ("d", key)
        if k not in self.sem:
            self.sem[k] = self.stack.enter_context(self.nc.semaphore("d_" + str(key)))
            self.cnt[k] = 0
        return k

    def _wait(self, eng, dep):
        k, v = dep
        sn = self.seen[eng]
        if sn.get(k, 0) >= v:
            return
        if k[0] != "d":
            for (n2, ep2) in list(sn.keys()):
                if n2 == k[0] and ep2 > k[1]:
                    return
        self.e[eng].wait_ge(self.sem[k], v)
        sn[k] = v

    def _collect(self, eng, me, reads, writes, dram_r=(), dram_w=(), is_dma=False):
        deps = []
        for ap in reads:
            lo, hi = _ap_interval(ap)
            self.space_of(ap).access(lo, hi, False, me, deps)
        for ap in writes:
            lo, hi = _ap_interval(ap)
            self.space_of(ap).access(lo, hi, True, me, deps)
        for key in dram_r:
            s = self.dram.setdefault(key, _Seg())
            if s.w is not None:
                deps.append(("raw", s.w))
            s.r.append(me)
        for key in dram_w:
            s = self.dram.setdefault(key, _Seg())
            if s.w is not None:
                deps.append(("waw", s.w))
            for r in s.r:
                deps.append(("war", r))
            s.w = me
            s.r = []
        best = {}
        for kind, (k, v) in deps:
            if (k, v) == me:
                continue
            if (not is_dma) and k[0] == eng:
                if eng == "pe" or kind == "war":
                    continue
            if best.get(k, 0) < v:
                best[k] = v
        for k, v in best.items():
            self._wait(eng, (k, v))

    def op(self, eng, fn, reads=(), writes=()):
        if self.cnt[(eng, self.epoch[eng])] >= self.EPOCH:
            self.epoch[eng] += 1
            self._new_epoch_sem(eng)
        k = (eng, self.epoch[eng])
        me = (k, self.cnt[k] + 1)
        self._collect(eng, me, reads, writes)
        ins = fn(self.e[eng])
        ins.then_inc(self.sem[k], 1)
        self.cnt[k] += 1
        self.n_inst += 1
        return ins

    def dma(self, q, out, in_, semkey, sb_reads=(), sb_writes=(), dram_r=(), dram_w=()):
        k = self.dsem(semkey)
        me = (k, self.cnt[k] + 16)
        if self.cnt[k] > 0:
            self._wait(q, (k, self.cnt[k]))
        self._collect(q, me, sb_reads, sb_writes, dram_r, dram_w, is_dma=True)
        ins = self.e[q].dma_start(out=out, in_=in_)
        ins.then_inc(self.sem[k], 16)
        self.cnt[k] += 16
        self.n_inst += 1
        return me

    def wait_all(self, eng):
        for k, v in self.cnt.items():
            if v > 0 and k[0] != eng:
                self._wait(eng, (k, v))


class Arena:
    def __init__(self, ap_f32):
        self.base = ap_f32
        self.size = ARENA_BYTES
        self.top = 0

    def mark(self):
        return self.top

    def reset(self, m):
        self.top = m

    def alloc(self, shape_free, dtype):
        esz = {F32: 4, BF16: 2}[dtype]
        n = int(np.prod(shape_free))
        nbytes = (n * esz + 31) // 32 * 32
        off = self.top
        assert off + nbytes <= self.size, ("arena overflow", off, nbytes, self.size)
        self.top = off + nbytes
        self.last_off = off
        return self.view(off, shape_free, dtype)

    def view(self, off, shape_free, dtype):
        esz = {F32: 4, BF16: 2}[dtype]
        n = int(np.prod(shape_free))
        nbytes = (n * esz + 31) // 32 * 32
        assert off % 32 == 0 and off + nbytes <= self.size
        v = self.base[:, off // 4: (off + nbytes) // 4]
        if dtype == BF16:
            v = v.bitcast(BF16)
        v = v[:, 0:n]
        if len(shape_free) == 1:
            return v
        names = " ".join("a%d" % i for i in range(len(shape_free)))
        kw = {"a%d" % i: int(s) for i, s in enumerate(shape_free)}
        return v.rearrange("p (%s) -> p %s" % (names, names), **kw)


TOK_TILES = [(0, 256), (256, 512), (768, 512), (1280, 512), (1792, 512)]
LAT_TILES = TOK_TILES[1:]
FBLOCKS = [(0, 3), (3, 3), (6, 3), (9, 2)]


def build_nc(NB, phases):
    from contextlib import ExitStack
    nc = bass.Bass("TRN2", target_bir_lowering=False)
    dt = nc.dram_tensor
    xT = dt("xT", [NB, 128, KC, NT], F32, kind="ExternalInput").ap()
    cT = dt("cT", [128, KC, 3], F32, kind="ExternalInput").ap()
    wmod = dt("wmod", [2, 72, 128, KC, 128], F32, kind="ExternalInput").ap()
    bmodT = dt("bmodT", [128, 2, 72], F32, kind="ExternalInput").ap()
    gT = dt("gT", [128, 2, 3, KC], F32, kind="ExternalInput").ap()
    fgT = dt("fgT", [128, KC], F32, kind="ExternalInput").ap()
    w13r = dt("w13r", [4, 11, 128, KC, 512], F32, kind="ExternalInput").ap()
    w2r = dt("w2r", [4, NFC, 128, D], F32, kind="ExternalInput").ap()
    abw = dt("abw", [5, 128, KC, 512], F32, kind="ExternalInput").ap()
    abwo = dt("abwo", [2, 128, KC, 512], F32, kind="ExternalInput").ap()
    bandD = dt("band", [128, 4, 5, 128], F32, kind="ExternalInput").ap()
    pwD = dt("pw", [128, 4, 128], F32, kind="ExternalInput").ap()
    pscD = dt("psc", [128, 4], F32, kind="ExternalInput").ap()
    tblP = dt("tblP", [4, 128, 2, 14, 64], F32, kind="ExternalInput").ap()
    tblO = dt("tblO", [4, 128, 2, 5, 64], F32, kind="ExternalInput").ap()
    identD = dt("ident", [128, 128], F32, kind="ExternalInput").ap()
    hgw = dt("hgw", [10, 128, KC, 512], F32, kind="ExternalInput").ap()
    hgwo = dt("hgwo", [2, 128, 4, D], F32, kind="ExternalInput").ap()
    lblD = dt("lbl", [128, 2, 2, D], F32, kind="ExternalInput").ap()
    gnD = dt("gn", [128, 1], F32, kind="ExternalInput").ap()
    hgcD = dt("hgc", [128, 5, 128], F32, kind="ExternalInput").ap()
    ofwD = dt("ofw", [2, 16, 128, 512], F32).ap()
    gscD = dt("gsc", [2, 128, 4, SEQ], BF16).ap()
    outT = dt("outT", [NB, 128, KC, SEQ], F32, kind="ExternalOutput").ap()

    with ExitStack() as st:
        arena_t = st.enter_context(nc.sbuf_tensor("arena", [128, ARENA_BYTES // 4], F32))
        psum_t = st.enter_context(nc.psum_tensor("ps", [128, 8, 512], F32))
        P = Prog(nc, st)
        A = Arena(arena_t[:, :])
        ps = psum_t

        H = A.alloc([KC, NT], F32)
        modT = A.alloc([2, 72, 3], F32)
        gS = A.alloc([2, 3, KC], F32)
        fgS = A.alloc([KC], F32)
        AB = A.alloc([2, 3, 3, KC, 3], F32)
        silc = A.alloc([KC, 3], F32)
        base_mark = A.mark()

        P.dma("sp", gS, gT, "c0", sb_writes=[gS])
        P.dma("sp", fgS, fgT, "c1", sb_writes=[fgS])
        cst = A.alloc([KC, 3], F32)
        bmS = A.alloc([2, 72], F32)
        P.dma("sp", cst, cT, "c2", sb_writes=[cst])
        P.dma("sp", bmS, bmodT, "c3", sb_writes=[bmS])
        P.op("act", lambda e: e.activation(out=silc, in_=cst, func=AF.Silu), reads=[cst], writes=[silc])

        GRP = 6
        wst = [A.alloc([GRP, KC, 128], BF16) for _ in range(2)]
        silb = A.alloc([KC, 3], BF16)
        P.op("dve", lambda e: e.tensor_copy(out=silb, in_=silc), reads=[silc], writes=[silb])
        gi = 0
        for l in range(2):
            mps = ps[:, 7, 0:216].rearrange("p (m j) -> p m j", j=3)
            for g0 in range(0, 72, GRP):
                wb = wst[gi % 2]
                P.dma("pool", wb, wmod[l, g0:g0 + GRP].rearrange("m p k c -> p m k c"), "wm%d" % (gi % 2),
                      sb_writes=[wb])
                for m in range(GRP):
                    for k in range(KC):
                        P.op("pe", lambda e, m=m, k=k, wb=wb, g0=g0: e.matmul(
                            mps[:, g0 + m, :], lhsT=wb[:, m, k, :], rhs=silb[:, k, :], start=(k == 0), stop=(k == KC - 1)),
                            reads=[wb[:, m, k, :], silb[:, k, :]], writes=[mps[:, g0 + m, :]])
                gi += 1
            P.op("dve", lambda e, l=l, mps=mps: e.tensor_tensor(
                out=modT[:, l], in0=mps, in1=bmS[:, l].unsqueeze(2).to_broadcast([128, 72, 3]), op=ALU.add),
                reads=[mps, bmS[:, l]], writes=[modT[:, l]])
        for l in range(2):
            for n in range(3):
                sc = modT[:, l, (3 * n + 1) * 8:(3 * n + 2) * 8, :]
                sh = modT[:, l, (3 * n) * 8:(3 * n + 1) * 8, :]
                gt = modT[:, l, (3 * n + 2) * 8:(3 * n + 3) * 8, :]
                P.op("dve", lambda e, l=l, n=n, sc=sc: e.scalar_tensor_tensor(
                    out=AB[:, l, n, 0], in0=sc, scalar=1.0, in1=gS[:, l, n].unsqueeze(2).to_broadcast([128, KC, 3]),
                    op0=ALU.add, op1=ALU.mult), reads=[sc, gS[:, l, n]], writes=[AB[:, l, n, 0]])
                P.op("dve", lambda e, l=l, n=n, sh=sh: e.tensor_copy(out=AB[:, l, n, 1], in_=sh),
                     reads=[sh], writes=[AB[:, l, n, 1]])
                gmul = 1.0 if n == 1 else 0.5
                P.op("dve", lambda e, l=l, n=n, gt=gt, gmul=gmul: e.tensor_scalar(
                    out=AB[:, l, n, 2], in0=gt, scalar1=gmul, scalar2=None, op0=ALU.mult),
                    reads=[gt], writes=[AB[:, l, n, 2]])
        A.reset(base_mark)

        def colj(t0, nb):
            return 2 if t0 < CTX else nb

        def emit_rstd(rs, stp):
            P.op("act", lambda e: e.activation(out=rs, in_=stp, func=AF.Ln, bias=epsT[:, 0:1], scale=1.0),
                 reads=[stp, epsT[:, 0:1]], writes=[rs])
            P.op("act", lambda e: e.activation(out=rs, in_=rs, func=AF.Exp, scale=-0.5), reads=[rs], writes=[rs])

        def emit_norm(l, n, nb, xn, tiles, scr):
            sqs, rstd2, t1b = scr

            def front(ti):
                t0, tn = tiles[ti]
                sq = sqs[ti % 2]
                P.op("act", lambda e: e.activation(out=sq[:, 0:4, 0:tn], in_=H[:, 0:4, t0:t0 + tn], func=AF.Square),
                     reads=[H[:, 0:4, t0:t0 + tn]], writes=[sq[:, 0:4, 0:tn]])
                P.op("dve", lambda e: e.tensor_tensor(out=sq[:, 4:8, 0:tn], in0=H[:, 4:8, t0:t0 + tn], in1=H[:, 4:8, t0:t0 + tn], op=ALU.mult),
                     reads=[H[:, 4:8, t0:t0 + tn]], writes=[sq[:, 4:8, 0:tn]])
                stp = ps[:, 6, 0:tn]
                for k in range(KC):
                    P.op("pe", lambda e, k=k: e.matmul(stp, lhsT=onesM[:, :], rhs=sq[:, k, 0:tn], start=(k == 0), stop=(k == KC - 1)),
                         reads=[onesM[:, :], sq[:, k, 0:tn]], writes=[stp])
                emit_rstd(rstd2[ti % 2][:, 0:tn], stp)

            def back(ti):
                t0, tn = tiles[ti]
                j = colj(t0, nb)
                rs = rstd2[ti % 2][:, 0:tn]
                for k in range(KC):
                    t1 = t1b[k % 2][:, 0:tn]
                    P.op("dve", lambda e, k=k, t1=t1: e.scalar_tensor_tensor(
                        out=t1, in0=H[:, k, t0:t0 + tn], scalar=AB[:, l, n, 0, k, j:j + 1], in1=rs,
                        op0=ALU.mult, op1=ALU.mult),
                        reads=[H[:, k, t0:t0 + tn], AB[:, l, n, 0, k, j:j + 1], rs], writes=[t1])
                    P.op("act", lambda e, k=k, t1=t1: e.activation(
                        out=xn[:, k, t0:t0 + tn], in_=t1, func=AF.Identity, bias=AB[:, l, n, 1, k, j:j + 1], scale=1.0),
                        reads=[t1, AB[:, l, n, 1, k, j:j + 1]], writes=[xn[:, k, t0:t0 + tn]])

            front(0)
            for ti in range(len(tiles)):
                if ti + 1 < len(tiles):
                    front(ti + 1)
                back(ti)

        onesM = A.alloc([128], BF16)
        P.op("pool", lambda e: e.memset(onesM, 1.0 / 1024.0), writes=[onesM])
        epsT = A.alloc([8], F32)
        P.op("pool", lambda e: e.memset(epsT, EPS), writes=[epsT])
        ident = A.alloc([128], BF16)
        P.dma("pool", ident, identD, "c4", sb_writes=[ident])
        pscS = A.alloc([4], F32)
        P.dma("sp", pscS, pscD, "c5", sb_writes=[pscS])
        base_mark = A.mark()

        def emit_ffn(l, jf, nb, tiles):
            m0 = A.mark()
            n = 0 if jf == 0 else 2
            wi = l * 2 + jf
            xn = A.alloc([KC, NT], BF16)
            gB = A.alloc([6, NT], BF16)
            gboff = A.last_off
            w2b = [A.alloc([6, D], BF16) for _ in range(2)]
            ring = [A.alloc([KC, 512], BF16) for _ in range(2)]
            sq = A.alloc([KC, 512], BF16)
            rstd2 = [A.alloc([512], F32) for _ in range(2)]
            t1b = [A.alloc([512], F32) for _ in range(2)]
            sab = [A.alloc([512], F32) for _ in range(2)]
            sq2 = A.view(gboff, [KC, 512], BF16)
            emit_norm(l, n, nb, xn, tiles, ([sq, sq2], rstd2, t1b))
            it = 0
            for bi, (s0, ns) in enumerate(FBLOCKS):
                nf = 2 * ns
                w2 = w2b[bi % 2]
                P.dma("pool", w2[:, 0:nf, :], w2r[wi, 2 * s0:2 * s0 + nf].rearrange("f p d -> p f d"),
                      "w2_%d" % (bi % 2), sb_writes=[w2[:, 0:nf, :]])
                for s in range(s0, s0 + ns):
                    rg = ring[s % 2]
                    P.dma("pool", rg, w13r[wi, s], "rg%d" % (s % 2), sb_writes=[rg])
                    for c in range(2):
                        fl = (s - s0) * 2 + c
                        for (t0, tn) in tiles:
                            pa = ps[:, it % 2, 0:tn]
                            pb = ps[:, 2 + it % 2, 0:tn]
                            for k in range(KC):
                                P.op("pe", lambda e, k=k, c=c, rg=rg, t0=t0, tn=tn, pa=pa: e.matmul(
                                    pa, lhsT=rg[:, k, c * 128:(c + 1) * 128], rhs=xn[:, k, t0:t0 + tn],
                                    start=(k == 0), stop=(k == KC - 1)),
                                    reads=[rg[:, k, c * 128:(c + 1) * 128], xn[:, k, t0:t0 + tn]], writes=[pa])
                            for k in range(KC):
                                P.op("pe", lambda e, k=k, c=c, rg=rg, t0=t0, tn=tn, pb=pb: e.matmul(
                                    pb, lhsT=rg[:, k, 256 + c * 128:256 + (c + 1) * 128], rhs=xn[:, k, t0:t0 + tn],
                                    start=(k == 0), stop=(k == KC - 1)),
                                    reads=[rg[:, k, 256 + c * 128:256 + (c + 1) * 128], xn[:, k, t0:t0 + tn]], writes=[pb])
                            sa = sab[it % 2][:, 0:tn]
                            P.op("act", lambda e, sa=sa, pa=pa: e.activation(out=sa, in_=pa, func=AF.Silu),
                                 reads=[pa], writes=[sa])
                            P.op("dve", lambda e, sa=sa, pb=pb, fl=fl, t0=t0, tn=tn: e.tensor_tensor(
                                out=gB[:, fl, t0:t0 + tn], in0=sa, in1=pb, op=ALU.mult),
                                reads=[sa, pb], writes=[gB[:, fl, t0:t0 + tn]])
                            it += 1
                for (t0, tn) in tiles:
                    j = colj(t0, nb)
                    for m in range(KC):
                        py = ps[:, 4 + it % 2, 0:tn]
                        for f in range(nf):
                            P.op("pe", lambda e, f=f, m=m, w2=w2, t0=t0, tn=tn, py=py: e.matmul(
                                py, lhsT=w2[:, f, m * 128:(m + 1) * 128], rhs=gB[:, f, t0:t0 + tn],
                                start=(f == 0), stop=(f == nf - 1)),
                                reads=[w2[:, f, m * 128:(m + 1) * 128], gB[:, f, t0:t0 + tn]], writes=[py])
                        P.op("dve", lambda e, m=m, t0=t0, tn=tn, py=py, j=j: e.scalar_tensor_tensor(
                            out=H[:, m, t0:t0 + tn], in0=py, scalar=AB[:, l, n, 2, m, j:j + 1], in1=H[:, m, t0:t0 + tn],
                            op0=ALU.mult, op1=ALU.add),
                            reads=[py, AB[:, l, n, 2, m, j:j + 1], H[:, m, t0:t0 + tn]], writes=[H[:, m, t0:t0 + tn]])
                        it += 1
            A.reset(m0)

        def emit_mixA(nb):
            l = 0
            m0 = A.mark()
            xn = A.alloc([KC, NT], BF16)
            Ureg = A.alloc([18, 512], BF16)
            uoff = A.last_off
            aT = A.view(A.last_off, [4, NT], BF16)
            plT = A.alloc([4, NT], BF16)
            qT = A.alloc([NT], BF16)
            kT = A.alloc([NT], BF16)
            Vh = A.alloc([18, 2, 66], BF16)
            ring0 = A.alloc([KC, 512], BF16)
            r0off = A.last_off
            ring1 = A.alloc([KC, 512], BF16)
            r1off = A.last_off
            tP = A.view(r1off, [2, 14, 64], F32)
            sq = A.view(r0off, [KC, 512], BF16)
            rstd2 = [A.view(r1off + i * 2048, [512], F32) for i in range(2)]
            t1b = [A.view(r1off + 4096 + i * 2048, [512], F32) for i in range(2)]
            tO = A.alloc([2, 5, 64], F32)
            band = A.alloc([4, 5, 128], BF16)
            pw = A.alloc([4, 128], BF16)
            dT = A.alloc([4, 512], BF16)
            sbb = [A.alloc([5, 64], F32) for _ in range(2)]
            Ptb = [A.alloc([7, 64], BF16) for _ in range(4)]
            atok = [A.alloc([128], BF16) for _ in range(2)]
            rcb = [A.alloc([2], F32) for _ in range(2)]
            sq2 = A.view(uoff, [KC, 512], BF16)
            emit_norm(l, 1, nb, xn, TOK_TILES, ([sq, sq2], rstd2, t1b))
            P.dma("pool", band, bandD, "mA_band", sb_writes=[band])
            P.dma("pool", pw, pwD, "mA_pw", sb_writes=[pw])
            P.op("pool", lambda e: e.memset(Vh[:, :, :, 64:66], 1.0), writes=[Vh[:, :, :, 64:66]])

            P.dma("pool", ring0, abw[0], "rg0", sb_writes=[ring0])
            ev = 0
            for i in range(18):
                pu = ps[:, ev % 2, :]
                for k in range(KC):
                    P.op("pe", lambda e, k=k, i=i, pu=pu: e.matmul(pu, lhsT=xn[:, k, i * 128:(i + 1) * 128], rhs=ring0[:, k, :],
                                                                   start=(k == 0), stop=(k == KC - 1)),
                         reads=[xn[:, k, i * 128:(i + 1) * 128], ring0[:, k, :]], writes=[pu])
                eng = "act" if ev % 2 == 0 else "dve"
                if eng == "act":
                    P.op("act", lambda e, i=i, pu=pu: e.copy(out=Ureg[:, i, :], in_=pu), reads=[pu], writes=[Ureg[:, i, :]])
                else:
                    P.op("dve", lambda e, i=i, pu=pu: e.tensor_copy(out=Ureg[:, i, :], in_=pu), reads=[pu], writes=[Ureg[:, i, :]])
                ev += 1
            quads = [(0, 2, 0, 2), (2, 6, 2, 18), (6, 10, 2, 18), (10, 14, 2, 18), (14, 18, 2, 18)]
            for (i0, i1, sf, se) in quads:
                nt4 = i1 - i0
                for g in range(4):
                    pd = ps[:, 2 + g % 2, :]
                    for ii in range(nt4):
                        i = i0 + ii
                        terms = []
                        if i > sf:
                            terms.append((i - 1, 0))
                        terms.append((i, 3 if i == sf else (4 if i == se - 1 else 1)))
                        if i < se - 1:
                            terms.append((i + 1, 2))
                        for ti, (j, ty) in enumerate(terms):
                            P.op("pe", lambda e, j=j, ty=ty, g=g, ii=ii, ti=ti, nn=len(terms), pd=pd: e.matmul(
                                pd[:, ii * 128:(ii + 1) * 128], lhsT=Ureg[:, j, g * 128:(g + 1) * 128], rhs=band[:, g, ty, :],
                                start=(ti == 0), stop=(ti == nn - 1)),
                                reads=[Ureg[:, j, g * 128:(g + 1) * 128], band[:, g, ty, :]], writes=[pd[:, ii * 128:(ii + 1) * 128]])
                    P.op("act", lambda e, g=g, pd=pd, nt4=nt4: e.copy(out=dT[:, g, 0:nt4 * 128], in_=pd[:, 0:nt4 * 128]),
                         reads=[pd[:, 0:nt4 * 128]], writes=[dT[:, g, 0:nt4 * 128]])
                for g in range(4):
                    py = ps[:, 4 + g % 2, 0:nt4 * 128]
                    P.op("pe", lambda e, g=g, py=py, nt4=nt4: e.matmul(py, lhsT=pw[:, g, :], rhs=dT[:, g, 0:nt4 * 128], start=True, stop=True),
                         reads=[pw[:, g, :], dT[:, g, 0:nt4 * 128]], writes=[py])
                    P.op("act", lambda e, g=g, py=py, i0=i0, nt4=nt4: e.activation(
                        out=plT[:, g, i0 * 128:i0 * 128 + nt4 * 128], in_=py, func=AF.Copy, scale=pscS[:, g:g + 1]),
                        reads=[py, pscS[:, g:g + 1]], writes=[plT[:, g, i0 * 128:i0 * 128 + nt4 * 128]])

            it = 0
            for pr in range(4):
                P.dma("pool", ring0[:, :, 0:384], abw[1 + pr][:, :, 0:384], "rg0", sb_writes=[ring0[:, :, 0:384]])
                for (t0, tn) in TOK_TILES:
                    for which in range(2):
                        pq = ps[:, it % 2, 0:tn]
                        for k in range(KC):
                            P.op("pe", lambda e, k=k, which=which, t0=t0, tn=tn, pq=pq: e.matmul(
                                pq, lhsT=ring0[:, k, which * 128:(which + 1) * 128], rhs=xn[:, k, t0:t0 + tn],
                                start=(k == 0), stop=(k == KC - 1)),
                                reads=[ring0[:, k, which * 128:(which + 1) * 128], xn[:, k, t0:t0 + tn]], writes=[pq])
                        if which == 0:
                            P.op("act", lambda e, t0=t0, tn=tn, pq=pq: e.mul(out=qT[:, t0:t0 + tn], in_=pq, mul=0.125),
                                 reads=[pq], writes=[qT[:, t0:t0 + tn]])
                        else:
                            P.op("dve", lambda e, t0=t0, tn=tn, pq=pq: e.tensor_copy(out=kT[:, t0:t0 + tn], in_=pq),
                                 reads=[pq], writes=[kT[:, t0:t0 + tn]])
                        it += 1
                for i in range(18):
                    pv = ps[:, 2 + i % 2, 0:128]
                    for k in range(KC):
                        P.op("pe", lambda e, k=k, i=i, pv=pv: e.matmul(pv, lhsT=xn[:, k, i * 128:(i + 1) * 128], rhs=ring0[:, k, 256:384],
                                                                       start=(k == 0), stop=(k == KC - 1)),
                             reads=[xn[:, k, i * 128:(i + 1) * 128], ring0[:, k, 256:384]], writes=[pv])
                    P.op("act", lambda e, i=i, pv=pv: e.copy(out=Vh[:, i, :, 0:64], in_=pv.rearrange("p (h d) -> p h d", h=2)),
                         reads=[pv], writes=[Vh[:, i, :, 0:64]])
                P.dma("sp", tP, tblP[pr], "mA_tp", sb_writes=[tP])
                P.dma("sp", tO, tblO[pr], "mA_to", sb_writes=[tO])

                def finish_rows(O, nq, tq0, u):
                    rc = rcb[u % 2]
                    at = atok[u % 2]
                    P.op("dve", lambda e: e.reciprocal(out=rc[0:nq, :], in_=O[0:nq, :, 64]), reads=[O[0:nq, :, 64]], writes=[rc[0:nq, :]])
                    P.op("dve", lambda e: e.tensor_tensor(
                        out=at[0:nq, :].rearrange("p (h d) -> p h d", h=2), in0=O[0:nq, :, 0:64],
                        in1=rc[0:nq, :].unsqueeze(2).to_broadcast([nq, 2, 64]), op=ALU.mult),
                        reads=[O[0:nq, :, 0:64], rc[0:nq, :]], writes=[at[0:nq, :]])
                    tp = ps[:, 6 + u % 2, 0:nq]
                    P.op("pe", lambda e: e.matmul(tp, lhsT=at[0:nq, :], rhs=ident[0:nq, 0:nq], start=True, stop=True),
                         reads=[at[0:nq, :], ident[0:nq, 0:nq]], writes=[tp])
                    P.op("act", lambda e: e.copy(out=aT[:, pr, tq0:tq0 + nq], in_=tp), reads=[tp], writes=[aT[:, pr, tq0:tq0 + nq]])

                items = [("ctx", qi, e_) for qi in range(2) for e_ in range(2)] + [("lat", r, e_) for r in range(32) for e_ in range(2)]
                DEPTH = 2
                ctxs = {}

                def stageA(n):
                    kind, a, e_ = items[n]
                    pl, ph = e_ * 64, (e_ + 1) * 64
                    rowi = n // 2
                    Pt_full = Ptb[n % 4]
                    if kind == "ctx":
                        qi = a
                        S = ps[:, n % 4, 0:256].rearrange("p (c q) -> p c q", c=2)
                        for ci in range(2):
                            P.op("pe", lambda e, ci=ci: e.matmul(
                                S[:, ci, :], lhsT=kT[pl:ph, ci * 128:(ci + 1) * 128], rhs=qT[pl:ph, qi * 128:(qi + 1) * 128],
                                start=True, stop=True),
                                reads=[kT[pl:ph, ci * 128:(ci + 1) * 128], qT[pl:ph, qi * 128:(qi + 1) * 128]], writes=[S[:, ci, :]])
                        Pt = Pt_full[:, 0:4, :].rearrange("p a b -> p (a b)").rearrange("p (c q) -> p c q", c=2)
                        P.op("act", lambda e: e.activation(out=Pt, in_=S, func=AF.Exp), reads=[S], writes=[Pt])
                        ctxs[n] = dict(kind=kind, Pt=Pt, e_=e_, rowi=rowi, nq=128, tq0=qi * 128, tks=[0, 1])
                    else:
                        r = a
                        tq0 = CTX + 64 * r
                        r0 = min(max(r - 4, 0), 24)
                        kt0, kt1 = r0 // 2, (r0 + 7) // 2
                        nk = kt1 - kt0 + 1
                        if nk == 5:
                            tvv = tO[:, e_, 0:5, :]
                        else:
                            ty0 = 2 * kt0 - r + 7
                            tvv = tP[:, e_, ty0:ty0 + 7:2, :]
                        S = ps[:, n % 4, 0:448].rearrange("p (c q) -> p c q", c=7)
                        tks = [(2 + kt0 + idx) if idx < nk else (idx - nk) for idx in range(nk + 2)]
                        for idx, tk in enumerate(tks):
                            P.op("pe", lambda e, idx=idx, tk=tk: e.matmul(
                                S[:, idx, :], lhsT=kT[pl:ph, tk * 128:(tk + 1) * 128], rhs=qT[pl:ph, tq0:tq0 + 64],
                                start=True, stop=True),
                                reads=[kT[pl:ph, tk * 128:(tk + 1) * 128], qT[pl:ph, tq0:tq0 + 64]], writes=[S[:, idx, :]])
                        sb = sbb[n % 2]
                        Pt = Pt_full
                        P.op("dve", lambda e: e.tensor_tensor(out=sb[:, 0:nk, :], in0=S[:, 0:nk, :], in1=tvv, op=ALU.add),
                             reads=[S[:, 0:nk, :], tvv], writes=[sb[:, 0:nk, :]])
                        P.op("act", lambda e: e.activation(out=Pt[:, 0:nk, :], in_=sb[:, 0:nk, :], func=AF.Exp),
                             reads=[sb[:, 0:nk, :]], writes=[Pt[:, 0:nk, :]])
                        P.op("act", lambda e: e.activation(out=Pt[:, nk:nk + 2, :], in_=S[:, nk:nk + 2, :], func=AF.Exp),
                             reads=[S[:, nk:nk + 2, :]], writes=[Pt[:, nk:nk + 2, :]])
                        ctxs[n] = dict(kind=kind, Pt=Pt, e_=e_, rowi=rowi, nq=64, tq0=tq0, tks=tks)

                def stageB(n):
                    c = ctxs.pop(n)
                    Pt, e_, rowi, nq, tks = c["Pt"], c["e_"], c["rowi"], c["nq"], c["tks"]
                    O = ps[:, 4 + rowi % 2, 0:132].rearrange("p (h d) -> p h d", h=2)
                    last = len(tks) - 1
                    for idx, tk in enumerate(tks):
                        P.op("pe", lambda e, idx=idx, tk=tk: e.matmul(
                            O[0:nq, e_, 0:65], lhsT=Pt[:, idx, :], rhs=Vh[:, tk, e_, 0:65], start=(idx == 0), stop=(idx == last)),
                            reads=[Pt[:, idx, :], Vh[:, tk, e_, 0:65]], writes=[O[0:nq, e_, 0:65]])
                    if e_ == 1:
                        finish_rows(O, nq, c["tq0"], rowi)

                for n in range(len(items) + DEPTH):
                    if n < len(items):
                        stageA(n)
                    if n >= DEPTH:
                        stageB(n - DEPTH)
                it += 4 - (it % 4) if it % 4 else 0

            for sidx in range(2):
                rg = ring0 if sidx == 0 else ring1
                P.dma("pool", rg, abwo[sidx], "rg%d" % sidx, sb_writes=[rg])
                for c in range(4):
                    m = sidx * 4 + c
                    for (t0, tn) in TOK_TILES:
                        j = colj(t0, nb)
                        py = ps[:, it % 2, 0:tn]
                        for k in range(KC):
                            src = aT[:, k, t0:t0 + tn] if k < 4 else plT[:, k - 4, t0:t0 + tn]
                            P.op("pe", lambda e, k=k, c=c, rg=rg, src=src, py=py: e.matmul(
                                py, lhsT=rg[:, k, c * 128:(c + 1) * 128], rhs=src, start=(k == 0), stop=(k == KC - 1)),
                                reads=[rg[:, k, c * 128:(c + 1) * 128], src], writes=[py])
                        P.op("dve", lambda e, m=m, t0=t0, tn=tn, py=py, j=j: e.scalar_tensor_tensor(
                            out=H[:, m, t0:t0 + tn], in0=py, scalar=AB[:, l, 1, 2, m, j:j + 1], in1=H[:, m, t0:t0 + tn],
                            op0=ALU.mult, op1=ALU.add),
                            reads=[py, AB[:, l, 1, 2, m, j:j + 1], H[:, m, t0:t0 + tn]], writes=[H[:, m, t0:t0 + tn]])
                        it += 1
            A.reset(m0)

        def emit_mixC(nb):
            l = 1
            m0 = A.mark()
            xn = A.alloc([KC, NT], BF16)
            WQ = A.alloc([KC, 512], BF16)
            wqoff = A.last_off
            WI = A.alloc([KC, 512], BF16)
            wioff = A.last_off
            WF = A.alloc([KC, 512], BF16)
            wfoff = A.last_off
            ONT = A.alloc([4, SEQ], BF16)
            hgc = A.alloc([5, 128], F32)
            gnS = A.alloc([1], F32)
            one128 = A.alloc([128], BF16)
            lbT = A.alloc([512], F32)
            omlT = A.alloc([512], F32)
            SQt = A.alloc([512], F32)
            Vtb = [A.alloc([512], BF16) for _ in range(2)]
            fT = A.alloc([512], F32)
            LF = A.alloc([512], F32)
            kk = A.alloc([512], F32)
            eX = [A.alloc([512], F32) for _ in range(2)]
            qt = A.alloc([512], BF16)
            kt = A.alloc([512], BF16)
            KHb = [A.alloc([4, 512], BF16) for _ in range(2)]
            QTb = [A.alloc([4, 128], BF16) for _ in range(2)]
            KTb = [A.alloc([4, 128], BF16) for _ in range(2)]
            AM4 = A.alloc([4, 128], BF16)
            Spp = [A.alloc([4, 128], F32) for _ in range(2)]
            NS = 5
            snap = [A.alloc([4, 128], BF16) for _ in range(NS)]
            EGb = [A.alloc([4, 4], F32) for _ in range(2)]
            oS = A.alloc([512], F32)
            osq = A.alloc([512], BF16)
            rsT = A.alloc([512], F32)
            sgt = A.alloc([512], BF16)
            gst = [A.alloc([512], BF16) for _ in range(2)]
            sq = A.view(wfoff, [KC, 512], BF16)
            rstd2 = [A.view(wioff + i * 2048, [512], F32) for i in range(2)]
            t1b = [A.view(wioff + 4096 + i * 2048, [512], F32) for i in range(2)]
            sq2 = A.view(wqoff, [KC, 512], BF16)
            emit_norm(l, 1, nb, xn, TOK_TILES, ([sq, sq2], rstd2, t1b))
            P.dma("sp", hgc, hgcD, "mC_c0", sb_writes=[hgc])
            P.dma("sp", gnS, gnD, "mC_c1", sb_writes=[gnS])
            P.op("pool", lambda e: e.memset(one128, 1.0 / 128.0), writes=[one128])
            TRI = [hgc[:, 0, :], hgc[:, 1, :]]
            TRIC = [hgc[:, 2, :], hgc[:, 3, :]]
            CHK = hgc[:, 4, 0:4]
            st = {"u": 0, "gch": 0}

            for hh in range(2):
                P.dma("pool", WQ, hgw[hh * 5 + 0], "mC_wq", sb_writes=[WQ])
                P.dma("pool", WI, hgw[hh * 5 + 1], "mC_wi", sb_writes=[WI])
                P.dma("pool", WF, hgw[hh * 5 + 4], "mC_wf", sb_writes=[WF])
                gi = 0
                for hd in range(4):
                    for (t0, tn) in LAT_TILES:
                        pg = ps[:, gi % 2, :]
                        for k in range(KC):
                            P.op("pe", lambda e, hd=hd, k=k, t0=t0, tn=tn, pg=pg: e.matmul(pg, lhsT=WF[:, k, hd * 128:(hd + 1) * 128], rhs=xn[:, k, t0:t0 + tn],
                                                                                    start=(k == 0), stop=(k == KC - 1)),
                                 reads=[WF[:, k, hd * 128:(hd + 1) * 128], xn[:, k, t0:t0 + tn]], writes=[pg])
                        gs = gst[gi % 2]
                        P.op("act", lambda e, gs=gs, pg=pg: e.activation(out=gs, in_=pg, func=AF.Silu), reads=[pg], writes=[gs])
                        P.dma("sp", gscD[hh, :, hd, t0 - CTX:t0 - CTX + tn], gs, "mC_gs%d" % (gi % 2), sb_reads=[gs],
                              dram_w=[("gsc", hh, hd, (t0 - CTX) // 512)])
                        gi += 1

                for d in range(2):
                    P.dma("sp", lbT, lblD[:, 1, d, hh * 512:(hh + 1) * 512], "mC_lb0", sb_writes=[lbT])
                    P.dma("sp", omlT, lblD[:, 0, d, hh * 512:(hh + 1) * 512], "mC_lb1", sb_writes=[omlT])
                    P.op("dve", lambda e: e.tensor_tensor(out=lbT, in0=lbT, in1=omlT, op=ALU.subtract), reads=[lbT, omlT], writes=[lbT])
                    P.op("act", lambda e: e.activation(out=lbT, in_=lbT, func=AF.Sigmoid), reads=[lbT], writes=[lbT])
                    P.op("dve", lambda e: e.tensor_scalar(out=omlT, in0=lbT, scalar1=-1.0, scalar2=1.0, op0=ALU.mult, op1=ALU.add),
                         reads=[lbT], writes=[omlT])
                    P.dma("pool", WF, hgw[hh * 5 + 2 + d], "mC_wf", sb_writes=[WF])
                    Scur = Spp[st["gch"] % 2]
                    P.op("pool", lambda e, Scur=Scur: e.memset(Scur, 0.0), writes=[Scur])
                    sn0 = snap[st["gch"] % NS]
                    P.op("pool", lambda e, sn0=sn0: e.memset(sn0, 0.0), writes=[sn0])
                    order = list(range(18)) if d == 0 else [1, 0] + list(range(17, 1, -1))
                    corder = [0, 1, 2, 3] if d == 0 else [3, 2, 1, 0]
                    ctxs = {}

                    def stageA(n, d=d, order=order):
                        i = order[n]
                        u = st["u"]
                        st["u"] += 1
                        is_lat = i >= 2
                        tsl = slice(i * 128, (i + 1) * 128)
                        Vt, QTt, KTt, EGLt, KHm = Vtb[u % 2], QTb[u % 2], KTb[u % 2], EGb[u % 2], KHb[u % 2]
                        ctxs[n] = dict(i=i, is_lat=is_lat, Vt=Vt, QTt=QTt, KTt=KTt, EGLt=EGLt, KHm=KHm)
                        PQ, PI, PF = ps[:, 0, :], ps[:, 1, :], ps[:, 2, :]
                        for (W, Pp) in ((WI, PI), (WF, PF), (WQ, PQ)):
                            if W is WQ and not is_lat:
                                continue
                            for k in range(KC):
                                P.op("pe", lambda e, k=k, W=W, Pp=Pp: e.matmul(Pp, lhsT=xn[:, k, tsl], rhs=W[:, k, :],
                                                                            start=(k == 0), stop=(k == KC - 1)),
                                     reads=[xn[:, k, tsl], W[:, k, :]], writes=[Pp])
                        P.op("act", lambda e: e.copy(out=Vt, in_=PI), reads=[PI], writes=[Vt])
                        P.op("act", lambda e: e.activation(out=fT, in_=PF, func=AF.Sigmoid), reads=[PF], writes=[fT])
                        if is_lat:
                            P.op("act", lambda e: e.activation(out=SQt, in_=PQ, func=AF.Silu), reads=[PQ], writes=[SQt])
                        yield
                        P.op("dve", lambda e: e.tensor_tensor(out=fT, in0=fT, in1=omlT, op=ALU.mult), reads=[fT, omlT], writes=[fT])
                        P.op("dve", lambda e: e.tensor_tensor(out=fT, in0=fT, in1=lbT, op=ALU.add), reads=[fT, lbT], writes=[fT])
                        P.op("act", lambda e: e.activation(out=LF, in_=fT, func=AF.Ln), reads=[fT], writes=[LF])
                        P.op("pool", lambda e: e.tensor_scalar(out=kk, in0=fT, scalar1=-1.0, scalar2=1.0, op0=ALU.mult, op1=ALU.add),
                             reads=[fT], writes=[kk])
                        PG, PD = ps[:, 3, :], ps[:, 4, :]
                        PEG = ps[:, 5, 0:16].rearrange("p (h c) -> p h c", h=4)
                        P.op("pe", lambda e: e.matmul(PD, lhsT=TRIC[d], rhs=LF, start=True, stop=True), reads=[TRIC[d], LF], writes=[PD])
                        for hd in range(4):
                            P.op("pe", lambda e, hd=hd: e.matmul(PEG[:, hd, :], lhsT=LF[:, hd * 128:(hd + 1) * 128], rhs=CHK, start=True, stop=True),
                                 reads=[LF[:, hd * 128:(hd + 1) * 128], CHK], writes=[PEG[:, hd, :]])
                        if is_lat:
                            P.op("pe", lambda e: e.matmul(PG, lhsT=TRI[d], rhs=LF, start=True, stop=True), reads=[TRI[d], LF], writes=[PG])
                        yield
                        P.op("act", lambda e: e.activation(out=eX[0], in_=PD, func=AF.Exp), reads=[PD], writes=[eX[0]])
                        P.op("act", lambda e: e.activation(out=EGLt, in_=PEG, func=AF.Exp), reads=[PEG], writes=[EGLt])
                        for c in range(4):
                            P.op("dve", lambda e, c=c: e.scalar_tensor_tensor(out=KHm[:, c, :], in0=kk, scalar=CHK[:, c:c + 1], in1=eX[0],
                                                                             op0=ALU.mult, op1=ALU.mult),
                                 reads=[kk, CHK[:, c:c + 1], eX[0]], writes=[KHm[:, c, :]])
                        yield
                        if not is_lat:
                            return
                        P.op("act", lambda e: e.activation(out=eX[1], in_=PG, func=AF.Exp), reads=[PG], writes=[eX[1]])
                        P.op("dve", lambda e: e.tensor_tensor(out=qt, in0=SQt, in1=eX[1], op=ALU.mult), reads=[SQt, eX[1]], writes=[qt])
                        P.op("act", lambda e: e.activation(out=eX[0], in_=PG, func=AF.Exp, scale=-1.0), reads=[PG], writes=[eX[0]])
                        P.op("dve", lambda e: e.tensor_tensor(out=kt, in0=kk, in1=eX[0], op=ALU.mult), reads=[kk, eX[0]], writes=[kt])
                        yield
                        PQT = ps[:, 3, :].rearrange("p (h t) -> p h t", h=4)
                        PKT = ps[:, 4, :].rearrange("p (h t) -> p h t", h=4)
                        for hd in range(4):
                            P.op("pe", lambda e, hd=hd: e.matmul(PQT[:, hd, :], lhsT=qt[:, hd * 128:(hd + 1) * 128], rhs=ident[:, :], start=True, stop=True),
                                 reads=[qt[:, hd * 128:(hd + 1) * 128], ident[:, :]], writes=[PQT[:, hd, :]])
                        for hd in range(4):
                            P.op("pe", lambda e, hd=hd: e.matmul(PKT[:, hd, :], lhsT=kt[:, hd * 128:(hd + 1) * 128], rhs=ident[:, :], start=True, stop=True),
                                 reads=[kt[:, hd * 128:(hd + 1) * 128], ident[:, :]], writes=[PKT[:, hd, :]])
                        P.op("act", lambda e: e.copy(out=QTt, in_=PQT), reads=[PQT], writes=[QTt])
                        P.op("dve", lambda e: e.tensor_copy(out=KTt, in_=PKT), reads=[PKT], writes=[KTt])

                    def stageB(n, d=d, corder=corder, hh=hh):
                        c_ = ctxs.pop(n)
                        i, is_lat, Vt, QTt, KTt, EGLt, KHm = c_["i"], c_["is_lat"], c_["Vt"], c_["QTt"], c_["KTt"], c_["EGLt"], c_["KHm"]
                        g0 = st["gch"]
                        PKV = ps[:, 7, :].rearrange("p (h v) -> p h v", h=4)
                        for ci, c in enumerate(corder):
                            gch = st["gch"]
                            for hd in range(4):
                                P.op("pe", lambda e, hd=hd, c=c: e.matmul(PKV[:, hd, :], lhsT=KHm[:, c, hd * 128:(hd + 1) * 128],
                                                                          rhs=Vt[:, hd * 128:(hd + 1) * 128], start=True, stop=True),
                                     reads=[KHm[:, c, hd * 128:(hd + 1) * 128], Vt[:, hd * 128:(hd + 1) * 128]], writes=[PKV[:, hd, :]])
                            Sa, Sb = Spp[gch % 2], Spp[(gch + 1) % 2]
                            P.op("dve", lambda e, c=c, Sa=Sa, Sb=Sb: e.tensor_tensor(
                                out=Sb, in0=Sa, in1=EGLt[:, :, c:c + 1].to_broadcast([128, 4, 128]), op=ALU.mult),
                                reads=[Sa, EGLt[:, :, c:c + 1]], writes=[Sb])
                            P.op("dve", lambda e, Sb=Sb: e.tensor_tensor(out=Sb, in0=Sb, in1=PKV, op=ALU.add),
                                 reads=[Sb, PKV], writes=[Sb])
                            sn = snap[(gch + 1) % NS]
                            P.op("act", lambda e, Sb=Sb, sn=sn: e.copy(out=sn, in_=Sb), reads=[Sb], writes=[sn])
                            st["gch"] += 1
                            yield
                        if not is_lat:
                            return
                        PO = ps[:, 6, :].rearrange("p (h t) -> p h t", h=4)
                        P.op("dve", lambda e: e.memset(ps[:, 6, :], 0.0), writes=[ps[:, 6, :]])
                        PA4 = ps[:, 7, :].rearrange("p (h t) -> p h t", h=4)
                        for hd in range(4):
                            P.op("pe", lambda e, hd=hd: e.matmul(PA4[:, hd, :], lhsT=KTt[:, hd, :], rhs=QTt[:, hd, :], start=True, stop=True),
                                 reads=[KTt[:, hd, :], QTt[:, hd, :]], writes=[PA4[:, hd, :]])
                        P.op("dve", lambda e: e.tensor_tensor(out=AM4, in0=PA4, in1=TRI[d].unsqueeze(1).to_broadcast([128, 4, 128]), op=ALU.mult),
                             reads=[PA4, TRI[d]], writes=[AM4])
                        for hd in range(4):
                            P.op("pe", lambda e, hd=hd: e.matmul(PO[:, hd, :], lhsT=Vt[:, hd * 128:(hd + 1) * 128], rhs=AM4[:, hd, :],
                                                                start=False, stop=False, skip_group_check=True),
                                 reads=[Vt[:, hd * 128:(hd + 1) * 128], AM4[:, hd, :]], writes=[PO[:, hd, :]])
                        yield
                        for ci, c in enumerate(corder):
                            sn = snap[(g0 + ci) % NS]
                            for hd in range(4):
                                P.op("pe", lambda e, hd=hd, c=c, ci=ci, sn=sn: e.matmul(
                                    PO[:, hd, c * 32:(c + 1) * 32], lhsT=sn[:, hd, :], rhs=QTt[:, hd, c * 32:(c + 1) * 32],
                                    start=False, stop=(ci == 3), skip_group_check=True),
                                    reads=[sn[:, hd, :], QTt[:, hd, c * 32:(c + 1) * 32]], writes=[PO[:, hd, c * 32:(c + 1) * 32]])
                        yield
                        li = i - 2
                        POf = ps[:, 6, :]
                        if d == 0:
                            P.op("act", lambda e: e.copy(out=oS, in_=POf), reads=[POf], writes=[oS])
                            P.dma("sp", ofwD[hh, li], oS, "mC_ost", sb_reads=[oS], dram_w=[("ofw", hh, li)])
                        else:
                            P.dma("sp", oS, ofwD[hh, li], "mC_old", sb_writes=[oS], dram_r=[("ofw", hh, li)])
                            P.dma("sp", sgt.rearrange("p (h t) -> p h t", h=4), gscD[hh, :, :, li * 128:(li + 1) * 128], "mC_gld",
                                  sb_writes=[sgt], dram_r=[("gsc", hh, hd_, li // 4) for hd_ in range(4)])
                            P.op("dve", lambda e: e.tensor_tensor(out=oS, in0=POf, in1=oS, op=ALU.add), reads=[POf, oS], writes=[oS])
                            P.op("act", lambda e: e.activation(out=osq, in_=oS, func=AF.Square), reads=[oS], writes=[osq])
                            PST = ps[:, 7, :]
                            P.op("pe", lambda e: e.matmul(PST, lhsT=one128[:, :], rhs=osq, start=True, stop=True), reads=[one128[:, :], osq], writes=[PST])
                            emit_rstd(rsT, PST)
                            P.op("dve", lambda e: e.scalar_tensor_tensor(out=oS, in0=oS, scalar=gnS[:, 0:1], in1=rsT, op0=ALU.mult, op1=ALU.mult),
                                 reads=[oS, gnS[:, 0:1], rsT], writes=[oS])
                            P.op("dve", lambda e: e.tensor_tensor(out=ONT[:, :, li * 128:(li + 1) * 128],
                                                                 in0=oS.rearrange("p (h t) -> p h t", h=4),
                                                                 in1=sgt.rearrange("p (h t) -> p h t", h=4), op=ALU.mult),
                                 reads=[oS, sgt], writes=[ONT[:, :, li * 128:(li + 1) * 128]])

                    nU = len(order)
                    for n in range(nU + 1):
                        gens = []
                        if n < nU:
                            gens.append(stageA(n))
                        if n >= 1:
                            gens.append(stageB(n - 1))
                        while gens:
                            for g_ in list(gens):
                                try:
                                    next(g_)
                                except StopIteration:
                                    gens.remove(g_)
                wo4 = A.view(wfoff, [4, D], BF16)
                P.dma("pool", wo4, hgwo[hh], "mC_wf", sb_writes=[wo4])
                it = 0
                for m in range(KC):
                    for (t0, tn) in LAT_TILES:
                        py = ps[:, it % 2, 0:tn]
                        for k in range(4):
                            P.op("pe", lambda e, k=k, m=m, t0=t0, tn=tn, py=py: e.matmul(
                                py, lhsT=wo4[:, k, m * 128:(m + 1) * 128], rhs=ONT[:, k, t0 - CTX:t0 - CTX + tn], start=(k == 0), stop=(k == 3)),
                                reads=[wo4[:, k, m * 128:(m + 1) * 128], ONT[:, k, t0 - CTX:t0 - CTX + tn]], writes=[py])
                        P.op("dve", lambda e, m=m, t0=t0, tn=tn, py=py: e.scalar_tensor_tensor(
                            out=H[:, m, t0:t0 + tn], in0=py, scalar=AB[:, l, 1, 2, m, nb:nb + 1], in1=H[:, m, t0:t0 + tn],
                            op0=ALU.mult, op1=ALU.add),
                            reads=[py, AB[:, l, 1, 2, m, nb:nb + 1], H[:, m, t0:t0 + tn]], writes=[H[:, m, t0:t0 + tn]])
                        it += 1
            A.reset(m0)

        def emit_final(b):
            m0 = A.mark()
            sq = A.alloc([KC, 512], BF16)
            rs = A.alloc([512], F32)
            ob = [A.alloc([KC, 512], F32) for _ in range(2)]
            for ti, (t0, tn) in enumerate(LAT_TILES):
                P.op("act", lambda e, t0=t0, tn=tn: e.activation(out=sq, in_=H[:, :, t0:t0 + tn], func=AF.Square),
                     reads=[H[:, :, t0:t0 + tn]], writes=[sq])
                stp = ps[:, 6, 0:tn]
                for k in range(KC):
                    P.op("pe", lambda e, k=k, stp=stp: e.matmul(stp, lhsT=onesM[:, :], rhs=sq[:, k, :],
                                                                 start=(k == 0), stop=(k == KC - 1)),
                         reads=[onesM[:, :], sq[:, k, :]], writes=[stp])
                emit_rstd(rs, stp)
                o = ob[ti % 2]
                for k in range(KC):
                    P.op("dve", lambda e, k=k, t0=t0, tn=tn, o=o: e.scalar_tensor_tensor(
                        out=o[:, k, :], in0=H[:, k, t0:t0 + tn], scalar=fgS[:, k:k + 1], in1=rs,
                        op0=ALU.mult, op1=ALU.mult),
                        reads=[H[:, k, t0:t0 + tn], fgS[:, k:k + 1], rs], writes=[o[:, k, :]])
                P.dma("sp", outT[b, :, :, t0 - CTX:t0 - CTX + tn], o, "out%d" % (ti % 2), sb_reads=[o])
            A.reset(m0)

        for b in range(NB):
            for k in range(KC):
                P.dma("sp", H[:, k, :], xT[b, :, k, :], "hload%d" % k, sb_writes=[H[:, k, :]])
            for ph in phases:
                if ph == "ffn00":
                    emit_ffn(0, 0, b, TOK_TILES)
                elif ph == "ffn01":
                    emit_ffn(0, 1, b, TOK_TILES)
                elif ph == "ffn10":
                    emit_ffn(1, 0, b, TOK_TILES)
                elif ph == "ffn11":
                    emit_ffn(1, 1, b, LAT_TILES)
                elif ph == "mixA":
                    emit_mixA(b)
                elif ph == "mixC":
                    emit_mixC(b)
            emit_final(b)
        for eng in ("sp", "pool", "act", "dve", "pe"):
            P.wait_all(eng)
        print("instructions:", P.n_inst, "sems:", len(P.sem))
    return nc


def _fm(v):
    v = np.asarray(v, np.float32)
    lead = v.shape[:-1]
    r = v.reshape(lead + (KC, 128))
    r = np.moveaxis(r, -1, 0)
    return np.ascontiguousarray(r)


def _slot(w):
    return np.ascontiguousarray(np.asarray(w, np.float32).reshape(KC, 128, -1).transpose(1, 0, 2))


def _band_mats():
    out = np.zeros((128, 4, 5, 128), np.float32)
    L = 384
    t = np.arange(L)
    for g, w in enumerate((2, 4, 8, 16)):
        lo = np.clip(t - w // 2, 0, L)
        hi = np.clip(t - w // 2 + w, 0, L)
        s = np.arange(L)[:, None]
        M = ((s >= lo[None, :]) & (s < hi[None, :])).astype(np.float64) / (hi - lo)[None, :] - np.eye(L)
        blk = lambda a, b: M[a * 128:(a + 1) * 128, b * 128:(b + 1) * 128]
        out[:, g, 0] = blk(0, 1)
        out[:, g, 1] = blk(1, 1)
        out[:, g, 2] = blk(2, 1)
        out[:, g, 3] = blk(0, 0)
        out[:, g, 4] = blk(2, 2)
    return out


def _bias_tables(rpb):
    rpb = np.asarray(rpb, np.float32)
    NEG = np.float32(-30000.0)
    c = np.arange(64)
    kc = np.arange(64)
    win0 = np.clip(c - 8, 0, 48)
    ok = (kc[:, None] >= win0[None, :]) & (kc[:, None] < win0[None, :] + 16)
    rel = np.clip(kc[:, None] - c[None, :], -15, 15) + 15

    def rowtab(h, dr, valid=True):
        if not valid or dr < -7 or dr > 7:
            return np.full((64, 64), NEG, np.float32)
        return np.where(ok, rpb[h, dr + 7][rel], NEG).astype(np.float32)

    tP = np.zeros((4, 128, 2, 14, 64), np.float32)
    tO = np.zeros((4, 128, 2, 5, 64), np.float32)
    for h in range(8):
        pr, e = h // 2, h % 2
        for i, dr0 in enumerate(range(-7, 7)):
            tP[pr, 0:64, e, i] = rowtab(h, dr0)
            tP[pr, 64:128, e, i] = rowtab(h, dr0 + 1)
        for i, dr0 in enumerate((-5, -3, -1, 1, 3)):
            tO[pr, 0:64, e, i] = rowtab(h, dr0, valid=(dr0 != -5))
            tO[pr, 64:128, e, i] = rowtab(h, dr0 + 1, valid=(dr0 != 3))
    return tP, tO


def _hg_consts():
    s = np.arange(128)[:, None]
    t = np.arange(128)[None, :]
    same = (s // 32) == (t // 32)
    out = np.zeros((128, 5, 128), np.float32)
    out[:, 0] = same & (s <= t)
    out[:, 1] = same & (s >= t)
    out[:, 2] = same & (s > t)
    out[:, 3] = same & (s < t)
    for c in range(4):
        out[:, 4, c] = (np.arange(128) // 32) == c
    return out


def prep_shared(inp):
    sh = {}
    wm = np.asarray(inp["w_mod"], np.float32)
    sh["wmod"] = np.ascontiguousarray(wm.reshape(2, KC, 128, 72, 128).transpose(0, 3, 2, 1, 4))
    bm = np.asarray(inp["b_mod"], np.float32).reshape(2, 72, 128)
    sh["bmodT"] = np.ascontiguousarray(bm.transpose(2, 0, 1))
    sh["gT"] = _fm(inp["norm_g"])
    sh["fgT"] = _fm(inp["final_g"])
    w13 = np.asarray(inp["ffn_w13"], np.float32).reshape(4, KC, 128, 2, 11, 2, 128)
    sh["w13r"] = np.ascontiguousarray(w13.transpose(0, 4, 2, 1, 3, 5, 6)).reshape(4, 11, 128, KC, 512)
    sh["w2r"] = np.ascontiguousarray(np.asarray(inp["ffn_w2"], np.float32).reshape(4, NFC, 128, D))
    wi = np.asarray(inp["ab_w_in"], np.float32)[0]
    slots = [wi[:, 1536:2048]]
    for pr in range(4):
        sl = np.zeros((D, 512), np.float32)
        sl[:, 0:128] = wi[:, pr * 128:(pr + 1) * 128]
        sl[:, 128:256] = wi[:, 512 + pr * 128:512 + (pr + 1) * 128]
        sl[:, 256:384] = wi[:, 1024 + pr * 128:1024 + (pr + 1) * 128]
        slots.append(sl)
    sh["abw"] = np.stack([_slot(x) for x in slots])
    wo = np.asarray(inp["ab_w_out"], np.float32)[0]
    sh["abwo"] = np.stack([_slot(wo[:, 0:512]), _slot(wo[:, 512:1024])])
    sh["band"] = _band_mats()
    sh["pw"] = np.ascontiguousarray(np.asarray(inp["ab_pool_w"], np.float32)[0].transpose(1, 0, 2))
    sh["psc"] = np.ascontiguousarray(np.asarray(inp["ab_pool_scale"], np.float32)[0].reshape(4, 128).T)
    sh["tblP"], sh["tblO"] = _bias_tables(inp["ab_rpb"][0])
    sh["ident"] = np.eye(128, dtype=np.float32)
    hw = np.asarray(inp["hg_w_in"], np.float32)[0]
    sl = []
    for hh in range(2):
        for base in (0, 1024, 2048, 3072, 4096):
            sl.append(_slot(hw[:, base + hh * 512: base + (hh + 1) * 512]))
    sh["hgw"] = np.stack(sl)
    hwo = np.asarray(inp["hg_w_out"], np.float32)[0]
    sh["hgwo"] = np.ascontiguousarray(hwo.reshape(2, 4, 128, D).transpose(0, 2, 1, 3))
    lbl = np.asarray(inp["hg_lb_logits"], np.float32)
    sh["lbl"] = np.ascontiguousarray(np.broadcast_to(lbl[None], (128, 2, 2, D)))
    sh["gn"] = np.ascontiguousarray(np.asarray(inp["hg_gnorm"], np.float32)[0].reshape(128, 1))
    sh["hgc"] = _hg_consts()
    return sh


def prep_core(inp, b0, NB):
    x = np.asarray(inp["x"], np.float32)
    ctx = np.asarray(inp["ctx"], np.float32)
    xT = np.empty((NB, 128, KC, NT), np.float32)
    for i in range(NB):
        full = np.concatenate([ctx[b0 + i], x[b0 + i]], axis=0)
        xT[i] = full.T.reshape(KC, 128, NT).transpose(1, 0, 2)
    c = np.asarray(inp["c"], np.float32)
    cols = [c[b0 + i] for i in range(NB)]
    while len(cols) < 2:
        cols.append(cols[-1])
    cols.append(np.asarray(inp["c_ctx"], np.float32))
    cT = np.stack(cols, axis=-1).reshape(KC, 128, 3).transpose(1, 0, 2)
    return {"xT": xT, "cT": np.ascontiguousarray(cT)}


ALL_PHASES = ["ffn00", "mixA", "ffn01", "ffn10", "mixC", "ffn11"]
_CACHE = {}


def run(inp, NB=2, n_cores=N_CORES, phases=ALL_PHASES, b_start=0, trace=False):
    key = (NB, tuple(phases))
    if key not in _CACHE:
        _CACHE[key] = build_nc(NB, phases)
    nc = _CACHE[key]
    sh = prep_shared(inp)
    in_maps = []
    for ci in range(n_cores):
        m = dict(sh)
        m.update(prep_core(inp, b_start + ci * NB, NB))
        in_maps.append(m)
    res = run_bass_kernel_spmd(nc, in_maps, core_ids=list(range(n_cores)), trace=trace)
    outs = []
    for r in res.results:
        o = np.asarray(r["outT"])
        outs.append(o.transpose(0, 3, 2, 1).reshape(NB, SEQ, D))
    return np.concatenate(outs, axis=0), res


def kernel(**inputs):
    out, _ = run(inputs)
    return out.astype(np.float32)
```

```python
import bisect
import numpy as np
import concourse.bass as bass
import concourse.mybir as mybir
from concourse.bass_utils import run_bass_kernel_spmd

F32 = mybir.dt.float32
BF16 = mybir.dt.bfloat16
AF = mybir.ActivationFunctionType
ALU = mybir.AluOpType

D = 1024
KC = 8
SEQ = 2048
CTX = 256
NT = SEQ + CTX
DFF = 2816
NFC = 22
EPS = 1e-6
N_CORES = 8
ARENA_BYTES = 206 * 1024


class _Seg:
    __slots__ = ("w", "r")

    def __init__(self, w=None, r=None):
        self.w = w
        self.r = r or []


class _Space:
    def __init__(self, size):
        self.starts = [0]
        self.segs = [_Seg()]
        self.size = size

    def _split(self, pos):
        i = bisect.bisect_right(self.starts, pos) - 1
        if self.starts[i] == pos:
            return i
        s = self.segs[i]
        self.starts.insert(i + 1, pos)
        self.segs.insert(i + 1, _Seg(s.w, list(s.r)))
        return i + 1

    def access(self, lo, hi, is_write, me, deps_out, war_only_other=None):
        assert 0 <= lo < hi <= self.size, (lo, hi, self.size)
        i0 = self._split(lo)
        i1 = self._split(hi) if hi < self.size else len(self.starts)
        for i in range(i0, i1):
            s = self.segs[i]
            if s.w is not None:
                deps_out.append(("raw" if not is_write else "waw", s.w))
            if is_write:
                for r in s.r:
                    deps_out.append(("war", r))
                s.w = me
                s.r = []
            else:
                s.r.append(me)
                if len(s.r) > 6:
                    best = {}
                    for k, v in s.r:
                        if best.get(k, -1) < v:
                            best[k] = v
                    s.r = list(best.items())
        if is_write and i1 - i0 > 1:
            del self.starts[i0 + 1:i1]
            del self.segs[i0 + 1:i1]


def _ap_interval(ap):
    pat = ap.ap
    esz = {F32: 4, BF16: 2}[ap.dtype]
    pstep = pat[0][0]
    off = ap.offset % pstep if pstep > 0 else ap.offset
    span = 0
    for st, n in pat[1:]:
        span += abs(st) * (n - 1)
    return off * esz, (off + span + 1) * esz


class Prog:
    ENG = ("pe", "act", "dve", "pool", "sp")
    EPOCH = 24000

    def __init__(self, nc, stack):
        self.nc = nc
        self.stack = stack
        self.e = {"pe": nc.tensor, "act": nc.scalar, "dve": nc.vector, "pool": nc.gpsimd, "sp": nc.sync}
        self.sem = {}
        self.cnt = {}
        self.epoch = {}
        for n in ("pe", "act", "dve", "pool"):
            self.epoch[n] = 0
            self._new_epoch_sem(n)
        self.seen = {n: {} for n in self.ENG}
        self.spaces = {}
        self.dram = {}
        self.n_inst = 0

    def _new_epoch_sem(self, n):
        k = (n, self.epoch[n])
        self.sem[k] = self.stack.enter_context(self.nc.semaphore("s_%s_%d" % (n, self.epoch[n])))
        self.cnt[k] = 0

    def space_of(self, ap):
        nm = ap.tensor.name
        sp = self.spaces.get(nm)
        if sp is None:
            esz = {F32: 4, BF16: 2}[ap.dtype]
            sp = self.spaces[nm] = _Space(ap.ap[0][0] * esz)
        return sp

    def dsem(self, key):
        k = ("d", key)
        if k not in self.sem:
            self.sem[k] = self.stack.enter_context(self.nc.semaphore("d_" + str(key)))
            self.cnt[k] = 0
        return k

    def _wait(self, eng, dep):
        k, v = dep
        sn = self.seen[eng]
        if sn.get(k, 0) >= v:
            return
        if k[0] != "d":
            for (n2, ep2) in list(sn.keys()):
                if n2 == k[0] and ep2 > k[1]:
                    return
        self.e[eng].wait_ge(self.sem[k], v)
        sn[k] = v

    def _collect(self, eng, me, reads, writes, dram_r=(), dram_w=(), is_dma=False):
        deps = []
        for ap in reads:
            lo, hi = _ap_interval(ap)
            self.space_of(ap).access(lo, hi, False, me, deps)
        for ap in writes:
            lo, hi = _ap_interval(ap)
            self.space_of(ap).access(lo, hi, True, me, deps)
        for key in dram_r:
            s = self.dram.setdefault(key, _Seg())
            if s.w is not None:
                deps.append(("raw", s.w))
            s.r.append(me)
        for key in dram_w:
            s = self.dram.setdefault(key, _Seg())
            if s.w is not None:
                deps.append(("waw", s.w))
            for r in s.r:
                deps.append(("war", r))
            s.w = me
            s.r = []
        best = {}
        for kind, (k, v) in deps:
            if (k, v) == me:
                continue
            if (not is_dma) and k[0] == eng:
                if eng == "pe" or kind == "war":
                    continue
            if best.get(k, 0) < v:
                best[k] = v
        for k, v in best.items():
            self._wait(eng, (k, v))

    def op(self, eng, fn, reads=(), writes=()):
        if self.cnt[(eng, self.epoch[eng])] >= self.EPOCH:
            self.epoch[eng] += 1
            self._new_epoch_sem(eng)
        k = (eng, self.epoch[eng])
        me = (k, self.cnt[k] + 1)
        self._collect(eng, me, reads, writes)
        ins = fn(self.e[eng])
        ins.then_inc(self.sem[k], 1)
        self.cnt[k] += 1
        self.n_inst += 1
        return ins

    def dma(self, q, out, in_, semkey, sb_reads=(), sb_writes=(), dram_r=(), dram_w=()):
        k = self.dsem(semkey)
        me = (k, self.cnt[k] + 16)
        if self.cnt[k] > 0:
            self._wait(q, (k, self.cnt[k]))
        self._collect(q, me, sb_reads, sb_writes, dram_r, dram_w, is_dma=True)
        ins = self.e[q].dma_start(out=out, in_=in_)
        ins.then_inc(self.sem[k], 16)
        self.cnt[k] += 16
        self.n_inst += 1
        return me

    def wait_all(self, eng):
        for k, v in self.cnt.items():
            if v > 0 and k[0] != eng:
                self._wait(eng, (k, v))


class Arena:
    def __init__(self, ap_f32):
        self.base = ap_f32
        self.size = ARENA_BYTES
        self.top = 0

    def mark(self):
        return self.top

    def reset(self, m):
        self.top = m

    def alloc(self, shape_free, dtype):
        esz = {F32: 4, BF16: 2}[dtype]
        n = int(np.prod(shape_free))
        nbytes = (n * esz + 31) // 32 * 32
        off = self.top
        assert off + nbytes <= self.size, ("arena overflow", off, nbytes, self.size)
        self.top = off + nbytes
        self.last_off = off
        return self.view(off, shape_free, dtype)

    def view(self, off, shape_free, dtype):
        esz = {F32: 4, BF16: 2}[dtype]
        n = int(np.prod(shape_free))
        nbytes = (n * esz + 31) // 32 * 32
        assert off % 32 == 0 and off + nbytes <= self.size
        v = self.base[:, off // 4: (off + nbytes) // 4]
        if dtype == BF16:
            v = v.bitcast(BF16)
        v = v[:, 0:n]
        if len(shape_free) == 1:
            return v
        names = " ".join("a%d" % i for i in range(len(shape_free)))
        kw = {"a%d" % i: int(s) for i, s in enumerate(shape_free)}
        return v.rearrange("p (%s) -> p %s" % (names, names), **kw)


TOK_TILES = [(0, 256), (256, 512), (768, 512), (1280, 512), (1792, 512)]
LAT_TILES = TOK_TILES[1:]
FBLOCKS = [(0, 3), (3, 3), (6, 3), (9, 2)]


def build_nc(NB, phases):
    from contextlib import ExitStack
    nc = bass.Bass("TRN2", target_bir_lowering=False)
    dt = nc.dram_tensor
    xT = dt("xT", [NB, 128, KC, NT], F32, kind="ExternalInput").ap()
    cT = dt("cT", [128, KC, 3], F32, kind="ExternalInput").ap()
    wmod = dt("wmod", [2, 72, 128, KC, 128], F32, kind="ExternalInput").ap()
    bmodT = dt("bmodT", [128, 2, 72], F32, kind="ExternalInput").ap()
    gT = dt("gT", [128, 2, 3, KC], F32, kind="ExternalInput").ap()
    fgT = dt("fgT", [128, KC], F32, kind="ExternalInput").ap()
    w13r = dt("w13r", [4, 11, 128, KC, 512], F32, kind="ExternalInput").ap()
    w2r = dt("w2r", [4, NFC, 128, D], F32, kind="ExternalInput").ap()
    abw = dt("abw", [5, 128, KC, 512], F32, kind="ExternalInput").ap()
    abwo = dt("abwo", [2, 128, KC, 512], F32, kind="ExternalInput").ap()
    bandD = dt("band", [128, 4, 5, 128], F32, kind="ExternalInput").ap()
    pwD = dt("pw", [128, 4, 128], F32, kind="ExternalInput").ap()
    pscD = dt("psc", [128, 4], F32, kind="ExternalInput").ap()
    tblP = dt("tblP", [4, 128, 2, 14, 64], F32, kind="ExternalInput").ap()
    tblO = dt("tblO", [4, 128, 2, 5, 64], F32, kind="ExternalInput").ap()
    identD = dt("ident", [128, 128], F32, kind="ExternalInput").ap()
    hgw = dt("hgw", [10, 128, KC, 512], F32, kind="ExternalInput").ap()
    hgwo = dt("hgwo", [2, 128, 4, D], F32, kind="ExternalInput").ap()
    lblD = dt("lbl", [128, 2, 2, D], F32, kind="ExternalInput").ap()
    gnD = dt("gn", [128, 1], F32, kind="ExternalInput").ap()
    hgcD = dt("hgc", [128, 5, 128], F32, kind="ExternalInput").ap()
    ofwD = dt("ofw", [2, 16, 128, 512], F32).ap()
    gscD = dt("gsc", [2, 128, 4, SEQ], BF16).ap()
    outT = dt("outT", [NB, 128, KC, SEQ], F32, kind="ExternalOutput").ap()

    with ExitStack() as st:
        arena_t = st.enter_context(nc.sbuf_tensor("arena", [128, ARENA_BYTES // 4], F32))
        psum_t = st.enter_context(nc.psum_tensor("ps", [128, 8, 512], F32))
        P = Prog(nc, st)
        A = Arena(arena_t[:, :])
        ps = psum_t

        H = A.alloc([KC, NT], F32)
        modT = A.alloc([2, 72, 3], F32)
        gS = A.alloc([2, 3, KC], F32)
        fgS = A.alloc([KC], F32)
        AB = A.alloc([2, 3, 3, KC, 3], F32)
        silc = A.alloc([KC, 3], F32)
        base_mark = A.mark()

        P.dma("sp", gS, gT, "c0", sb_writes=[gS])
        P.dma("sp", fgS, fgT, "c1", sb_writes=[fgS])
        cst = A.alloc([KC, 3], F32)
        bmS = A.alloc([2, 72], F32)
        P.dma("sp", cst, cT, "c2", sb_writes=[cst])
        P.dma("sp", bmS, bmodT, "c3", sb_writes=[bmS])
        P.op("act", lambda e: e.activation(out=silc, in_=cst, func=AF.Silu), reads=[cst], writes=[silc])

        GRP = 6
        wst = [A.alloc([GRP, KC, 128], BF16) for _ in range(2)]
        silb = A.alloc([KC, 3], BF16)
        P.op("dve", lambda e: e.tensor_copy(out=silb, in_=silc), reads=[silc], writes=[silb])
        gi = 0
        for l in range(2):
            mps = ps[:, 7, 0:216].rearrange("p (m j) -> p m j", j=3)
            for g0 in range(0, 72, GRP):
                wb = wst[gi % 2]
                P.dma("pool", wb, wmod[l, g0:g0 + GRP].rearrange("m p k c -> p m k c"), "wm%d" % (gi % 2),
                      sb_writes=[wb])
                for m in range(GRP):
                    for k in range(KC):
                        P.op("pe", lambda e, m=m, k=k, wb=wb, g0=g0: e.matmul(
                            mps[:, g0 + m, :], lhsT=wb[:, m, k, :], rhs=silb[:, k, :], start=(k == 0), stop=(k == KC - 1)),
                            reads=[wb[:, m, k, :], silb[:, k, :]], writes=[mps[:, g0 + m, :]])
                gi += 1
            P.op("dve", lambda e, l=l, mps=mps: e.tensor_tensor(
                out=modT[:, l], in0=mps, in1=bmS[:, l].unsqueeze(2).to_broadcast([128, 72, 3]), op=ALU.add),
                reads=[mps, bmS[:, l]], writes=[modT[:, l]])
        for l in range(2):
            for n in range(3):
                sc = modT[:, l, (3 * n + 1) * 8:(3 * n + 2) * 8, :]
                sh = modT[:, l, (3 * n) * 8:(3 * n + 1) * 8, :]
                gt = modT[:, l, (3 * n + 2) * 8:(3 * n + 3) * 8, :]
                P.op("dve", lambda e, l=l, n=n, sc=sc: e.scalar_tensor_tensor(
                    out=AB[:, l, n, 0], in0=sc, scalar=1.0, in1=gS[:, l, n].unsqueeze(2).to_broadcast([128, KC, 3]),
                    op0=ALU.add, op1=ALU.mult), reads=[sc, gS[:, l, n]], writes=[AB[:, l, n, 0]])
                P.op("dve", lambda e, l=l, n=n, sh=sh: e.tensor_copy(out=AB[:, l, n, 1], in_=sh),
                     reads=[sh], writes=[AB[:, l, n, 1]])
                gmul = 1.0 if n == 1 else 0.5
                P.op("dve", lambda e, l=l, n=n, gt=gt, gmul=gmul: e.tensor_scalar(
                    out=AB[:, l, n, 2], in0=gt, scalar1=gmul, scalar2=None, op0=ALU.mult),
                    reads=[gt], writes=[AB[:, l, n, 2]])
        A.reset(base_mark)

        def colj(t0, nb):
            return 2 if t0 < CTX else nb

        def emit_rstd(rs, stp):
            P.op("act", lambda e: e.activation(out=rs, in_=stp, func=AF.Ln, bias=epsT[:, 0:1], scale=1.0),
                 reads=[stp, epsT[:, 0:1]], writes=[rs])
            P.op("act", lambda e: e.activation(out=rs, in_=rs, func=AF.Exp, scale=-0.5), reads=[rs], writes=[rs])

        def emit_norm(l, n, nb, xn, tiles, scr):
            sqs, rstd2, t1b = scr

            def front(ti):
                t0, tn = tiles[ti]
                sq = sqs[ti % 2]
                P.op("act", lambda e: e.activation(out=sq[:, 0:4, 0:tn], in_=H[:, 0:4, t0:t0 + tn], func=AF.Square),
                     reads=[H[:, 0:4, t0:t0 + tn]], writes=[sq[:, 0:4, 0:tn]])
                P.op("dve", lambda e: e.tensor_tensor(out=sq[:, 4:8, 0:tn], in0=H[:, 4:8, t0:t0 + tn], in1=H[:, 4:8, t0:t0 + tn], op=ALU.mult),
                     reads=[H[:, 4:8, t0:t0 + tn]], writes=[sq[:, 4:8, 0:tn]])
                stp = ps[:, 6, 0:tn]
                for k in range(KC):
                    P.op("pe", lambda e, k=k: e.matmul(stp, lhsT=onesM[:, :], rhs=sq[:, k, 0:tn], start=(k == 0), stop=(k == KC - 1)),
                         reads=[onesM[:, :], sq[:, k, 0:tn]], writes=[stp])
                emit_rstd(rstd2[ti % 2][:, 0:tn], stp)

            def back(ti):
                t0, tn = tiles[ti]
                j = colj(t0, nb)
                rs = rstd2[ti % 2][:, 0:tn]
                for k in range(KC):
                    t1 = t1b[k % 2][:, 0:tn]
                    P.op("dve", lambda e, k=k, t1=t1: e.scalar_tensor_tensor(
                        out=t1, in0=H[:, k, t0:t0 + tn], scalar=AB[:, l, n, 0, k, j:j + 1], in1=rs,
                        op0=ALU.mult, op1=ALU.mult),
                        reads=[H[:, k, t0:t0 + tn], AB[:, l, n, 0, k, j:j + 1], rs], writes=[t1])
                    P.op("act", lambda e, k=k, t1=t1: e.activation(
                        out=xn[:, k, t0:t0 + tn], in_=t1, func=AF.Identity, bias=AB[:, l, n, 1, k, j:j + 1], scale=1.0),
                        reads=[t1, AB[:, l, n, 1, k, j:j + 1]], writes=[xn[:, k, t0:t0 + tn]])

            front(0)
            for ti in range(len(tiles)):
                if ti + 1 < len(tiles):
                    front(ti + 1)
                back(ti)

        onesM = A.alloc([128], BF16)
        P.op("pool", lambda e: e.memset(onesM, 1.0 / 1024.0), writes=[onesM])
        epsT = A.alloc([8], F32)
        P.op("pool", lambda e: e.memset(epsT, EPS), writes=[epsT])
        ident = A.alloc([128], BF16)
        P.dma("pool", ident, identD, "c4", sb_writes=[ident])
        pscS = A.alloc([4], F32)
        P.dma("sp", pscS, pscD, "c5", sb_writes=[pscS])
        base_mark = A.mark()

        def emit_ffn(l, jf, nb, tiles):
            m0 = A.mark()
            n = 0 if jf == 0 else 2
            wi = l * 2 + jf
            xn = A.alloc([KC, NT], BF16)
            gB = A.alloc([6, NT], BF16)
            gboff = A.last_off
            w2b = [A.alloc([6, D], BF16) for _ in range(2)]
            ring = [A.alloc([KC, 512], BF16) for _ in range(2)]
            sq = A.alloc([KC, 512], BF16)
            rstd2 = [A.alloc([512], F32) for _ in range(2)]
            t1b = [A.alloc([512], F32) for _ in range(2)]
            sab = [A.alloc([512], F32) for _ in range(2)]
            sq2 = A.view(gboff, [KC, 512], BF16)
            emit_norm(l, n, nb, xn, tiles, ([sq, sq2], rstd2, t1b))
            it = 0
            for bi, (s0, ns) in enumerate(FBLOCKS):
                nf = 2 * ns
                w2 = w2b[bi % 2]
                P.dma("pool", w2[:, 0:nf, :], w2r[wi, 2 * s0:2 * s0 + nf].rearrange("f p d -> p f d"),
                      "w2_%d" % (bi % 2), sb_writes=[w2[:, 0:nf, :]])
                for s in range(s0, s0 + ns):
                    rg = ring[s % 2]
                    P.dma("pool", rg, w13r[wi, s], "rg%d" % (s % 2), sb_writes=[rg])
                    for c in range(2):
                        fl = (s - s0) * 2 + c
                        for (t0, tn) in tiles:
                            pa = ps[:, it % 2, 0:tn]
                            pb = ps[:, 2 + it % 2, 0:tn]
                            for k in range(KC):
                                P.op("pe", lambda e, k=k, c=c, rg=rg, t0=t0, tn=tn, pa=pa: e.matmul(
                                    pa, lhsT=rg[:, k, c * 128:(c + 1) * 128], rhs=xn[:, k, t0:t0 + tn],
                                    start=(k == 0), stop=(k == KC - 1)),
                                    reads=[rg[:, k, c * 128:(c + 1) * 128], xn[:, k, t0:t0 + tn]], writes=[pa])
                            for k in range(KC):
                                P.op("pe", lambda e, k=k, c=c, rg=rg, t0=t0, tn=tn, pb=pb: e.matmul(
                                    pb, lhsT=rg[:, k, 256 + c * 128:256 + (c + 1) * 128], rhs=xn[:, k, t0:t0 + tn],
                                    start=(k == 0), stop=(k == KC - 1)),
                                    reads=[rg[:, k, 256 + c * 128:256 + (c + 1) * 128], xn[:, k, t0:t0 + tn]], writes=[pb])
                            sa = sab[it % 2][:, 0:tn]
                            P.op("act", lambda e, sa=sa, pa=pa: e.activation(out=sa, in_=pa, func=AF.Silu),
                                 reads=[pa], writes=[sa])
                            P.op("dve", lambda e, sa=sa, pb=pb, fl=fl, t0=t0, tn=tn: e.tensor_tensor(
                                out=gB[:, fl, t0:t0 + tn], in0=sa, in1=pb, op=ALU.mult),
                                reads=[sa, pb], writes=[gB[:, fl, t0:t0 + tn]])
                            it += 1
                for (t0, tn) in tiles:
                    j = colj(t0, nb)
                    for m in range(KC):
                        py = ps[:, 4 + it % 2, 0:tn]
                        for f in range(nf):
                            P.op("pe", lambda e, f=f, m=m, w2=w2, t0=t0, tn=tn, py=py: e.matmul(
                                py, lhsT=w2[:, f, m * 128:(m + 1) * 128], rhs=gB[:, f, t0:t0 + tn],
                                start=(f == 0), stop=(f == nf - 1)),
                                reads=[w2[:, f, m * 128:(m + 1) * 128], gB[:, f, t0:t0 + tn]], writes=[py])
                        P.op("dve", lambda e, m=m, t0=t0, tn=tn, py=py, j=j: e.scalar_tensor_tensor(
                            out=H[:, m, t0:t0 + tn], in0=py, scalar=AB[:, l, n, 2, m, j:j + 1], in1=H[:, m, t0:t0 + tn],
                            op0=ALU.mult, op1=ALU.add),
                            reads=[py, AB[:, l, n, 2, m, j:j + 1], H[:, m, t0:t0 + tn]], writes=[H[:, m, t0:t0 + tn]])
                        it += 1
            A.reset(m0)

        def emit_mixA(nb):
            l = 0
            m0 = A.mark()
            xn = A.alloc([KC, NT], BF16)
            Ureg = A.alloc([18, 512], BF16)
            uoff = A.last_off
            aT = A.view(A.last_off, [4, NT], BF16)
            plT = A.alloc([4, NT], BF16)
            qT = A.alloc([NT], BF16)
            kT = A.alloc([NT], BF16)
            Vh = A.alloc([18, 2, 66], BF16)
            ring0 = A.alloc([KC, 512], BF16)
            r0off = A.last_off
            ring1 = A.alloc([KC, 512], BF16)
            r1off = A.last_off
            tP = A.view(r1off, [2, 14, 64], F32)
            sq = A.view(r0off, [KC, 512], BF16)
            rstd2 = [A.view(r1off + i * 2048, [512], F32) for i in range(2)]
            t1b = [A.view(r1off + 4096 + i * 2048, [512], F32) for i in range(2)]
            tO = A.alloc([2, 5, 64], F32)
            band = A.alloc([4, 5, 128], BF16)
            pw = A.alloc([4, 128], BF16)
            dT = A.alloc([4, 512], BF16)
            sbb = [A.alloc([5, 64], F32) for _ in range(2)]
            Ptb = [A.alloc([7, 64], BF16) for _ in range(4)]
            atok = [A.alloc([128], BF16) for _ in range(2)]
            rcb = [A.alloc([2], F32) for _ in range(2)]
            sq2 = A.view(uoff, [KC, 512], BF16)
            emit_norm(l, 1, nb, xn, TOK_TILES, ([sq, sq2], rstd2, t1b))
            P.dma("pool", band, bandD, "mA_band", sb_writes=[band])
            P.dma("pool", pw, pwD, "mA_pw", sb_writes=[pw])
            P.op("pool", lambda e: e.memset(Vh[:, :, :, 64:66], 1.0), writes=[Vh[:, :, :, 64:66]])

            P.dma("pool", ring0, abw[0], "rg0", sb_writes=[ring0])
            ev = 0
            for i in range(18):
                pu = ps[:, ev % 2, :]
                for k in range(KC):
                    P.op("pe", lambda e, k=k, i=i, pu=pu: e.matmul(pu, lhsT=xn[:, k, i * 128:(i + 1) * 128], rhs=ring0[:, k, :],
                                                                   start=(k == 0), stop=(k == KC - 1)),
                         reads=[xn[:, k, i * 128:(i + 1) * 128], ring0[:, k, :]], writes=[pu])
                eng = "act" if ev % 2 == 0 else "dve"
                if eng == "act":
                    P.op("act", lambda e, i=i, pu=pu: e.copy(out=Ureg[:, i, :], in_=pu), reads=[pu], writes=[Ureg[:, i, :]])
                else:
                    P.op("dve", lambda e, i=i, pu=pu: e.tensor_copy(out=Ureg[:, i, :], in_=pu), reads=[pu], writes=[Ureg[:, i, :]])
                ev += 1
            quads = [(0, 2, 0, 2), (2, 6, 2, 18), (6, 10, 2, 18), (10, 14, 2, 18), (14, 18, 2, 18)]
            for (i0, i1, sf, se) in quads:
                nt4 = i1 - i0
                for g in range(4):
                    pd = ps[:, 2 + g % 2, :]
                    for ii in range(nt4):
                        i = i0 + ii
                        terms = []
                        if i > sf:
                            terms.append((i - 1, 0))
                        terms.append((i, 3 if i == sf else (4 if i == se - 1 else 1)))
                        if i < se - 1:
                            terms.append((i + 1, 2))
                        for ti, (j, ty) in enumerate(terms):
                            P.op("pe", lambda e, j=j, ty=ty, g=g, ii=ii, ti=ti, nn=len(terms), pd=pd: e.matmul(
                                pd[:, ii * 128:(ii + 1) * 128], lhsT=Ureg[:, j, g * 128:(g + 1) * 128], rhs=band[:, g, ty, :],
                                start=(ti == 0), stop=(ti == nn - 1)),
                                reads=[Ureg[:, j, g * 128:(g + 1) * 128], band[:, g, ty, :]], writes=[pd[:, ii * 128:(ii + 1) * 128]])
                    P.op("act", lambda e, g=g, pd=pd, nt4=nt4: e.copy(out=dT[:, g, 0:nt4 * 128], in_=pd[:, 0:nt4 * 128]),
                         reads=[pd[:, 0:nt4 * 128]], writes=[dT[:, g, 0:nt4 * 128]])
                for g in range(4):
                    py = ps[:, 4 + g % 2, 0:nt4 * 128]
                    P.op("pe", lambda e, g=g, py=py, nt4=nt4: e.matmul(py, lhsT=pw[:, g, :], rhs=dT[:, g, 0:nt4 * 128], start=True, stop=True),
                         reads=[pw[:, g, :], dT[:, g, 0:nt4 * 128]], writes=[py])
                    P.op("act", lambda e, g=g, py=py, i0=i0, nt4=nt4: e.activation(
                        out=plT[:, g, i0 * 128:i0 * 128 + nt4 * 128], in_=py, func=AF.Copy, scale=pscS[:, g:g + 1]),
                        reads=[py, pscS[:, g:g + 1]], writes=[plT[:, g, i0 * 128:i0 * 128 + nt4 * 128]])

            it = 0
            for pr in range(4):
                P.dma("pool", ring0[:, :, 0:384], abw[1 + pr][:, :, 0:384], "rg0", sb_writes=[ring0[:, :, 0:384]])
                for (t0, tn) in TOK_TILES:
                    for which in range(2):
                        pq = ps[:, it % 2, 0:tn]
                        for k in range(KC):
                            P.op("pe", lambda e, k=k, which=which, t0=t0, tn=tn, pq=pq: e.matmul(
                                pq, lhsT=ring0[:, k, which * 128:(which + 1) * 128], rhs=xn[:, k, t0:t0 + tn],
                                start=(k == 0), stop=(k == KC - 1)),
                                reads=[ring0[:, k, which * 128:(which + 1) * 128], xn[:, k, t0:t0 + tn]], writes=[pq])
                        if which == 0:
                            P.op("act", lambda e, t0=t0, tn=tn, pq=pq: e.mul(out=qT[:, t0:t0 + tn], in_=pq, mul=0.125),
                                 reads=[pq], writes=[qT[:, t0:t0 + tn]])
                        else:
                            P.op("dve", lambda e, t0=t0, tn=tn, pq=pq: e.tensor_copy(out=kT[:, t0:t0 + tn], in_=pq),
                                 reads=[pq], writes=[kT[:, t0:t0 + tn]])
                        it += 1
                for i in range(18):
                    pv = ps[:, 2 + i % 2, 0:128]
                    for k in range(KC):
                        P.op("pe", lambda e, k=k, i=i, pv=pv: e.matmul(pv, lhsT=xn[:, k, i * 128:(i + 1) * 128], rhs=ring0[:, k, 256:384],
                                                                       start=(k == 0), stop=(k == KC - 1)),
                             reads=[xn[:, k, i * 128:(i + 1) * 128], ring0[:, k, 256:384]], writes=[pv])
                    P.op("act", lambda e, i=i, pv=pv: e.copy(out=Vh[:, i, :, 0:64], in_=pv.rearrange("p (h d) -> p h d", h=2)),
                         reads=[pv], writes=[Vh[:, i, :, 0:64]])
                P.dma("sp", tP, tblP[pr], "mA_tp", sb_writes=[tP])
                P.dma("sp", tO, tblO[pr], "mA_to", sb_writes=[tO])

                def finish_rows(O, nq, tq0, u):
                    rc = rcb[u % 2]
                    at = atok[u % 2]
                    P.op("dve", lambda e: e.reciprocal(out=rc[0:nq, :], in_=O[0:nq, :, 64]), reads=[O[0:nq, :, 64]], writes=[rc[0:nq, :]])
                    P.op("dve", lambda e: e.tensor_tensor(
                        out=at[0:nq, :].rearrange("p (h d) -> p h d", h=2), in0=O[0:nq, :, 0:64],
                        in1=rc[0:nq, :].unsqueeze(2).to_broadcast([nq, 2, 64]), op=ALU.mult),
                        reads=[O[0:nq, :, 0:64], rc[0:nq, :]], writes=[at[0:nq, :]])
                    tp = ps[:, 6 + u % 2, 0:nq]
                    P.op("pe", lambda e: e.matmul(tp, lhsT=at[0:nq, :], rhs=ident[0:nq, 0:nq], start=True, stop=True),
                         reads=[at[0:nq, :], ident[0:nq, 0:nq]], writes=[tp])
                    P.op("act", lambda e: e.copy(out=aT[:, pr, tq0:tq0 + nq], in_=tp), reads=[tp], writes=[aT[:, pr, tq0:tq0 + nq]])

                items = [("ctx", qi, e_) for qi in range(2) for e_ in range(2)] + [("lat", r, e_) for r in range(32) for e_ in range(2)]
                DEPTH = 2
                ctxs = {}

                def stageA(n):
                    kind, a, e_ = items[n]
                    pl, ph = e_ * 64, (e_ + 1) * 64
                    rowi = n // 2
                    Pt_full = Ptb[n % 4]
                    if kind == "ctx":
                        qi = a
                        S = ps[:, n % 4, 0:256].rearrange("p (c q) -> p c q", c=2)
                        for ci in range(2):
                            P.op("pe", lambda e, ci=ci: e.matmul(
                                S[:, ci, :], lhsT=kT[pl:ph, ci * 128:(ci + 1) * 128], rhs=qT[pl:ph, qi * 128:(qi + 1) * 128],
                                start=True, stop=True),
                                reads=[kT[pl:ph, ci * 128:(ci + 1) * 128], qT[pl:ph, qi * 128:(qi + 1) * 128]], writes=[S[:, ci, :]])
                        Pt = Pt_full[:, 0:4, :].rearrange("p a b -> p (a b)").rearrange("p (c q) -> p c q", c=2)
                        P.op("act", lambda e: e.activation(out=Pt, in_=S, func=AF.Exp), reads=[S], writes=[Pt])
                        ctxs[n] = dict(kind=kind, Pt=Pt, e_=e_, rowi=rowi, nq=128, tq0=qi * 128, tks=[0, 1])
                    else:
                        r = a
                        tq0 = CTX + 64 * r
                        r0 = min(max(r - 4, 0), 24)
                        kt0, kt1 = r0 // 2, (r0 + 7) // 2
                        nk = kt1 - kt0 + 1
                        if nk == 5:
                            tvv = tO[:, e_, 0:5, :]
                        else:
                            ty0 = 2 * kt0 - r + 7
                            tvv = tP[:, e_, ty0:ty0 + 7:2, :]
                        S = ps[:, n % 4, 0:448].rearrange("p (c q) -> p c q", c=7)
                        tks = [(2 + kt0 + idx) if idx < nk else (idx - nk) for idx in range(nk + 2)]
                        for idx, tk in enumerate(tks):
                            P.op("pe", lambda e, idx=idx, tk=tk: e.matmul(
                                S[:, idx, :], lhsT=kT[pl:ph, tk * 128:(tk + 1) * 128], rhs=qT[pl:ph, tq0:tq0 + 64],
                                start=True, stop=True),
                                reads=[kT[pl:ph, tk * 128:(tk + 1) * 128], qT[pl:ph, tq0:tq0 + 64]], writes=[S[:, idx, :]])
                        sb = sbb[n % 2]
                        Pt = Pt_full
                        P.op("dve", lambda e: e.tensor_tensor(out=sb[:, 0:nk, :], in0=S[:, 0:nk, :], in1=tvv, op=ALU.add),
                             reads=[S[:, 0:nk, :], tvv], writes=[sb[:, 0:nk, :]])
                        P.op("act", lambda e: e.activation(out=Pt[:, 0:nk, :], in_=sb[:, 0:nk, :], func=AF.Exp),
                             reads=[sb[:, 0:nk, :]], writes=[Pt[:, 0:nk, :]])
                        P.op("act", lambda e: e.activation(out=Pt[:, nk:nk + 2, :], in_=S[:, nk:nk + 2, :], func=AF.Exp),
                             reads=[S[:, nk:nk + 2, :]], writes=[Pt[:, nk:nk + 2, :]])
                        ctxs[n] = dict(kind=kind, Pt=Pt, e_=e_, rowi=rowi, nq=64, tq0=tq0, tks=tks)

                def stageB(n):
                    c = ctxs.pop(n)
                    Pt, e_, rowi, nq, tks = c["Pt"], c["e_"], c["rowi"], c["nq"], c["tks"]
                    O = ps[:, 4 + rowi % 2, 0:132].rearrange("p (h d) -> p h d", h=2)
                    last = len(tks) - 1
                    for idx, tk in enumerate(tks):
                        P.op("pe", lambda e, idx=idx, tk=tk: e.matmul(
                            O[0:nq, e_, 0:65], lhsT=Pt[:, idx, :], rhs=Vh[:, tk, e_, 0:65], start=(idx == 0), stop=(idx == last)),
                            reads=[Pt[:, idx, :], Vh[:, tk, e_, 0:65]], writes=[O[0:nq, e_, 0:65]])
                    if e_ == 1:
                        finish_rows(O, nq, c["tq0"], rowi)

                for n in range(len(items) + DEPTH):
                    if n < len(items):
                        stageA(n)
                    if n >= DEPTH:
                        stageB(n - DEPTH)
                it += 4 - (it % 4) if it % 4 else 0

            for sidx in range(2):
                rg = ring0 if sidx == 0 else ring1
                P.dma("pool", rg, abwo[sidx], "rg%d" % sidx, sb_writes=[rg])
                for c in range(4):
                    m = sidx * 4 + c
                    for (t0, tn) in TOK_TILES:
                        j = colj(t0, nb)
                        py = ps[:, it % 2, 0:tn]
                        for k in range(KC):
                            src = aT[:, k, t0:t0 + tn] if k < 4 else plT[:, k - 4, t0:t0 + tn]
                            P.op("pe", lambda e, k=k, c=c, rg=rg, src=src, py=py: e.matmul(
                                py, lhsT=rg[:, k, c * 128:(c + 1) * 128], rhs=src, start=(k == 0), stop=(k == KC - 1)),
                                reads=[rg[:, k, c * 128:(c + 1) * 128], src], writes=[py])
                        P.op("dve", lambda e, m=m, t0=t0, tn=tn, py=py, j=j: e.scalar_tensor_tensor(
                            out=H[:, m, t0:t0 + tn], in0=py, scalar=AB[:, l, 1, 2, m, j:j + 1], in1=H[:, m, t0:t0 + tn],
                            op0=ALU.mult, op1=ALU.add),
                            reads=[py, AB[:, l, 1, 2, m, j:j + 1], H[:, m, t0:t0 + tn]], writes=[H[:, m, t0:t0 + tn]])
                        it += 1
            A.reset(m0)

        def emit_mixC(nb):
            l = 1
            m0 = A.mark()
            xn = A.alloc([KC, NT], BF16)
            WQ = A.alloc([KC, 512], BF16)
            wqoff = A.last_off
            WI = A.alloc([KC, 512], BF16)
            wioff = A.last_off
            WF = A.alloc([KC, 512], BF16)
            wfoff = A.last_off
            ONT = A.alloc([4, SEQ], BF16)
            hgc = A.alloc([5, 128], F32)
            gnS = A.alloc([1], F32)
            oneT = A.alloc([8], F32)
            one128 = A.alloc([128], BF16)
            lbT = A.alloc([512], F32)
            omlT = A.alloc([512], F32)
            SQt = A.alloc([512], F32)
            Vtb = [A.alloc([512], BF16) for _ in range(2)]
            fT = A.alloc([512], F32)
            LF = A.alloc([512], F32)
            kk = A.alloc([512], F32)
            eX = [A.alloc([512], F32) for _ in range(2)]
            qt = A.alloc([512], BF16)
            kt = A.alloc([512], BF16)
            KHb = [A.alloc([4, 512], BF16) for _ in range(2)]
            QTb = [A.alloc([4, 128], BF16) for _ in range(2)]
            KTb = [A.alloc([4, 128], BF16) for _ in range(2)]
            AM4 = A.alloc([4, 128], BF16)
            Spp = [A.alloc([4, 128], F32) for _ in range(2)]
            NS = 5
            snap = [A.alloc([4, 128], BF16) for _ in range(NS)]
            EGb = [A.alloc([4, 4], F32) for _ in range(2)]
            oS = A.alloc([512], F32)
            osq = A.alloc([512], BF16)
            rsT = A.alloc([512], F32)
            sgt = A.alloc([512], BF16)
            gst = [A.alloc([512], BF16) for _ in range(2)]
            sq = A.view(wfoff, [KC, 512], BF16)
            rstd2 = [A.view(wioff + i * 2048, [512], F32) for i in range(2)]
            t1b = [A.view(wioff + 4096 + i * 2048, [512], F32) for i in range(2)]
            sq2 = A.view(wqoff, [KC, 512], BF16)
            emit_norm(l, 1, nb, xn, TOK_TILES, ([sq, sq2], rstd2, t1b))
            P.dma("sp", hgc, hgcD, "mC_c0", sb_writes=[hgc])
            P.dma("sp", gnS, gnD, "mC_c1", sb_writes=[gnS])
            P.op("pool", lambda e: e.memset(one128, 1.0 / 128.0), writes=[one128])
            P.op("pool", lambda e: e.memset(oneT, 1.0), writes=[oneT])
            TRI = [hgc[:, 0, :], hgc[:, 1, :]]
            TRIC = [hgc[:, 2, :], hgc[:, 3, :]]
            CHK = hgc[:, 4, 0:4]
            st = {"u": 0, "gch": 0}

            for hh in range(2):
                P.dma("pool", WQ, hgw[hh * 5 + 0], "mC_wq", sb_writes=[WQ])
                P.dma("pool", WI, hgw[hh * 5 + 1], "mC_wi", sb_writes=[WI])
                P.dma("pool", WF, hgw[hh * 5 + 4], "mC_wf", sb_writes=[WF])
                gi = 0
                for hd in range(4):
                    for (t0, tn) in LAT_TILES:
                        pg = ps[:, gi % 2, :]
                        for k in range(KC):
                            P.op("pe", lambda e, hd=hd, k=k, t0=t0, tn=tn, pg=pg: e.matmul(pg, lhsT=WF[:, k, hd * 128:(hd + 1) * 128], rhs=xn[:, k, t0:t0 + tn],
                                                                                    start=(k == 0), stop=(k == KC - 1)),
                                 reads=[WF[:, k, hd * 128:(hd + 1) * 128], xn[:, k, t0:t0 + tn]], writes=[pg])
                        gs = gst[gi % 2]
                        P.op("act", lambda e, gs=gs, pg=pg: e.activation(out=gs, in_=pg, func=AF.Silu), reads=[pg], writes=[gs])
                        P.dma("sp", gscD[hh, :, hd, t0 - CTX:t0 - CTX + tn], gs, "mC_gs%d" % (gi % 2), sb_reads=[gs],
                              dram_w=[("gsc", hh, hd, (t0 - CTX) // 512)])
                        gi += 1

                for d in range(2):
                    P.dma("sp", lbT, lblD[:, 1, d, hh * 512:(hh + 1) * 512], "mC_lb0", sb_writes=[lbT])
                    P.dma("sp", omlT, lblD[:, 0, d, hh * 512:(hh + 1) * 512], "mC_lb1", sb_writes=[omlT])
                    P.op("dve", lambda e: e.tensor_tensor(out=lbT, in0=lbT, in1=omlT, op=ALU.subtract), reads=[lbT, omlT], writes=[lbT])
                    P.op("act", lambda e: e.activation(out=lbT, in_=lbT, func=AF.Sigmoid), reads=[lbT], writes=[lbT])
                    P.op("dve", lambda e: e.tensor_scalar(out=omlT, in0=lbT, scalar1=-1.0, scalar2=1.0, op0=ALU.mult, op1=ALU.add),
                         reads=[lbT], writes=[omlT])
                    P.dma("pool", WF, hgw[hh * 5 + 2 + d], "mC_wf", sb_writes=[WF])
                    Scur = Spp[st["gch"] % 2]
                    P.op("pool", lambda e, Scur=Scur: e.memset(Scur, 0.0), writes=[Scur])
                    sn0 = snap[st["gch"] % NS]
                    P.op("pool", lambda e, sn0=sn0: e.memset(sn0, 0.0), writes=[sn0])
                    order = list(range(18)) if d == 0 else [1, 0] + list(range(17, 1, -1))
                    corder = [0, 1, 2, 3] if d == 0 else [3, 2, 1, 0]
                    ctxs = {}

                    def stageA(n, d=d, order=order):
                        i = order[n]
                        u = st["u"]
                        st["u"] += 1
                        is_lat = i >= 2
                        tsl = slice(i * 128, (i + 1) * 128)
                        Vt, QTt, KTt, EGLt, KHm = Vtb[u % 2], QTb[u % 2], KTb[u % 2], EGb[u % 2], KHb[u % 2]
                        ctxs[n] = dict(i=i, is_lat=is_lat, Vt=Vt, QTt=QTt, KTt=KTt, EGLt=EGLt, KHm=KHm)
                        PQ, PI, PF = ps[:, 0, :], ps[:, 1, :], ps[:, 2, :]
                        for (W, Pp) in ((WI, PI), (WF, PF), (WQ, PQ)):
                            if W is WQ and not is_lat:
                                continue
                            for k in range(KC):
                                P.op("pe", lambda e, k=k, W=W, Pp=Pp: e.matmul(Pp, lhsT=xn[:, k, tsl], rhs=W[:, k, :],
                                                                            start=(k == 0), stop=(k == KC - 1)),
                                     reads=[xn[:, k, tsl], W[:, k, :]], writes=[Pp])
                        P.op("act", lambda e: e.copy(out=Vt, in_=PI), reads=[PI], writes=[Vt])
                        P.op("act", lambda e: e.activation(out=fT, in_=PF, func=AF.Sigmoid, scale=-1.0), reads=[PF], writes=[fT])
                        if is_lat:
                            P.op("act", lambda e: e.activation(out=SQt, in_=PQ, func=AF.Silu), reads=[PQ], writes=[SQt])
                        yield
                        P.op("dve", lambda e: e.tensor_tensor(out=kk, in0=fT, in1=omlT, op=ALU.mult), reads=[fT, omlT], writes=[kk])
                        P.op("act", lambda e: e.activation(out=LF, in_=kk, func=AF.Ln, bias=oneT[:, 0:1], scale=-1.0),
                             reads=[kk, oneT[:, 0:1]], writes=[LF])
                        PG, PD = ps[:, 3, :], ps[:, 4, :]
                        PEG = ps[:, 5, 0:16].rearrange("p (h c) -> p h c", h=4)
                        P.op("pe", lambda e: e.matmul(PD, lhsT=TRIC[d], rhs=LF, start=True, stop=True), reads=[TRIC[d], LF], writes=[PD])
                        for hd in range(4):
                            P.op("pe", lambda e, hd=hd: e.matmul(PEG[:, hd, :], lhsT=LF[:, hd * 128:(hd + 1) * 128], rhs=CHK, start=True, stop=True),
                                 reads=[LF[:, hd * 128:(hd + 1) * 128], CHK], writes=[PEG[:, hd, :]])
                        if is_lat:
                            P.op("pe", lambda e: e.matmul(PG, lhsT=TRI[d], rhs=LF, start=True, stop=True), reads=[TRI[d], LF], writes=[PG])
                        yield
                        P.op("act", lambda e: e.activation(out=eX[0], in_=PD, func=AF.Exp), reads=[PD], writes=[eX[0]])
                        P.op("act", lambda e: e.activation(out=EGLt, in_=PEG, func=AF.Exp), reads=[PEG], writes=[EGLt])
                        for c in range(4):
                            P.op("dve", lambda e, c=c: e.scalar_tensor_tensor(out=KHm[:, c, :], in0=kk, scalar=CHK[:, c:c + 1], in1=eX[0],
                                                                             op0=ALU.mult, op1=ALU.mult),
                                 reads=[kk, CHK[:, c:c + 1], eX[0]], writes=[KHm[:, c, :]])
                        yield
                        if not is_lat:
                            return
                        P.op("act", lambda e: e.activation(out=eX[1], in_=PG, func=AF.Exp), reads=[PG], writes=[eX[1]])
                        P.op("dve", lambda e: e.tensor_tensor(out=qt, in0=SQt, in1=eX[1], op=ALU.mult), reads=[SQt, eX[1]], writes=[qt])
                        P.op("act", lambda e: e.activation(out=eX[0], in_=PG, func=AF.Exp, scale=-1.0), reads=[PG], writes=[eX[0]])
                        P.op("dve", lambda e: e.tensor_tensor(out=kt, in0=kk, in1=eX[0], op=ALU.mult), reads=[kk, eX[0]], writes=[kt])
                        yield
                        PQT = ps[:, 3, :].rearrange("p (h t) -> p h t", h=4)
                        PKT = ps[:, 4, :].rearrange("p (h t) -> p h t", h=4)
                        for hd in range(4):
                            P.op("pe", lambda e, hd=hd: e.matmul(PQT[:, hd, :], lhsT=qt[:, hd * 128:(hd + 1) * 128], rhs=ident[:, :], start=True, stop=True),
                                 reads=[qt[:, hd * 128:(hd + 1) * 128], ident[:, :]], writes=[PQT[:, hd, :]])
                        for hd in range(4):
                            P.op("pe", lambda e, hd=hd: e.matmul(PKT[:, hd, :], lhsT=kt[:, hd * 128:(hd + 1) * 128], rhs=ident[:, :], start=True, stop=True),
                                 reads=[kt[:, hd * 128:(hd + 1) * 128], ident[:, :]], writes=[PKT[:, hd, :]])
                        P.op("act", lambda e: e.copy(out=QTt, in_=PQT), reads=[PQT], writes=[QTt])
                        P.op("dve", lambda e: e.tensor_copy(out=KTt, in_=PKT), reads=[PKT], writes=[KTt])

                    def stageB(n, d=d, corder=corder, hh=hh):
                        c_ = ctxs.pop(n)
                        i, is_lat, Vt, QTt, KTt, EGLt, KHm = c_["i"], c_["is_lat"], c_["Vt"], c_["QTt"], c_["KTt"], c_["EGLt"], c_["KHm"]
                        g0 = st["gch"]
                        PKV = ps[:, 7, :].rearrange("p (h v) -> p h v", h=4)
                        for ci, c in enumerate(corder):
                            gch = st["gch"]
                            for hd in range(4):
                                P.op("pe", lambda e, hd=hd, c=c: e.matmul(PKV[:, hd, :], lhsT=KHm[:, c, hd * 128:(hd + 1) * 128],
                                                                          rhs=Vt[:, hd * 128:(hd + 1) * 128], start=True, stop=True),
                                     reads=[KHm[:, c, hd * 128:(hd + 1) * 128], Vt[:, hd * 128:(hd + 1) * 128]], writes=[PKV[:, hd, :]])
                            Sa, Sb = Spp[gch % 2], Spp[(gch + 1) % 2]
                            P.op("dve", lambda e, c=c, Sa=Sa, Sb=Sb: e.tensor_tensor(
                                out=Sb, in0=Sa, in1=EGLt[:, :, c:c + 1].to_broadcast([128, 4, 128]), op=ALU.mult),
                                reads=[Sa, EGLt[:, :, c:c + 1]], writes=[Sb])
                            P.op("dve", lambda e, Sb=Sb: e.tensor_tensor(out=Sb, in0=Sb, in1=PKV, op=ALU.add),
                                 reads=[Sb, PKV], writes=[Sb])
                            sn = snap[(gch + 1) % NS]
                            P.op("act", lambda e, Sb=Sb, sn=sn: e.copy(out=sn, in_=Sb), reads=[Sb], writes=[sn])
                            st["gch"] += 1
                            yield
                        if not is_lat:
                            return
                        PO = ps[:, 6, :].rearrange("p (h t) -> p h t", h=4)
                        P.op("dve", lambda e: e.memset(ps[:, 6, :], 0.0), writes=[ps[:, 6, :]])
                        PA4 = ps[:, 7, :].rearrange("p (h t) -> p h t", h=4)
                        for hd in range(4):
                            P.op("pe", lambda e, hd=hd: e.matmul(PA4[:, hd, :], lhsT=KTt[:, hd, :], rhs=QTt[:, hd, :], start=True, stop=True),
                                 reads=[KTt[:, hd, :], QTt[:, hd, :]], writes=[PA4[:, hd, :]])
                        P.op("dve", lambda e: e.tensor_tensor(out=AM4, in0=PA4, in1=TRI[d].unsqueeze(1).to_broadcast([128, 4, 128]), op=ALU.mult),
                             reads=[PA4, TRI[d]], writes=[AM4])
                        for hd in range(4):
                            P.op("pe", lambda e, hd=hd: e.matmul(PO[:, hd, :], lhsT=Vt[:, hd * 128:(hd + 1) * 128], rhs=AM4[:, hd, :],
                                                                start=False, stop=False, skip_group_check=True),
                                 reads=[Vt[:, hd * 128:(hd + 1) * 128], AM4[:, hd, :]], writes=[PO[:, hd, :]])
                        yield
                        for ci, c in enumerate(corder):
                            sn = snap[(g0 + ci) % NS]
                            for hd in range(4):
                                P.op("pe", lambda e, hd=hd, c=c, ci=ci, sn=sn: e.matmul(
                                    PO[:, hd, c * 32:(c + 1) * 32], lhsT=sn[:, hd, :], rhs=QTt[:, hd, c * 32:(c + 1) * 32],
                                    start=False, stop=(ci == 3), skip_group_check=True),
                                    reads=[sn[:, hd, :], QTt[:, hd, c * 32:(c + 1) * 32]], writes=[PO[:, hd, c * 32:(c + 1) * 32]])
                        yield
                        li = i - 2
                        POf = ps[:, 6, :]
                        if d == 0:
                            P.op("act", lambda e: e.copy(out=oS, in_=POf), reads=[POf], writes=[oS])
                            P.dma("sp", ofwD[hh, li], oS, "mC_ost", sb_reads=[oS], dram_w=[("ofw", hh, li)])
                        else:
                            P.dma("sp", oS, ofwD[hh, li], "mC_old", sb_writes=[oS], dram_r=[("ofw", hh, li)])
                            P.dma("sp", sgt.rearrange("p (h t) -> p h t", h=4), gscD[hh, :, :, li * 128:(li + 1) * 128], "mC_gld",
                                  sb_writes=[sgt], dram_r=[("gsc", hh, hd_, li // 4) for hd_ in range(4)])
                            P.op("dve", lambda e: e.tensor_tensor(out=oS, in0=POf, in1=oS, op=ALU.add), reads=[POf, oS], writes=[oS])
                            P.op("act", lambda e: e.activation(out=osq, in_=oS, func=AF.Square), reads=[oS], writes=[osq])
                            PST = ps[:, 7, :]
                            P.op("pe", lambda e: e.matmul(PST, lhsT=one128[:, :], rhs=osq, start=True, stop=True), reads=[one128[:, :], osq], writes=[PST])
                            emit_rstd(rsT, PST)
                            P.op("dve", lambda e: e.scalar_tensor_tensor(out=oS, in0=oS, scalar=gnS[:, 0:1], in1=rsT, op0=ALU.mult, op1=ALU.mult),
                                 reads=[oS, gnS[:, 0:1], rsT], writes=[oS])
                            P.op("dve", lambda e: e.tensor_tensor(out=ONT[:, :, li * 128:(li + 1) * 128],
                                                                 in0=oS.rearrange("p (h t) -> p h t", h=4),
                                                                 in1=sgt.rearrange("p (h t) -> p h t", h=4), op=ALU.mult),
                                 reads=[oS, sgt], writes=[ONT[:, :, li * 128:(li + 1) * 128]])

                    nU = len(order)
                    for n in range(nU + 1):
                        gens = []
                        if n < nU:
                            gens.append(stageA(n))
                        if n >= 1:
                            gens.append(stageB(n - 1))
                        while gens:
                            for g_ in list(gens):
                                try:
                                    next(g_)
                                except StopIteration:
                                    gens.remove(g_)
                wo4 = A.view(wfoff, [4, D], BF16)
                P.dma("pool", wo4, hgwo[hh], "mC_wf", sb_writes=[wo4])
                it = 0
                for m in range(KC):
                    for (t0, tn) in LAT_TILES:
                        py = ps[:, it % 2, 0:tn]
                        for k in range(4):
                            P.op("pe", lambda e, k=k, m=m, t0=t0, tn=tn, py=py: e.matmul(
                                py, lhsT=wo4[:, k, m * 128:(m + 1) * 128], rhs=ONT[:, k, t0 - CTX:t0 - CTX + tn], start=(k == 0), stop=(k == 3)),
                                reads=[wo4[:, k, m * 128:(m + 1) * 128], ONT[:, k, t0 - CTX:t0 - CTX + tn]], writes=[py])
                        P.op("dve", lambda e, m=m, t0=t0, tn=tn, py=py: e.scalar_tensor_tensor(
                            out=H[:, m, t0:t0 + tn], in0=py, scalar=AB[:, l, 1, 2, m, nb:nb + 1], in1=H[:, m, t0:t0 + tn],
                            op0=ALU.mult, op1=ALU.add),
                            reads=[py, AB[:, l, 1, 2, m, nb:nb + 1], H[:, m, t0:t0 + tn]], writes=[H[:, m, t0:t0 + tn]])
                        it += 1
            A.reset(m0)

        def emit_final(b):
            m0 = A.mark()
            sq = A.alloc([KC, 512], BF16)
            rs = A.alloc([512], F32)
            ob = [A.alloc([KC, 512], F32) for _ in range(2)]
            for ti, (t0, tn) in enumerate(LAT_TILES):
                P.op("act", lambda e, t0=t0, tn=tn: e.activation(out=sq, in_=H[:, :, t0:t0 + tn], func=AF.Square),
                     reads=[H[:, :, t0:t0 + tn]], writes=[sq])
                stp = ps[:, 6, 0:tn]
                for k in range(KC):
                    P.op("pe", lambda e, k=k, stp=stp: e.matmul(stp, lhsT=onesM[:, :], rhs=sq[:, k, :],
                                                                 start=(k == 0), stop=(k == KC - 1)),
                         reads=[onesM[:, :], sq[:, k, :]], writes=[stp])
                emit_rstd(rs, stp)
                o = ob[ti % 2]
                for k in range(KC):
                    P.op("dve", lambda e, k=k, t0=t0, tn=tn, o=o: e.scalar_tensor_tensor(
                        out=o[:, k, :], in0=H[:, k, t0:t0 + tn], scalar=fgS[:, k:k + 1], in1=rs,
                        op0=ALU.mult, op1=ALU.mult),
                        reads=[H[:, k, t0:t0 + tn], fgS[:, k:k + 1], rs], writes=[o[:, k, :]])
                P.dma("sp", outT[b, :, :, t0 - CTX:t0 - CTX + tn], o, "out%d" % (ti % 2), sb_reads=[o])
            A.reset(m0)

        for b in range(NB):
            for k in range(KC):
                P.dma("sp", H[:, k, :], xT[b, :, k, :], "hload%d" % k, sb_writes=[H[:, k, :]])
            for ph in phases:
                if ph == "ffn00":
                    emit_ffn(0, 0, b, TOK_TILES)
                elif ph == "ffn01":
                    emit_ffn(0, 1, b, TOK_TILES)
                elif ph == "ffn10":
                    emit_ffn(1, 0, b, TOK_TILES)
                elif ph == "ffn11":
                    emit_ffn(1, 1, b, LAT_TILES)
                elif ph == "mixA":
                    emit_mixA(b)
                elif ph == "mixC":
                    emit_mixC(b)
            emit_final(b)
        for eng in ("sp", "pool", "act", "dve", "pe"):
            P.wait_all(eng)
        print("instructions:", P.n_inst, "sems:", len(P.sem))
    return nc


def _fm(v):
    v = np.asarray(v, np.float32)
    lead = v.shape[:-1]
    r = v.reshape(lead + (KC, 128))
    r = np.moveaxis(r, -1, 0)
    return np.ascontiguousarray(r)


def _slot(w):
    return np.ascontiguousarray(np.asarray(w, np.float32).reshape(KC, 128, -1).transpose(1, 0, 2))


def _band_mats():
    out = np.zeros((128, 4, 5, 128), np.float32)
    L = 384
    t = np.arange(L)
    for g, w in enumerate((2, 4, 8, 16)):
        lo = np.clip(t - w // 2, 0, L)
        hi = np.clip(t - w // 2 + w, 0, L)
        s = np.arange(L)[:, None]
        M = ((s >= lo[None, :]) & (s < hi[None, :])).astype(np.float64) / (hi - lo)[None, :] - np.eye(L)
        blk = lambda a, b: M[a * 128:(a + 1) * 128, b * 128:(b + 1) * 128]
        out[:, g, 0] = blk(0, 1)
        out[:, g, 1] = blk(1, 1)
        out[:, g, 2] = blk(2, 1)
        out[:, g, 3] = blk(0, 0)
        out[:, g, 4] = blk(2, 2)
    return out


def _bias_tables(rpb):
    rpb = np.asarray(rpb, np.float32)
    NEG = np.float32(-30000.0)
    c = np.arange(64)
    kc = np.arange(64)
    win0 = np.clip(c - 8, 0, 48)
    ok = (kc[:, None] >= win0[None, :]) & (kc[:, None] < win0[None, :] + 16)
    rel = np.clip(kc[:, None] - c[None, :], -15, 15) + 15

    def rowtab(h, dr, valid=True):
        if not valid or dr < -7 or dr > 7:
            return np.full((64, 64), NEG, np.float32)
        return np.where(ok, rpb[h, dr + 7][rel], NEG).astype(np.float32)

    tP = np.zeros((4, 128, 2, 14, 64), np.float32)
    tO = np.zeros((4, 128, 2, 5, 64), np.float32)
    for h in range(8):
        pr, e = h // 2, h % 2
        for i, dr0 in enumerate(range(-7, 7)):
            tP[pr, 0:64, e, i] = rowtab(h, dr0)
            tP[pr, 64:128, e, i] = rowtab(h, dr0 + 1)
        for i, dr0 in enumerate((-5, -3, -1, 1, 3)):
            tO[pr, 0:64, e, i] = rowtab(h, dr0, valid=(dr0 != -5))
            tO[pr, 64:128, e, i] = rowtab(h, dr0 + 1, valid=(dr0 != 3))
    return tP, tO


def _hg_consts():
    s = np.arange(128)[:, None]
    t = np.arange(128)[None, :]
    same = (s // 32) == (t // 32)
    out = np.zeros((128, 5, 128), np.float32)
    out[:, 0] = same & (s <= t)
    out[:, 1] = same & (s >= t)
    out[:, 2] = same & (s > t)
    out[:, 3] = same & (s < t)
    for c in range(4):
        out[:, 4, c] = (np.arange(128) // 32) == c
    return out


def prep_shared(inp):
    sh = {}
    wm = np.asarray(inp["w_mod"], np.float32)
    sh["wmod"] = np.ascontiguousarray(wm.reshape(2, KC, 128, 72, 128).transpose(0, 3, 2, 1, 4))
    bm = np.asarray(inp["b_mod"], np.float32).reshape(2, 72, 128)
    sh["bmodT"] = np.ascontiguousarray(bm.transpose(2, 0, 1))
    sh["gT"] = _fm(inp["norm_g"])
    sh["fgT"] = _fm(inp["final_g"])
    w13 = np.asarray(inp["ffn_w13"], np.float32).reshape(4, KC, 128, 2, 11, 2, 128)
    sh["w13r"] = np.ascontiguousarray(w13.transpose(0, 4, 2, 1, 3, 5, 6)).reshape(4, 11, 128, KC, 512)
    sh["w2r"] = np.ascontiguousarray(np.asarray(inp["ffn_w2"], np.float32).reshape(4, NFC, 128, D))
    wi = np.asarray(inp["ab_w_in"], np.float32)[0]
    slots = [wi[:, 1536:2048]]
    for pr in range(4):
        sl = np.zeros((D, 512), np.float32)
        sl[:, 0:128] = wi[:, pr * 128:(pr + 1) * 128]
        sl[:, 128:256] = wi[:, 512 + pr * 128:512 + (pr + 1) * 128]
        sl[:, 256:384] = wi[:, 1024 + pr * 128:1024 + (pr + 1) * 128]
        slots.append(sl)
    sh["abw"] = np.stack([_slot(x) for x in slots])
    wo = np.asarray(inp["ab_w_out"], np.float32)[0]
    sh["abwo"] = np.stack([_slot(wo[:, 0:512]), _slot(wo[:, 512:1024])])
    sh["band"] = _band_mats()
    sh["pw"] = np.ascontiguousarray(np.asarray(inp["ab_pool_w"], np.float32)[0].transpose(1, 0, 2))
    sh["psc"] = np.ascontiguousarray(np.asarray(inp["ab_pool_scale"], np.float32)[0].reshape(4, 128).T)
    sh["tblP"], sh["tblO"] = _bias_tables(inp["ab_rpb"][0])
    sh["ident"] = np.eye(128, dtype=np.float32)
    hw = np.asarray(inp["hg_w_in"], np.float32)[0]
    sl = []
    for hh in range(2):
        for base in (0, 1024, 2048, 3072, 4096):
            sl.append(_slot(hw[:, base + hh * 512: base + (hh + 1) * 512]))
    sh["hgw"] = np.stack(sl)
    hwo = np.asarray(inp["hg_w_out"], np.float32)[0]
    sh["hgwo"] = np.ascontiguousarray(hwo.reshape(2, 4, 128, D).transpose(0, 2, 1, 3))
    lbl = np.asarray(inp["hg_lb_logits"], np.float32)
    sh["lbl"] = np.ascontiguousarray(np.broadcast_to(lbl[None], (128, 2, 2, D)))
    sh["gn"] = np.ascontiguousarray(np.asarray(inp["hg_gnorm"], np.float32)[0].reshape(128, 1))
    sh["hgc"] = _hg_consts()
    return sh


def prep_core(inp, b0, NB):
    x = np.asarray(inp["x"], np.float32)
    ctx = np.asarray(inp["ctx"], np.float32)
    xT = np.empty((NB, 128, KC, NT), np.float32)
    for i in range(NB):
        full = np.concatenate([ctx[b0 + i], x[b0 + i]], axis=0)
        xT[i] = full.T.reshape(KC, 128, NT).transpose(1, 0, 2)
    c = np.asarray(inp["c"], np.float32)
    cols = [c[b0 + i] for i in range(NB)]
    while len(cols) < 2:
        cols.append(cols[-1])
    cols.append(np.asarray(inp["c_ctx"], np.float32))
    cT = np.stack(cols, axis=-1).reshape(KC, 128, 3).transpose(1, 0, 2)
    return {"xT": xT, "cT": np.ascontiguousarray(cT)}


ALL_PHASES = ["ffn00", "mixA", "ffn01", "ffn10", "mixC", "ffn11"]
_CACHE = {}


def run(inp, NB=2, n_cores=N_CORES, phases=ALL_PHASES, b_start=0, trace=False):
    key = (NB, tuple(phases))
    if key not in _CACHE:
        _CACHE[key] = build_nc(NB, phases)
    nc = _CACHE[key]
    sh = prep_shared(inp)
    in_maps = []
    for ci in range(n_cores):
        m = dict(sh)
        m.update(prep_core(inp, b_start + ci * NB, NB))
        in_maps.append(m)
    res = run_bass_kernel_spmd(nc, in_maps, core_ids=list(range(n_cores)), trace=trace)
    outs = []
    for r in res.results:
        o = np.asarray(r["outT"])
        outs.append(o.transpose(0, 3, 2, 1).reshape(NB, SEQ, D))
    return np.concatenate(outs, axis=0), res


def kernel(**inputs):
    out, _ = run(inputs)
    return out.astype(np.float32)
```

```python
import bisect
import numpy as np
import concourse.bass as bass
import concourse.mybir as mybir
from concourse.bass_utils import run_bass_kernel_spmd

F32 = mybir.dt.float32
BF16 = mybir.dt.bfloat16
AF = mybir.ActivationFunctionType
ALU = mybir.AluOpType

D = 1024
KC = 8
SEQ = 2048
CTX = 256
NT = SEQ + CTX
DFF = 2816
NFC = 22
EPS = 1e-6
N_CORES = 8
ARENA_BYTES = 206 * 1024


class _Seg:
    __slots__ = ("w", "r")

    def __init__(self, w=None, r=None):
        self.w = w
        self.r = r or []


class _Space:
    def __init__(self, size):
        self.starts = [0]
        self.segs = [_Seg()]
        self.size = size

    def _split(self, pos):
        i = bisect.bisect_right(self.starts, pos) - 1
        if self.starts[i] == pos:
            return i
        s = self.segs[i]
        self.starts.insert(i + 1, pos)
        self.segs.insert(i + 1, _Seg(s.w, list(s.r)))
        return i + 1

    def access(self, lo, hi, is_write, me, deps_out, war_only_other=None):
        assert 0 <= lo < hi <= self.size, (lo, hi, self.size)
        i0 = self._split(lo)
        i1 = self._split(hi) if hi < self.size else len(self.starts)
        for i in range(i0, i1):
            s = self.segs[i]
            if s.w is not None:
                deps_out.append(("raw" if not is_write else "waw", s.w))
            if is_write:
                for r in s.r:
                    deps_out.append(("war", r))
                s.w = me
                s.r = []
            else:
                s.r.append(me)
                if len(s.r) > 6:
                    best = {}
                    for k, v in s.r:
                        if best.get(k, -1) < v:
                            best[k] = v
                    s.r = list(best.items())
        if is_write and i1 - i0 > 1:
            del self.starts[i0 + 1:i1]
            del self.segs[i0 + 1:i1]


def _ap_interval(ap):
    pat = ap.ap
    esz = {F32: 4, BF16: 2}[ap.dtype]
    pstep = pat[0][0]
    off = ap.offset % pstep if pstep > 0 else ap.offset
    span = 0
    for st, n in pat[1:]:
        span += abs(st) * (n - 1)
    return off * esz, (off + span + 1) * esz


class Prog:
    ENG = ("pe", "act", "dve", "pool", "sp")
    EPOCH = 24000

    def __init__(self, nc, stack):
        self.nc = nc
        self.stack = stack
        self.e = {"pe": nc.tensor, "act": nc.scalar, "dve": nc.vector, "pool": nc.gpsimd, "sp": nc.sync}
        self.sem = {}
        self.cnt = {}
        self.epoch = {}
        for n in ("pe", "act", "dve", "pool"):
            self.epoch[n] = 0
            self._new_epoch_sem(n)
        self.seen = {n: {} for n in self.ENG}
        self.spaces = {}
        self.dram = {}
        self.n_inst = 0

    def _new_epoch_sem(self, n):
        k = (n, self.epoch[n])
        self.sem[k] = self.stack.enter_context(self.nc.semaphore("s_%s_%d" % (n, self.epoch[n])))
        self.cnt[k] = 0

    def space_of(self, ap):
        nm = ap.tensor.name
        sp = self.spaces.get(nm)
        if sp is None:
            esz = {F32: 4, BF16: 2}[ap.dtype]
            sp = self.spaces[nm] = _Space(ap.ap[0][0] * esz)
        return sp

    def dsem(self, key):
        k = ("d", key)
        if k not in self.sem:
            self.sem[k] = self.stack.enter_context(self.nc.semaphore("d_" + str(key)))
            self.cnt[k] = 0
        return k

    def _wait(self, eng, dep):
        k, v = dep
        sn = self.seen[eng]
        if sn.get(k, 0) >= v:
            return
        if k[0] != "d":
            for (n2, ep2) in list(sn.keys()):
                if n2 == k[0] and ep2 > k[1]:
                    return
        self.e[eng].wait_ge(self.sem[k], v)
        sn[k] = v

    def _collect(self, eng, me, reads, writes, dram_r=(), dram_w=(), is_dma=False):
        deps = []
        for ap in reads:
            lo, hi = _ap_interval(ap)
            self.space_of(ap).access(lo, hi, False, me, deps)
        for ap in writes:
            lo, hi = _ap_interval(ap)
            self.space_of(ap).access(lo, hi, True, me, deps)
        for key in dram_r:
            s = self.dram.setdefault(key, _Seg())
            if s.w is not None:
                deps.append(("raw", s.w))
            s.r.append(me)
        for key in dram_w:
            s = self.dram.setdefault(key, _Seg())
            if s.w is not None:
                deps.append(("waw", s.w))
            for r in s.r:
                deps.append(("war", r))
            s.w = me
            s.r = []
        best = {}
        for kind, (k, v) in deps:
            if (k, v) == me:
                continue
            if (not is_dma) and k[0] == eng:
                if eng == "pe" or kind == "war":
                    continue
            if best.get(k, 0) < v:
                best[k] = v
        for k, v in best.items():
            self._wait(eng, (k, v))

    def op(self, eng, fn, reads=(), writes=()):
        if self.cnt[(eng, self.epoch[eng])] >= self.EPOCH:
            self.epoch[eng] += 1
            self._new_epoch_sem(eng)
        k = (eng, self.epoch[eng])
        me = (k, self.cnt[k] + 1)
        self._collect(eng, me, reads, writes)
        ins = fn(self.e[eng])
        ins.then_inc(self.sem[k], 1)
        self.cnt[k] += 1
        self.n_inst += 1
        return ins

    def dma(self, q, out, in_, semkey, sb_reads=(), sb_writes=(), dram_r=(), dram_w=()):
        k = self.dsem(semkey)
        me = (k, self.cnt[k] + 16)
        if self.cnt[k] > 0:
            self._wait(q, (k, self.cnt[k]))
        self._collect(q, me, sb_reads, sb_writes, dram_r, dram_w, is_dma=True)
        ins = self.e[q].dma_start(out=out, in_=in_)
        ins.then_inc(self.sem[k], 16)
        self.cnt[k] += 16
        self.n_inst += 1
        return me

    def wait_all(self, eng):
        for k, v in self.cnt.items():
            if v > 0 and k[0] != eng:
                self._wait(eng, (k, v))


class Arena:
    def __init__(self, ap_f32):
        self.base = ap_f32
        self.size = ARENA_BYTES
        self.top = 0

    def mark(self):
        return self.top

    def reset(self, m):
        self.top = m

    def alloc(self, shape_free, dtype):
        esz = {F32: 4, BF16: 2}[dtype]
        n = int(np.prod(shape_free))
        nbytes = (n * esz + 31) // 32 * 32
        off = self.top
        assert off + nbytes <= self.size, ("arena overflow", off, nbytes, self.size)
        self.top = off + nbytes
        self.last_off = off
        return self.view(off, shape_free, dtype)

    def view(self, off, shape_free, dtype):
        esz = {F32: 4, BF16: 2}[dtype]
        n = int(np.prod(shape_free))
        nbytes = (n * esz + 31) // 32 * 32
        assert off % 32 == 0 and off + nbytes <= self.size
        v = self.base[:, off // 4: (off + nbytes) // 4]
        if dtype == BF16:
            v = v.bitcast(BF16)
        v = v[:, 0:n]
        if len(shape_free) == 1:
            return v
        names = " ".join("a%d" % i for i in range(len(shape_free)))
        kw = {"a%d" % i: int(s) for i, s in enumerate(shape_free)}
        return v.rearrange("p (%s) -> p %s" % (names, names), **kw)


TOK_TILES = [(0, 256), (256, 512), (768, 512), (1280, 512), (1792, 512)]
LAT_TILES = TOK_TILES[1:]
FBLOCKS = [(0, 3), (3, 3), (6, 3), (9, 2)]


def build_nc(NB, phases):
    from contextlib import ExitStack
    nc = bass.Bass("TRN2", target_bir_lowering=False)
    dt = nc.dram_tensor
    xT = dt("xT", [NB, 128, KC, NT], F32, kind="ExternalInput").ap()
    cT = dt("cT", [128, KC, 3], F32, kind="ExternalInput").ap()
    wmod = dt("wmod", [2, 72, 128, KC, 128], F32, kind="ExternalInput").ap()
    bmodT = dt("bmodT", [128, 2, 72], F32, kind="ExternalInput").ap()
    gT = dt("gT", [128, 2, 3, KC], F32, kind="ExternalInput").ap()
    fgT = dt("fgT", [128, KC], F32, kind="ExternalInput").ap()
    w13r = dt("w13r", [4, 11, 128, KC, 512], F32, kind="ExternalInput").ap()
    w2r = dt("w2r", [4, NFC, 128, D], F32, kind="ExternalInput").ap()
    abw = dt("abw", [5, 128, KC, 512], F32, kind="ExternalInput").ap()
    abwo = dt("abwo", [2, 128, KC, 512], F32, kind="ExternalInput").ap()
    bandD = dt("band", [128, 4, 5, 128], F32, kind="ExternalInput").ap()
    pwD = dt("pw", [128, 4, 128], F32, kind="ExternalInput").ap()
    pscD = dt("psc", [128, 4], F32, kind="ExternalInput").ap()
    tblP = dt("tblP", [4, 128, 2, 14, 64], F32, kind="ExternalInput").ap()
    tblO = dt("tblO", [4, 128, 2, 5, 64], F32, kind="ExternalInput").ap()
    identD = dt("ident", [128, 128], F32, kind="ExternalInput").ap()
    hgw = dt("hgw", [10, 128, KC, 512], F32, kind="ExternalInput").ap()
    hgwo = dt("hgwo", [2, 128, 4, D], F32, kind="ExternalInput").ap()
    lblD = dt("lbl", [128, 2, 2, D], F32, kind="ExternalInput").ap()
    gnD = dt("gn", [128, 1], F32, kind="ExternalInput").ap()
    hgcD = dt("hgc", [128, 10, 128], F32, kind="ExternalInput").ap()
    hgiD = dt("hgi", [128, 8], F32, kind="ExternalInput").ap()
    ofwD = dt("ofw", [2, 16, 128, 512], F32).ap()
    gscD = dt("gsc", [2, 128, 4, SEQ], BF16).ap()
    outT = dt("outT", [NB, 128, KC, SEQ], F32, kind="ExternalOutput").ap()

    with ExitStack() as st:
        arena_t = st.enter_context(nc.sbuf_tensor("arena", [128, ARENA_BYTES // 4], F32))
        psum_t = st.enter_context(nc.psum_tensor("ps", [128, 8, 512], F32))
        P = Prog(nc, st)
        A = Arena(arena_t[:, :])
        ps = psum_t

        H = A.alloc([KC, NT], F32)
        modT = A.alloc([2, 72, 3], F32)
        gS = A.alloc([2, 3, KC], F32)
        fgS = A.alloc([KC], F32)
        AB = A.alloc([2, 3, 3, KC, 3], F32)
        silc = A.alloc([KC, 3], F32)
        base_mark = A.mark()

        P.dma("sp", gS, gT, "c0", sb_writes=[gS])
        P.dma("sp", fgS, fgT, "c1", sb_writes=[fgS])
        cst = A.alloc([KC, 3], F32)
        bmS = A.alloc([2, 72], F32)
        P.dma("sp", cst, cT, "c2", sb_writes=[cst])
        P.dma("sp", bmS, bmodT, "c3", sb_writes=[bmS])
        P.op("act", lambda e: e.activation(out=silc, in_=cst, func=AF.Silu), reads=[cst], writes=[silc])

        GRP = 6
        wst = [A.alloc([GRP, KC, 128], BF16) for _ in range(2)]
        silb = A.alloc([KC, 3], BF16)
        P.op("dve", lambda e: e.tensor_copy(out=silb, in_=silc), reads=[silc], writes=[silb])
        gi = 0
        for l in range(2):
            mps = ps[:, 7, 0:216].rearrange("p (m j) -> p m j", j=3)
            for g0 in range(0, 72, GRP):
                wb = wst[gi % 2]
                P.dma("pool", wb, wmod[l, g0:g0 + GRP].rearrange("m p k c -> p m k c"), "wm%d" % (gi % 2),
                      sb_writes=[wb])
                for m in range(GRP):
                    for k in range(KC):
                        P.op("pe", lambda e, m=m, k=k, wb=wb, g0=g0: e.matmul(
                            mps[:, g0 + m, :], lhsT=wb[:, m, k, :], rhs=silb[:, k, :], start=(k == 0), stop=(k == KC - 1)),
                            reads=[wb[:, m, k, :], silb[:, k, :]], writes=[mps[:, g0 + m, :]])
                gi += 1
            P.op("dve", lambda e, l=l, mps=mps: e.tensor_tensor(
                out=modT[:, l], in0=mps, in1=bmS[:, l].unsqueeze(2).to_broadcast([128, 72, 3]), op=ALU.add),
                reads=[mps, bmS[:, l]], writes=[modT[:, l]])
        for l in range(2):
            for n in range(3):
                sc = modT[:, l, (3 * n + 1) * 8:(3 * n + 2) * 8, :]
                sh = modT[:, l, (3 * n) * 8:(3 * n + 1) * 8, :]
                gt = modT[:, l, (3 * n + 2) * 8:(3 * n + 3) * 8, :]
                P.op("dve", lambda e, l=l, n=n, sc=sc: e.scalar_tensor_tensor(
                    out=AB[:, l, n, 0], in0=sc, scalar=1.0, in1=gS[:, l, n].unsqueeze(2).to_broadcast([128, KC, 3]),
                    op0=ALU.add, op1=ALU.mult), reads=[sc, gS[:, l, n]], writes=[AB[:, l, n, 0]])
                P.op("dve", lambda e, l=l, n=n, sh=sh: e.tensor_copy(out=AB[:, l, n, 1], in_=sh),
                     reads=[sh], writes=[AB[:, l, n, 1]])
                gmul = 1.0 if n == 1 else 0.5
                P.op("dve", lambda e, l=l, n=n, gt=gt, gmul=gmul: e.tensor_scalar(
                    out=AB[:, l, n, 2], in0=gt, scalar1=gmul, scalar2=None, op0=ALU.mult),
                    reads=[gt], writes=[AB[:, l, n, 2]])
        A.reset(base_mark)

        def colj(t0, nb):
            return 2 if t0 < CTX else nb

        def emit_rstd(rs, stp):
            P.op("act", lambda e: e.activation(out=rs, in_=stp, func=AF.Ln, bias=epsT[:, 0:1], scale=1.0),
                 reads=[stp, epsT[:, 0:1]], writes=[rs])
            P.op("act", lambda e: e.activation(out=rs, in_=rs, func=AF.Exp, scale=-0.5), reads=[rs], writes=[rs])

        def emit_norm(l, n, nb, xn, tiles, scr):
            sqs, rstd2, t1b = scr

            def front(ti):
                t0, tn = tiles[ti]
                sq = sqs[ti % 2]
                P.op("act", lambda e: e.activation(out=sq[:, 0:4, 0:tn], in_=H[:, 0:4, t0:t0 + tn], func=AF.Square),
                     reads=[H[:, 0:4, t0:t0 + tn]], writes=[sq[:, 0:4, 0:tn]])
                P.op("dve", lambda e: e.tensor_tensor(out=sq[:, 4:8, 0:tn], in0=H[:, 4:8, t0:t0 + tn], in1=H[:, 4:8, t0:t0 + tn], op=ALU.mult),
                     reads=[H[:, 4:8, t0:t0 + tn]], writes=[sq[:, 4:8, 0:tn]])
                stp = ps[:, 6, 0:tn]
                for k in range(KC):
                    P.op("pe", lambda e, k=k: e.matmul(stp, lhsT=onesM[:, :], rhs=sq[:, k, 0:tn], start=(k == 0), stop=(k == KC - 1)),
                         reads=[onesM[:, :], sq[:, k, 0:tn]], writes=[stp])
                emit_rstd(rstd2[ti % 2][:, 0:tn], stp)

            def back(ti):
                t0, tn = tiles[ti]
                j = colj(t0, nb)
                rs = rstd2[ti % 2][:, 0:tn]
                for k in range(KC):
                    t1 = t1b[k % 2][:, 0:tn]
                    P.op("dve", lambda e, k=k, t1=t1: e.scalar_tensor_tensor(
                        out=t1, in0=H[:, k, t0:t0 + tn], scalar=AB[:, l, n, 0, k, j:j + 1], in1=rs,
                        op0=ALU.mult, op1=ALU.mult),
                        reads=[H[:, k, t0:t0 + tn], AB[:, l, n, 0, k, j:j + 1], rs], writes=[t1])
                    P.op("act", lambda e, k=k, t1=t1: e.activation(
                        out=xn[:, k, t0:t0 + tn], in_=t1, func=AF.Identity, bias=AB[:, l, n, 1, k, j:j + 1], scale=1.0),
                        reads=[t1, AB[:, l, n, 1, k, j:j + 1]], writes=[xn[:, k, t0:t0 + tn]])

            front(0)
            for ti in range(len(tiles)):
                if ti + 1 < len(tiles):
                    front(ti + 1)
                back(ti)

        onesM = A.alloc([128], BF16)
        P.op("pool", lambda e: e.memset(onesM, 1.0 / 1024.0), writes=[onesM])
        epsT = A.alloc([8], F32)
        P.op("pool", lambda e: e.memset(epsT, EPS), writes=[epsT])
        ident = A.alloc([128], BF16)
        P.dma("pool", ident, identD, "c4", sb_writes=[ident])
        pscS = A.alloc([4], F32)
        P.dma("sp", pscS, pscD, "c5", sb_writes=[pscS])
        base_mark = A.mark()

        def emit_ffn(l, jf, nb, tiles):
            m0 = A.mark()
            n = 0 if jf == 0 else 2
            wi = l * 2 + jf
            xn = A.alloc([KC, NT], BF16)
            gB = A.alloc([6, NT], BF16)
            gboff = A.last_off
            w2b = [A.alloc([6, D], BF16) for _ in range(2)]
            ring = [A.alloc([KC, 512], BF16) for _ in range(2)]
            sq = A.alloc([KC, 512], BF16)
            rstd2 = [A.alloc([512], F32) for _ in range(2)]
            t1b = [A.alloc([512], F32) for _ in range(2)]
            sab = [A.alloc([512], F32) for _ in range(2)]
            sq2 = A.view(gboff, [KC, 512], BF16)
            emit_norm(l, n, nb, xn, tiles, ([sq, sq2], rstd2, t1b))
            it = 0
            for bi, (s0, ns) in enumerate(FBLOCKS):
                nf = 2 * ns
                w2 = w2b[bi % 2]
                P.dma("pool", w2[:, 0:nf, :], w2r[wi, 2 * s0:2 * s0 + nf].rearrange("f p d -> p f d"),
                      "w2_%d" % (bi % 2), sb_writes=[w2[:, 0:nf, :]])
                for s in range(s0, s0 + ns):
                    rg = ring[s % 2]
                    P.dma("pool", rg, w13r[wi, s], "rg%d" % (s % 2), sb_writes=[rg])
                    for c in range(2):
                        fl = (s - s0) * 2 + c
                        for (t0, tn) in tiles:
                            pa = ps[:, it % 2, 0:tn]
                            pb = ps[:, 2 + it % 2, 0:tn]
                            for k in range(KC):
                                P.op("pe", lambda e, k=k, c=c, rg=rg, t0=t0, tn=tn, pa=pa: e.matmul(
                                    pa, lhsT=rg[:, k, c * 128:(c + 1) * 128], rhs=xn[:, k, t0:t0 + tn],
                                    start=(k == 0), stop=(k == KC - 1)),
                                    reads=[rg[:, k, c * 128:(c + 1) * 128], xn[:, k, t0:t0 + tn]], writes=[pa])
                            for k in range(KC):
                                P.op("pe", lambda e, k=k, c=c, rg=rg, t0=t0, tn=tn, pb=pb: e.matmul(
                                    pb, lhsT=rg[:, k, 256 + c * 128:256 + (c + 1) * 128], rhs=xn[:, k, t0:t0 + tn],
                                    start=(k == 0), stop=(k == KC - 1)),
                                    reads=[rg[:, k, 256 + c * 128:256 + (c + 1) * 128], xn[:, k, t0:t0 + tn]], writes=[pb])
                            sa = sab[it % 2][:, 0:tn]
                            P.op("act", lambda e, sa=sa, pa=pa: e.activation(out=sa, in_=pa, func=AF.Silu),
                                 reads=[pa], writes=[sa])
                            P.op("dve", lambda e, sa=sa, pb=pb, fl=fl, t0=t0, tn=tn: e.tensor_tensor(
                                out=gB[:, fl, t0:t0 + tn], in0=sa, in1=pb, op=ALU.mult),
                                reads=[sa, pb], writes=[gB[:, fl, t0:t0 + tn]])
                            it += 1
                for (t0, tn) in tiles:
                    j = colj(t0, nb)
                    for m in range(KC):
                        py = ps[:, 4 + it % 2, 0:tn]
                        for f in range(nf):
                            P.op("pe", lambda e, f=f, m=m, w2=w2, t0=t0, tn=tn, py=py: e.matmul(
                                py, lhsT=w2[:, f, m * 128:(m + 1) * 128], rhs=gB[:, f, t0:t0 + tn],
                                start=(f == 0), stop=(f == nf - 1)),
                                reads=[w2[:, f, m * 128:(m + 1) * 128], gB[:, f, t0:t0 + tn]], writes=[py])
                        P.op("dve", lambda e, m=m, t0=t0, tn=tn, py=py, j=j: e.scalar_tensor_tensor(
                            out=H[:, m, t0:t0 + tn], in0=py, scalar=AB[:, l, n, 2, m, j:j + 1], in1=H[:, m, t0:t0 + tn],
                            op0=ALU.mult, op1=ALU.add),
                            reads=[py, AB[:, l, n, 2, m, j:j + 1], H[:, m, t0:t0 + tn]], writes=[H[:, m, t0:t0 + tn]])
                        it += 1
            A.reset(m0)

        def emit_mixA(nb):
            l = 0
            m0 = A.mark()
            xn = A.alloc([KC, NT], BF16)
            Ureg = A.alloc([18, 512], BF16)
            uoff = A.last_off
            aT = A.view(A.last_off, [4, NT], BF16)
            plT = A.alloc([4, NT], BF16)
            qT = A.alloc([NT], BF16)
            kT = A.alloc([NT], BF16)
            Vh = A.alloc([18, 2, 66], BF16)
            ring0 = A.alloc([KC, 512], BF16)
            r0off = A.last_off
            ring1 = A.alloc([KC, 512], BF16)
            r1off = A.last_off
            tP = A.view(r1off, [2, 14, 64], F32)
            sq = A.view(r0off, [KC, 512], BF16)
            rstd2 = [A.view(r1off + i * 2048, [512], F32) for i in range(2)]
            t1b = [A.view(r1off + 4096 + i * 2048, [512], F32) for i in range(2)]
            tO = A.alloc([2, 5, 64], F32)
            band = A.alloc([4, 5, 128], BF16)
            pw = A.alloc([4, 128], BF16)
            dT = A.alloc([4, 512], BF16)
            sbb = [A.alloc([5, 64], F32) for _ in range(2)]
            Ptb = [A.alloc([7, 64], BF16) for _ in range(4)]
            atok = [A.alloc([128], BF16) for _ in range(2)]
            rcb = [A.alloc([2], F32) for _ in range(2)]
            sq2 = A.view(uoff, [KC, 512], BF16)
            emit_norm(l, 1, nb, xn, TOK_TILES, ([sq, sq2], rstd2, t1b))
            P.dma("pool", band, bandD, "mA_band", sb_writes=[band])
            P.dma("pool", pw, pwD, "mA_pw", sb_writes=[pw])
            P.op("pool", lambda e: e.memset(Vh[:, :, :, 64:66], 1.0), writes=[Vh[:, :, :, 64:66]])

            P.dma("pool", ring0, abw[0], "rg0", sb_writes=[ring0])
            ev = 0
            for i in range(18):
                pu = ps[:, ev % 2, :]
                for k in range(KC):
                    P.op("pe", lambda e, k=k, i=i, pu=pu: e.matmul(pu, lhsT=xn[:, k, i * 128:(i + 1) * 128], rhs=ring0[:, k, :],
                                                                   start=(k == 0), stop=(k == KC - 1)),
                         reads=[xn[:, k, i * 128:(i + 1) * 128], ring0[:, k, :]], writes=[pu])
                eng = "act" if ev % 2 == 0 else "dve"
                if eng == "act":
                    P.op("act", lambda e, i=i, pu=pu: e.copy(out=Ureg[:, i, :], in_=pu), reads=[pu], writes=[Ureg[:, i, :]])
                else:
                    P.op("dve", lambda e, i=i, pu=pu: e.tensor_copy(out=Ureg[:, i, :], in_=pu), reads=[pu], writes=[Ureg[:, i, :]])
                ev += 1
            quads = [(0, 2, 0, 2), (2, 6, 2, 18), (6, 10, 2, 18), (10, 14, 2, 18), (14, 18, 2, 18)]
            for (i0, i1, sf, se) in quads:
                nt4 = i1 - i0
                for g in range(4):
                    pd = ps[:, 2 + g % 2, :]
                    for ii in range(nt4):
                        i = i0 + ii
                        terms = []
                        if i > sf:
                            terms.append((i - 1, 0))
                        terms.append((i, 3 if i == sf else (4 if i == se - 1 else 1)))
                        if i < se - 1:
                            terms.append((i + 1, 2))
                        for ti, (j, ty) in enumerate(terms):
                            P.op("pe", lambda e, j=j, ty=ty, g=g, ii=ii, ti=ti, nn=len(terms), pd=pd: e.matmul(
                                pd[:, ii * 128:(ii + 1) * 128], lhsT=Ureg[:, j, g * 128:(g + 1) * 128], rhs=band[:, g, ty, :],
                                start=(ti == 0), stop=(ti == nn - 1)),
                                reads=[Ureg[:, j, g * 128:(g + 1) * 128], band[:, g, ty, :]], writes=[pd[:, ii * 128:(ii + 1) * 128]])
                    P.op("act", lambda e, g=g, pd=pd, nt4=nt4: e.copy(out=dT[:, g, 0:nt4 * 128], in_=pd[:, 0:nt4 * 128]),
                         reads=[pd[:, 0:nt4 * 128]], writes=[dT[:, g, 0:nt4 * 128]])
                for g in range(4):
                    py = ps[:, 4 + g % 2, 0:nt4 * 128]
                    P.op("pe", lambda e, g=g, py=py, nt4=nt4: e.matmul(py, lhsT=pw[:, g, :], rhs=dT[:, g, 0:nt4 * 128], start=True, stop=True),
                         reads=[pw[:, g, :], dT[:, g, 0:nt4 * 128]], writes=[py])
                    P.op("act", lambda e, g=g, py=py, i0=i0, nt4=nt4: e.activation(
                        out=plT[:, g, i0 * 128:i0 * 128 + nt4 * 128], in_=py, func=AF.Copy, scale=pscS[:, g:g + 1]),
                        reads=[py, pscS[:, g:g + 1]], writes=[plT[:, g, i0 * 128:i0 * 128 + nt4 * 128]])

            it = 0
            for pr in range(4):
                P.dma("pool", ring0[:, :, 0:384], abw[1 + pr][:, :, 0:384], "rg0", sb_writes=[ring0[:, :, 0:384]])
                for (t0, tn) in TOK_TILES:
                    for which in range(2):
                        pq = ps[:, it % 2, 0:tn]
                        for k in range(KC):
                            P.op("pe", lambda e, k=k, which=which, t0=t0, tn=tn, pq=pq: e.matmul(
                                pq, lhsT=ring0[:, k, which * 128:(which + 1) * 128], rhs=xn[:, k, t0:t0 + tn],
                                start=(k == 0), stop=(k == KC - 1)),
                                reads=[ring0[:, k, which * 128:(which + 1) * 128], xn[:, k, t0:t0 + tn]], writes=[pq])
                        if which == 0:
                            P.op("act", lambda e, t0=t0, tn=tn, pq=pq: e.mul(out=qT[:, t0:t0 + tn], in_=pq, mul=0.125),
                                 reads=[pq], writes=[qT[:, t0:t0 + tn]])
                        else:
                            P.op("dve", lambda e, t0=t0, tn=tn, pq=pq: e.tensor_copy(out=kT[:, t0:t0 + tn], in_=pq),
                                 reads=[pq], writes=[kT[:, t0:t0 + tn]])
                        it += 1
                for i in range(18):
                    pv = ps[:, 2 + i % 2, 0:128]
                    for k in range(KC):
                        P.op("pe", lambda e, k=k, i=i, pv=pv: e.matmul(pv, lhsT=xn[:, k, i * 128:(i + 1) * 128], rhs=ring0[:, k, 256:384],
                                                                       start=(k == 0), stop=(k == KC - 1)),
                             reads=[xn[:, k, i * 128:(i + 1) * 128], ring0[:, k, 256:384]], writes=[pv])
                    P.op("act", lambda e, i=i, pv=pv: e.copy(out=Vh[:, i, :, 0:64], in_=pv.rearrange("p (h d) -> p h d", h=2)),
                         reads=[pv], writes=[Vh[:, i, :, 0:64]])
                P.dma("sp", tP, tblP[pr], "mA_tp", sb_writes=[tP])
                P.dma("sp", tO, tblO[pr], "mA_to", sb_writes=[tO])

                def finish_rows(O, nq, tq0, u):
                    rc = rcb[u % 2]
                    at = atok[u % 2]
                    P.op("dve", lambda e: e.reciprocal(out=rc[0:nq, :], in_=O[0:nq, :, 64]), reads=[O[0:nq, :, 64]], writes=[rc[0:nq, :]])
                    P.op("dve", lambda e: e.tensor_tensor(
                        out=at[0:nq, :].rearrange("p (h d) -> p h d", h=2), in0=O[0:nq, :, 0:64],
                        in1=rc[0:nq, :].unsqueeze(2).to_broadcast([nq, 2, 64]), op=ALU.mult),
                        reads=[O[0:nq, :, 0:64], rc[0:nq, :]], writes=[at[0:nq, :]])
                    tp = ps[:, 6 + u % 2, 0:nq]
                    P.op("pe", lambda e: e.matmul(tp, lhsT=at[0:nq, :], rhs=ident[0:nq, 0:nq], start=True, stop=True),
                         reads=[at[0:nq, :], ident[0:nq, 0:nq]], writes=[tp])
                    P.op("act", lambda e: e.copy(out=aT[:, pr, tq0:tq0 + nq], in_=tp), reads=[tp], writes=[aT[:, pr, tq0:tq0 + nq]])

                items = [("ctx", qi, e_) for qi in range(2) for e_ in range(2)] + [("lat", r, e_) for r in range(32) for e_ in range(2)]
                DEPTH = 2
                ctxs = {}

                def stageA(n):
                    kind, a, e_ = items[n]
                    pl, ph = e_ * 64, (e_ + 1) * 64
                    rowi = n // 2
                    Pt_full = Ptb[n % 4]
                    if kind == "ctx":
                        qi = a
                        S = ps[:, n % 4, 0:256].rearrange("p (c q) -> p c q", c=2)
                        for ci in range(2):
                            P.op("pe", lambda e, ci=ci: e.matmul(
                                S[:, ci, :], lhsT=kT[pl:ph, ci * 128:(ci + 1) * 128], rhs=qT[pl:ph, qi * 128:(qi + 1) * 128],
                                start=True, stop=True),
                                reads=[kT[pl:ph, ci * 128:(ci + 1) * 128], qT[pl:ph, qi * 128:(qi + 1) * 128]], writes=[S[:, ci, :]])
                        Pt = Pt_full[:, 0:4, :].rearrange("p a b -> p (a b)").rearrange("p (c q) -> p c q", c=2)
                        P.op("act", lambda e: e.activation(out=Pt, in_=S, func=AF.Exp), reads=[S], writes=[Pt])
                        ctxs[n] = dict(kind=kind, Pt=Pt, e_=e_, rowi=rowi, nq=128, tq0=qi * 128, tks=[0, 1])
                    else:
                        r = a
                        tq0 = CTX + 64 * r
                        r0 = min(max(r - 4, 0), 24)
                        kt0, kt1 = r0 // 2, (r0 + 7) // 2
                        nk = kt1 - kt0 + 1
                        if nk == 5:
                            tvv = tO[:, e_, 0:5, :]
                        else:
                            ty0 = 2 * kt0 - r + 7
                            tvv = tP[:, e_, ty0:ty0 + 7:2, :]
                        S = ps[:, n % 4, 0:448].rearrange("p (c q) -> p c q", c=7)
                        tks = [(2 + kt0 + idx) if idx < nk else (idx - nk) for idx in range(nk + 2)]
                        for idx, tk in enumerate(tks):
                            P.op("pe", lambda e, idx=idx, tk=tk: e.matmul(
                                S[:, idx, :], lhsT=kT[pl:ph, tk * 128:(tk + 1) * 128], rhs=qT[pl:ph, tq0:tq0 + 64],
                                start=True, stop=True),
                                reads=[kT[pl:ph, tk * 128:(tk + 1) * 128], qT[pl:ph, tq0:tq0 + 64]], writes=[S[:, idx, :]])
                        sb = sbb[n % 2]
                        Pt = Pt_full
                        P.op("dve", lambda e: e.tensor_tensor(out=sb[:, 0:nk, :], in0=S[:, 0:nk, :], in1=tvv, op=ALU.add),
                             reads=[S[:, 0:nk, :], tvv], writes=[sb[:, 0:nk, :]])
                        P.op("act", lambda e: e.activation(out=Pt[:, 0:nk, :], in_=sb[:, 0:nk, :], func=AF.Exp),
                             reads=[sb[:, 0:nk, :]], writes=[Pt[:, 0:nk, :]])
                        P.op("act", lambda e: e.activation(out=Pt[:, nk:nk + 2, :], in_=S[:, nk:nk + 2, :], func=AF.Exp),
                             reads=[S[:, nk:nk + 2, :]], writes=[Pt[:, nk:nk + 2, :]])
                        ctxs[n] = dict(kind=kind, Pt=Pt, e_=e_, rowi=rowi, nq=64, tq0=tq0, tks=tks)

                def stageB(n):
                    c = ctxs.pop(n)
                    Pt, e_, rowi, nq, tks = c["Pt"], c["e_"], c["rowi"], c["nq"], c["tks"]
                    O = ps[:, 4 + rowi % 2, 0:132].rearrange("p (h d) -> p h d", h=2)
                    last = len(tks) - 1
                    for idx, tk in enumerate(tks):
                        P.op("pe", lambda e, idx=idx, tk=tk: e.matmul(
                            O[0:nq, e_, 0:65], lhsT=Pt[:, idx, :], rhs=Vh[:, tk, e_, 0:65], start=(idx == 0), stop=(idx == last)),
                            reads=[Pt[:, idx, :], Vh[:, tk, e_, 0:65]], writes=[O[0:nq, e_, 0:65]])
                    if e_ == 1:
                        finish_rows(O, nq, c["tq0"], rowi)

                for n in range(len(items) + DEPTH):
                    if n < len(items):
                        stageA(n)
                    if n >= DEPTH:
                        stageB(n - DEPTH)
                it += 4 - (it % 4) if it % 4 else 0

            for sidx in range(2):
                rg = ring0 if sidx == 0 else ring1
                P.dma("pool", rg, abwo[sidx], "rg%d" % sidx, sb_writes=[rg])
                for c in range(4):
                    m = sidx * 4 + c
                    for (t0, tn) in TOK_TILES:
                        j = colj(t0, nb)
                        py = ps[:, it % 2, 0:tn]
                        for k in range(KC):
                            src = aT[:, k, t0:t0 + tn] if k < 4 else plT[:, k - 4, t0:t0 + tn]
                            P.op("pe", lambda e, k=k, c=c, rg=rg, src=src, py=py: e.matmul(
                                py, lhsT=rg[:, k, c * 128:(c + 1) * 128], rhs=src, start=(k == 0), stop=(k == KC - 1)),
                                reads=[rg[:, k, c * 128:(c + 1) * 128], src], writes=[py])
                        P.op("dve", lambda e, m=m, t0=t0, tn=tn, py=py, j=j: e.scalar_tensor_tensor(
                            out=H[:, m, t0:t0 + tn], in0=py, scalar=AB[:, l, 1, 2, m, j:j + 1], in1=H[:, m, t0:t0 + tn],
                            op0=ALU.mult, op1=ALU.add),
                            reads=[py, AB[:, l, 1, 2, m, j:j + 1], H[:, m, t0:t0 + tn]], writes=[H[:, m, t0:t0 + tn]])
                        it += 1
            A.reset(m0)

        def emit_mixC(nb):
            l = 1
            m0 = A.mark()
            xn = A.alloc([KC, NT], BF16)
            WQ = A.alloc([KC, 512], BF16)
            wqoff = A.last_off
            WI = A.alloc([KC, 512], BF16)
            wioff = A.last_off
            WF = A.alloc([KC, 512], BF16)
            wfoff = A.last_off
            ONT = A.alloc([4, SEQ], BF16)
            hgc = A.alloc([10, 128], F32)
            hgi = A.alloc([8], F32)
            gnS = A.alloc([1], F32)
            oneT = A.alloc([8], F32)
            one128 = A.alloc([128], BF16)
            lbT = A.alloc([512], F32)
            omlT = A.alloc([512], F32)
            SQt = A.alloc([512], F32)
            Vtb = [A.alloc([512], BF16) for _ in range(2)]
            fT = A.alloc([512], F32)
            LF = A.alloc([512], F32)
            kk = A.alloc([512], F32)
            eX = []
            eXoff = []
            for _ in range(2):
                eX.append(A.alloc([512], F32))
                eXoff.append(A.last_off)
            qt = A.alloc([512], BF16)
            kt = A.alloc([512], BF16)
            KHb = [A.alloc([4, 512], BF16) for _ in range(2)]
            QTb = [A.alloc([4, 128], BF16) for _ in range(2)]
            KTb = [A.alloc([4, 128], BF16) for _ in range(2)]
            AM4 = A.alloc([4, 128], BF16)
            Spp = [A.alloc([4, 128], F32) for _ in range(2)]
            snap0 = [A.alloc([4, 128], BF16) for _ in range(2)]
            Tsn = [None] + [A.alloc([4, 128], BF16) for _ in range(3)]
            EGb = [A.alloc([4, 4], F32) for _ in range(2)]
            oS = A.alloc([512], F32)
            tmpS = A.view(A.last_off, [4, 128], F32)
            osq = A.alloc([512], BF16)
            rsT = A.alloc([512], F32)
            sgt = A.alloc([512], BF16)
            gst = [A.view(eXoff[i], [512], BF16) for i in range(2)]
            sq = A.view(wfoff, [KC, 512], BF16)
            rstd2 = [A.view(wioff + i * 2048, [512], F32) for i in range(2)]
            t1b = [A.view(wioff + 4096 + i * 2048, [512], F32) for i in range(2)]
            sq2 = A.view(wqoff, [KC, 512], BF16)
            emit_norm(l, 1, nb, xn, TOK_TILES, ([sq, sq2], rstd2, t1b))
            P.dma("sp", hgc, hgcD, "mC_c0", sb_writes=[hgc])
            P.dma("sp", hgi, hgiD, "mC_c2", sb_writes=[hgi])
            P.dma("sp", gnS, gnD, "mC_c1", sb_writes=[gnS])
            P.op("pool", lambda e: e.memset(one128, 1.0 / 128.0), writes=[one128])
            P.op("pool", lambda e: e.memset(oneT, 1.0), writes=[oneT])
            TRI = [hgc[:, 0, :], hgc[:, 1, :]]
            MM = [[hgc[:, 2 + dd * 4 + c_, :] for c_ in range(4)] for dd in range(2)]
            st = {"u": 0, "gch": 0}

            for hh in range(2):
                P.dma("pool", WQ, hgw[hh * 5 + 0], "mC_wq", sb_writes=[WQ])
                P.dma("pool", WI, hgw[hh * 5 + 1], "mC_wi", sb_writes=[WI])
                P.dma("pool", WF, hgw[hh * 5 + 4], "mC_wf", sb_writes=[WF])
                gi = 0
                for hd in range(4):
                    for (t0, tn) in LAT_TILES:
                        pg = ps[:, gi % 2, :]
                        for k in range(KC):
                            P.op("pe", lambda e, hd=hd, k=k, t0=t0, tn=tn, pg=pg: e.matmul(pg, lhsT=WF[:, k, hd * 128:(hd + 1) * 128], rhs=xn[:, k, t0:t0 + tn],
                                                                                    start=(k == 0), stop=(k == KC - 1)),
                                 reads=[WF[:, k, hd * 128:(hd + 1) * 128], xn[:, k, t0:t0 + tn]], writes=[pg])
                        gs = gst[gi % 2]
                        P.op("act", lambda e, gs=gs, pg=pg: e.activation(out=gs, in_=pg, func=AF.Silu), reads=[pg], writes=[gs])
                        P.dma("sp", gscD[hh, :, hd, t0 - CTX:t0 - CTX + tn], gs, "mC_gs%d" % (gi % 2), sb_reads=[gs],
                              dram_w=[("gsc", hh, hd, (t0 - CTX) // 512)])
                        gi += 1

                for d in range(2):
                    P.dma("sp", lbT, lblD[:, 1, d, hh * 512:(hh + 1) * 512], "mC_lb0", sb_writes=[lbT])
                    P.dma("sp", omlT, lblD[:, 0, d, hh * 512:(hh + 1) * 512], "mC_lb1", sb_writes=[omlT])
                    P.op("dve", lambda e: e.tensor_tensor(out=lbT, in0=lbT, in1=omlT, op=ALU.subtract), reads=[lbT, omlT], writes=[lbT])
                    P.op("act", lambda e: e.activation(out=lbT, in_=lbT, func=AF.Sigmoid), reads=[lbT], writes=[lbT])
                    P.op("dve", lambda e: e.tensor_scalar(out=omlT, in0=lbT, scalar1=-1.0, scalar2=1.0, op0=ALU.mult, op1=ALU.add),
                         reads=[lbT], writes=[omlT])
                    P.dma("pool", WF, hgw[hh * 5 + 2 + d], "mC_wf", sb_writes=[WF])
                    Scur = Spp[st["u"] % 2]
                    P.op("pool", lambda e, Scur=Scur: e.memset(Scur, 0.0), writes=[Scur])
                    sn0 = snap0[st["u"] % 2]
                    P.op("pool", lambda e, sn0=sn0: e.memset(sn0, 0.0), writes=[sn0])
                    order = list(range(18)) if d == 0 else [1, 0] + list(range(17, 1, -1))
                    corder = [0, 1, 2, 3] if d == 0 else [3, 2, 1, 0]
                    ctxs = {}

                    def stageA(n, d=d, order=order):
                        i = order[n]
                        u = st["u"]
                        st["u"] += 1
                        is_lat = i >= 2
                        tsl = slice(i * 128, (i + 1) * 128)
                        Vt, QTt, KTt, EGLt, KHm = Vtb[u % 2], QTb[u % 2], KTb[u % 2], EGb[u % 2], KHb[u % 2]
                        ctxs[n] = dict(i=i, is_lat=is_lat, Vt=Vt, QTt=QTt, KTt=KTt, EGLt=EGLt, KHm=KHm, u=u)
                        PQ, PI, PF = ps[:, 0, :], ps[:, 1, :], ps[:, 2, :]
                        for (W, Pp) in ((WI, PI), (WF, PF), (WQ, PQ)):
                            if W is WQ and not is_lat:
                                continue
                            for k in range(KC):
                                P.op("pe", lambda e, k=k, W=W, Pp=Pp: e.matmul(Pp, lhsT=xn[:, k, tsl], rhs=W[:, k, :],
                                                                            start=(k == 0), stop=(k == KC - 1)),
                                     reads=[xn[:, k, tsl], W[:, k, :]], writes=[Pp])
                        P.op("act", lambda e: e.copy(out=Vt, in_=PI), reads=[PI], writes=[Vt])
                        P.op("act", lambda e: e.activation(out=fT, in_=PF, func=AF.Sigmoid, scale=-1.0), reads=[PF], writes=[fT])
                        if is_lat:
                            P.op("act", lambda e: e.activation(out=SQt, in_=PQ, func=AF.Silu), reads=[PQ], writes=[SQt])
                        yield
                        P.op("dve", lambda e: e.tensor_tensor(out=kk, in0=fT, in1=omlT, op=ALU.mult), reads=[fT, omlT], writes=[kk])
                        P.op("act", lambda e: e.activation(out=LF, in_=kk, func=AF.Ln, bias=oneT[:, 0:1], scale=-1.0),
                             reads=[kk, oneT[:, 0:1]], writes=[LF])
                        PG = ps[:, 3, :]
                        PEG = ps[:, 5, 0:16].rearrange("p (h c) -> p h c", h=4)
                        for hd in range(4):
                            P.op("pe", lambda e, hd=hd: e.matmul(PEG[:, hd, :], lhsT=LF[:, hd * 128:(hd + 1) * 128], rhs=hgi[:, d * 4:(d + 1) * 4], start=True, stop=True),
                                 reads=[LF[:, hd * 128:(hd + 1) * 128], hgi[:, d * 4:(d + 1) * 4]], writes=[PEG[:, hd, :]])
                        if is_lat:
                            P.op("pe", lambda e: e.matmul(PG, lhsT=TRI[d], rhs=LF, start=True, stop=True), reads=[TRI[d], LF], writes=[PG])
                        P.op("act", lambda e: e.activation(out=EGLt, in_=PEG, func=AF.Exp), reads=[PEG], writes=[EGLt])
                        yield
                        for ci in ([4, 1, 2, 3] if is_lat else [4]):
                            PDc = ps[:, 4 if ci % 2 == 0 else 2, :]
                            eXc = eX[ci % 2]
                            P.op("pe", lambda e, ci=ci, PDc=PDc: e.matmul(PDc, lhsT=MM[d][ci - 1], rhs=LF, start=True, stop=True),
                                 reads=[MM[d][ci - 1], LF], writes=[PDc])
                            P.op("act", lambda e, PDc=PDc, eXc=eXc: e.activation(out=eXc, in_=PDc, func=AF.Exp), reads=[PDc], writes=[eXc])
                            P.op("dve", lambda e, ci=ci, eXc=eXc: e.scalar_tensor_tensor(
                                out=KHm[:, ci - 1, :], in0=kk, scalar=hgi[:, d * 4 + ci - 1:d * 4 + ci], in1=eXc, op0=ALU.mult, op1=ALU.mult),
                                reads=[kk, hgi[:, d * 4 + ci - 1:d * 4 + ci], eXc], writes=[KHm[:, ci - 1, :]])
                            if ci == 4:
                                yield
                        yield
                        if not is_lat:
                            return
                        P.op("act", lambda e: e.activation(out=eX[1], in_=PG, func=AF.Exp), reads=[PG], writes=[eX[1]])
                        P.op("dve", lambda e: e.tensor_tensor(out=qt, in0=SQt, in1=eX[1], op=ALU.mult), reads=[SQt, eX[1]], writes=[qt])
                        P.op("act", lambda e: e.activation(out=eX[0], in_=PG, func=AF.Exp, scale=-1.0), reads=[PG], writes=[eX[0]])
                        P.op("dve", lambda e: e.tensor_tensor(out=kt, in0=kk, in1=eX[0], op=ALU.mult), reads=[kk, eX[0]], writes=[kt])
                        yield
                        PQT = ps[:, 3, :].rearrange("p (h t) -> p h t", h=4)
                        PKT = ps[:, 4, :].rearrange("p (h t) -> p h t", h=4)
                        for hd in range(4):
                            P.op("pe", lambda e, hd=hd: e.matmul(PQT[:, hd, :], lhsT=qt[:, hd * 128:(hd + 1) * 128], rhs=ident[:, :], start=True, stop=True),
                                 reads=[qt[:, hd * 128:(hd + 1) * 128], ident[:, :]], writes=[PQT[:, hd, :]])
                        for hd in range(4):
                            P.op("pe", lambda e, hd=hd: e.matmul(PKT[:, hd, :], lhsT=kt[:, hd * 128:(hd + 1) * 128], rhs=ident[:, :], start=True, stop=True),
                                 reads=[kt[:, hd * 128:(hd + 1) * 128], ident[:, :]], writes=[PKT[:, hd, :]])
                        P.op("act", lambda e: e.copy(out=QTt, in_=PQT), reads=[PQT], writes=[QTt])
                        P.op("dve", lambda e: e.tensor_copy(out=KTt, in_=PKT), reads=[PKT], writes=[KTt])

                    def stageB(n, d=d, corder=corder, hh=hh):
                        c_ = ctxs.pop(n)
                        i, is_lat, Vt, QTt, KTt, EGLt, KHm = c_["i"], c_["is_lat"], c_["Vt"], c_["QTt"], c_["KTt"], c_["EGLt"], c_["KHm"]
                        u = c_["u"]
                        Sa, Sb = Spp[u % 2], Spp[(u + 1) % 2]
                        PKV = ps[:, 7, :].rearrange("p (h v) -> p h v", h=4)
                        for ci in ([4, 1, 2, 3] if is_lat else [4]):
                            for hd in range(4):
                                P.op("pe", lambda e, hd=hd, ci=ci: e.matmul(PKV[:, hd, :], lhsT=KHm[:, ci - 1, hd * 128:(hd + 1) * 128],
                                                                            rhs=Vt[:, hd * 128:(hd + 1) * 128], start=True, stop=True),
                                     reads=[KHm[:, ci - 1, hd * 128:(hd + 1) * 128], Vt[:, hd * 128:(hd + 1) * 128]], writes=[PKV[:, hd, :]])
                            P.op("dve", lambda e, ci=ci: e.tensor_tensor(
                                out=tmpS, in0=Sa, in1=EGLt[:, :, ci - 1:ci].to_broadcast([128, 4, 128]), op=ALU.mult),
                                reads=[Sa, EGLt[:, :, ci - 1:ci]], writes=[tmpS])
                            if ci == 4:
                                P.op("dve", lambda e: e.tensor_tensor(out=Sb, in0=tmpS, in1=PKV, op=ALU.add), reads=[tmpS, PKV], writes=[Sb])
                                sn = snap0[(u + 1) % 2]
                                P.op("act", lambda e, sn=sn: e.copy(out=sn, in_=Sb), reads=[Sb], writes=[sn])
                            else:
                                P.op("dve", lambda e, ci=ci: e.tensor_tensor(out=Tsn[ci], in0=tmpS, in1=PKV, op=ALU.add),
                                     reads=[tmpS, PKV], writes=[Tsn[ci]])
                            yield
                        if not is_lat:
                            return
                        PO = ps[:, 6, :].rearrange("p (h t) -> p h t", h=4)
                        P.op("dve", lambda e: e.memset(ps[:, 6, :], 0.0), writes=[ps[:, 6, :]])
                        PA4 = ps[:, 7, :].rearrange("p (h t) -> p h t", h=4)
                        for hd in range(4):
                            P.op("pe", lambda e, hd=hd: e.matmul(PA4[:, hd, :], lhsT=KTt[:, hd, :], rhs=QTt[:, hd, :], start=True, stop=True),
                                 reads=[KTt[:, hd, :], QTt[:, hd, :]], writes=[PA4[:, hd, :]])
                        P.op("dve", lambda e: e.tensor_tensor(out=AM4, in0=PA4, in1=TRI[d].unsqueeze(1).to_broadcast([128, 4, 128]), op=ALU.mult),
                             reads=[PA4, TRI[d]], writes=[AM4])
                        for hd in range(4):
                            P.op("pe", lambda e, hd=hd: e.matmul(PO[:, hd, :], lhsT=Vt[:, hd * 128:(hd + 1) * 128], rhs=AM4[:, hd, :],
                                                                start=False, stop=False, skip_group_check=True),
                                 reads=[Vt[:, hd * 128:(hd + 1) * 128], AM4[:, hd, :]], writes=[PO[:, hd, :]])
                        yield
                        for ci, c in enumerate(corder):
                            sn = snap0[u % 2] if ci == 0 else Tsn[ci]
                            for hd in range(4):
                                P.op("pe", lambda e, hd=hd, c=c, ci=ci, sn=sn: e.matmul(
                                    PO[:, hd, c * 32:(c + 1) * 32], lhsT=sn[:, hd, :], rhs=QTt[:, hd, c * 32:(c + 1) * 32],
                                    start=False, stop=(ci == 3), skip_group_check=True),
                                    reads=[sn[:, hd, :], QTt[:, hd, c * 32:(c + 1) * 32]], writes=[PO[:, hd, c * 32:(c + 1) * 32]])
                        yield
                        li = i - 2
                        POf = ps[:, 6, :]
                        if d == 0:
                            P.op("act", lambda e: e.copy(out=oS, in_=POf), reads=[POf], writes=[oS])
                            P.dma("sp", ofwD[hh, li], oS, "mC_ost", sb_reads=[oS], dram_w=[("ofw", hh, li)])
                        else:
                            P.dma("sp", oS, ofwD[hh, li], "mC_old", sb_writes=[oS], dram_r=[("ofw", hh, li)])
                            P.dma("sp", sgt.rearrange("p (h t) -> p h t", h=4), gscD[hh, :, :, li * 128:(li + 1) * 128], "mC_gld",
                                  sb_writes=[sgt], dram_r=[("gsc", hh, hd_, li // 4) for hd_ in range(4)])
                            P.op("dve", lambda e: e.tensor_tensor(out=oS, in0=POf, in1=oS, op=ALU.add), reads=[POf, oS], writes=[oS])
                            P.op("act", lambda e: e.activation(out=osq, in_=oS, func=AF.Square), reads=[oS], writes=[osq])
                            PST = ps[:, 7, :]
                            P.op("pe", lambda e: e.matmul(PST, lhsT=one128[:, :], rhs=osq, start=True, stop=True), reads=[one128[:, :], osq], writes=[PST])
                            emit_rstd(rsT, PST)
                            P.op("dve", lambda e: e.scalar_tensor_tensor(out=oS, in0=oS, scalar=gnS[:, 0:1], in1=rsT, op0=ALU.mult, op1=ALU.mult),
                                 reads=[oS, gnS[:, 0:1], rsT], writes=[oS])
                            P.op("dve", lambda e: e.tensor_tensor(out=ONT[:, :, li * 128:(li + 1) * 128],
                                                                 in0=oS.rearrange("p (h t) -> p h t", h=4),
                                                                 in1=sgt.rearrange("p (h t) -> p h t", h=4), op=ALU.mult),
                                 reads=[oS, sgt], writes=[ONT[:, :, li * 128:(li + 1) * 128]])

                    nU = len(order)
                    for n in range(nU + 1):
                        gens = []
                        if n < nU:
                            gens.append(stageA(n))
                        if n >= 1:
                            gens.append(stageB(n - 1))
                        while gens:
                            for g_ in list(gens):
                                try:
                                    next(g_)
                                except StopIteration:
                                    gens.remove(g_)
                wo4 = A.view(wfoff, [4, D], BF16)
                P.dma("pool", wo4, hgwo[hh], "mC_wf", sb_writes=[wo4])
                it = 0
                for m in range(KC):
                    for (t0, tn) in LAT_TILES:
                        py = ps[:, it % 2, 0:tn]
                        for k in range(4):
                            P.op("pe", lambda e, k=k, m=m, t0=t0, tn=tn, py=py: e.matmul(
                                py, lhsT=wo4[:, k, m * 128:(m + 1) * 128], rhs=ONT[:, k, t0 - CTX:t0 - CTX + tn], start=(k == 0), stop=(k == 3)),
                                reads=[wo4[:, k, m * 128:(m + 1) * 128], ONT[:, k, t0 - CTX:t0 - CTX + tn]], writes=[py])
                        P.op("dve", lambda e, m=m, t0=t0, tn=tn, py=py: e.scalar_tensor_tensor(
                            out=H[:, m, t0:t0 + tn], in0=py, scalar=AB[:, l, 1, 2, m, nb:nb + 1], in1=H[:, m, t0:t0 + tn],
                            op0=ALU.mult, op1=ALU.add),
                            reads=[py, AB[:, l, 1, 2, m, nb:nb + 1], H[:, m, t0:t0 + tn]], writes=[H[:, m, t0:t0 + tn]])
                        it += 1
            A.reset(m0)

        def emit_final(b):
            m0 = A.mark()
            sq = A.alloc([KC, 512], BF16)
            rs = A.alloc([512], F32)
            ob = [A.alloc([KC, 512], F32) for _ in range(2)]
            for ti, (t0, tn) in enumerate(LAT_TILES):
                P.op("act", lambda e, t0=t0, tn=tn: e.activation(out=sq, in_=H[:, :, t0:t0 + tn], func=AF.Square),
                     reads=[H[:, :, t0:t0 + tn]], writes=[sq])
                stp = ps[:, 6, 0:tn]
                for k in range(KC):
                    P.op("pe", lambda e, k=k, stp=stp: e.matmul(stp, lhsT=onesM[:, :], rhs=sq[:, k, :],
                                                                 start=(k == 0), stop=(k == KC - 1)),
                         reads=[onesM[:, :], sq[:, k, :]], writes=[stp])
                emit_rstd(rs, stp)
                o = ob[ti % 2]
                for k in range(KC):
                    P.op("dve", lambda e, k=k, t0=t0, tn=tn, o=o: e.scalar_tensor_tensor(
                        out=o[:, k, :], in0=H[:, k, t0:t0 + tn], scalar=fgS[:, k:k + 1], in1=rs,
                        op0=ALU.mult, op1=ALU.mult),
                        reads=[H[:, k, t0:t0 + tn], fgS[:, k:k + 1], rs], writes=[o[:, k, :]])
                P.dma("sp", outT[b, :, :, t0 - CTX:t0 - CTX + tn], o, "out%d" % (ti % 2), sb_reads=[o])
            A.reset(m0)

        for b in range(NB):
            for k in range(KC):
                P.dma("sp", H[:, k, :], xT[b, :, k, :], "hload%d" % k, sb_writes=[H[:, k, :]])
            for ph in phases:
                if ph == "ffn00":
                    emit_ffn(0, 0, b, TOK_TILES)
                elif ph == "ffn01":
                    emit_ffn(0, 1, b, TOK_TILES)
                elif ph == "ffn10":
                    emit_ffn(1, 0, b, TOK_TILES)
                elif ph == "ffn11":
                    emit_ffn(1, 1, b, LAT_TILES)
                elif ph == "mixA":
                    emit_mixA(b)
                elif ph == "mixC":
                    emit_mixC(b)
            emit_final(b)
        for eng in ("sp", "pool", "act", "dve", "pe"):
            P.wait_all(eng)
        print("instructions:", P.n_inst, "sems:", len(P.sem))
    return nc


def _fm(v):
    v = np.asarray(v, np.float32)
    lead = v.shape[:-1]
    r = v.reshape(lead + (KC, 128))
    r = np.moveaxis(r, -1, 0)
    return np.ascontiguousarray(r)


def _slot(w):
    return np.ascontiguousarray(np.asarray(w, np.float32).reshape(KC, 128, -1).transpose(1, 0, 2))


def _band_mats():
    out = np.zeros((128, 4, 5, 128), np.float32)
    L = 384
    t = np.arange(L)
    for g, w in enumerate((2, 4, 8, 16)):
        lo = np.clip(t - w // 2, 0, L)
        hi = np.clip(t - w // 2 + w, 0, L)
        s = np.arange(L)[:, None]
        M = ((s >= lo[None, :]) & (s < hi[None, :])).astype(np.float64) / (hi - lo)[None, :] - np.eye(L)
        blk = lambda a, b: M[a * 128:(a + 1) * 128, b * 128:(b + 1) * 128]
        out[:, g, 0] = blk(0, 1)
        out[:, g, 1] = blk(1, 1)
        out[:, g, 2] = blk(2, 1)
        out[:, g, 3] = blk(0, 0)
        out[:, g, 4] = blk(2, 2)
    return out


def _bias_tables(rpb):
    rpb = np.asarray(rpb, np.float32)
    NEG = np.float32(-30000.0)
    c = np.arange(64)
    kc = np.arange(64)
    win0 = np.clip(c - 8, 0, 48)
    ok = (kc[:, None] >= win0[None, :]) & (kc[:, None] < win0[None, :] + 16)
    rel = np.clip(kc[:, None] - c[None, :], -15, 15) + 15

    def rowtab(h, dr, valid=True):
        if not valid or dr < -7 or dr > 7:
            return np.full((64, 64), NEG, np.float32)
        return np.where(ok, rpb[h, dr + 7][rel], NEG).astype(np.float32)

    tP = np.zeros((4, 128, 2, 14, 64), np.float32)
    tO = np.zeros((4, 128, 2, 5, 64), np.float32)
    for h in range(8):
        pr, e = h // 2, h % 2
        for i, dr0 in enumerate(range(-7, 7)):
            tP[pr, 0:64, e, i] = rowtab(h, dr0)
            tP[pr, 64:128, e, i] = rowtab(h, dr0 + 1)
        for i, dr0 in enumerate((-5, -3, -1, 1, 3)):
            tO[pr, 0:64, e, i] = rowtab(h, dr0, valid=(dr0 != -5))
            tO[pr, 64:128, e, i] = rowtab(h, dr0 + 1, valid=(dr0 != 3))
    return tP, tO


def _hg_consts():
    s_ = np.arange(128)[:, None]
    t_ = np.arange(128)[None, :]
    same = (s_ // 32) == (t_ // 32)
    out = np.zeros((128, 10, 128), np.float32)
    out[:, 0] = same & (s_ <= t_)
    out[:, 1] = same & (s_ >= t_)
    idx = np.zeros((128, 8), np.float32)
    for d in range(2):
        pos = np.arange(128) if d == 0 else 127 - np.arange(128)
        pu = pos[:, None]
        psn = pos[None, :]
        for ci in range(1, 5):
            out[:, 2 + d * 4 + ci - 1] = (psn < pu) & (pu < 32 * ci)
            idx[:, d * 4 + ci - 1] = pos < 32 * ci
    return out, idx


def prep_shared(inp):
    sh = {}
    wm = np.asarray(inp["w_mod"], np.float32)
    sh["wmod"] = np.ascontiguousarray(wm.reshape(2, KC, 128, 72, 128).transpose(0, 3, 2, 1, 4))
    bm = np.asarray(inp["b_mod"], np.float32).reshape(2, 72, 128)
    sh["bmodT"] = np.ascontiguousarray(bm.transpose(2, 0, 1))
    sh["gT"] = _fm(inp["norm_g"])
    sh["fgT"] = _fm(inp["final_g"])
    w13 = np.asarray(inp["ffn_w13"], np.float32).reshape(4, KC, 128, 2, 11, 2, 128)
    sh["w13r"] = np.ascontiguousarray(w13.transpose(0, 4, 2, 1, 3, 5, 6)).reshape(4, 11, 128, KC, 512)
    sh["w2r"] = np.ascontiguousarray(np.asarray(inp["ffn_w2"], np.float32).reshape(4, NFC, 128, D))
    wi = np.asarray(inp["ab_w_in"], np.float32)[0]
    slots = [wi[:, 1536:2048]]
    for pr in range(4):
        sl = np.zeros((D, 512), np.float32)
        sl[:, 0:128] = wi[:, pr * 128:(pr + 1) * 128]
        sl[:, 128:256] = wi[:, 512 + pr * 128:512 + (pr + 1) * 128]
        sl[:, 256:384] = wi[:, 1024 + pr * 128:1024 + (pr + 1) * 128]
        slots.append(sl)
    sh["abw"] = np.stack([_slot(x) for x in slots])
    wo = np.asarray(inp["ab_w_out"], np.float32)[0]
    sh["abwo"] = np.stack([_slot(wo[:, 0:512]), _slot(wo[:, 512:1024])])
    sh["band"] = _band_mats()
    sh["pw"] = np.ascontiguousarray(np.asarray(inp["ab_pool_w"], np.float32)[0].transpose(1, 0, 2))
    sh["psc"] = np.ascontiguousarray(np.asarray(inp["ab_pool_scale"], np.float32)[0].reshape(4, 128).T)
    sh["tblP"], sh["tblO"] = _bias_tables(inp["ab_rpb"][0])
    sh["ident"] = np.eye(128, dtype=np.float32)
    hw = np.asarray(inp["hg_w_in"], np.float32)[0]
    sl = []
    for hh in range(2):
        for base in (0, 1024, 2048, 3072, 4096):
            sl.append(_slot(hw[:, base + hh * 512: base + (hh + 1) * 512]))
    sh["hgw"] = np.stack(sl)
    hwo = np.asarray(inp["hg_w_out"], np.float32)[0]
    sh["hgwo"] = np.ascontiguousarray(hwo.reshape(2, 4, 128, D).transpose(0, 2, 1, 3))
    lbl = np.asarray(inp["hg_lb_logits"], np.float32)
    sh["lbl"] = np.ascontiguousarray(np.broadcast_to(lbl[None], (128, 2, 2, D)))
    sh["gn"] = np.ascontiguousarray(np.asarray(inp["hg_gnorm"], np.float32)[0].reshape(128, 1))
    sh["hgc"], sh["hgi"] = _hg_consts()
    return sh


def prep_core(inp, b0, NB):
    x = np.asarray(inp["x"], np.float32)
    ctx = np.asarray(inp["ctx"], np.float32)
    xT = np.empty((NB, 128, KC, NT), np.float32)
    for i in range(NB):
        full = np.concatenate([ctx[b0 + i], x[b0 + i]], axis=0)
        xT[i] = full.T.reshape(KC, 128, NT).transpose(1, 0, 2)
    c = np.asarray(inp["c"], np.float32)
    cols = [c[b0 + i] for i in range(NB)]
    while len(cols) < 2:
        cols.append(cols[-1])
    cols.append(np.asarray(inp["c_ctx"], np.float32))
    cT = np.stack(cols, axis=-1).reshape(KC, 128, 3).transpose(1, 0, 2)
    return {"xT": xT, "cT": np.ascontiguousarray(cT)}


ALL_PHASES = ["ffn00", "mixA", "ffn01", "ffn10", "mixC", "ffn11"]
_CACHE = {}


def run(inp, NB=2, n_cores=N_CORES, phases=ALL_PHASES, b_start=0, trace=False):
    key = (NB, tuple(phases))
    if key not in _CACHE:
        _CACHE[key] = build_nc(NB, phases)
    nc = _CACHE[key]
    sh = prep_shared(inp)
    in_maps = []
    for ci in range(n_cores):
        m = dict(sh)
        m.update(prep_core(inp, b_start + ci * NB, NB))
        in_maps.append(m)
    res = run_bass_kernel_spmd(nc, in_maps, core_ids=list(range(n_cores)), trace=trace)
    outs = []
    for r in res.results:
        o = np.asarray(r["outT"])
        outs.append(o.transpose(0, 3, 2, 1).reshape(NB, SEQ, D))
    return np.concatenate(outs, axis=0), res


def kernel(**inputs):
    out, _ = run(inputs)
    return out.astype(np.float32)
```
